# Optimizing a Trainium2 kernel written in Bass

```python
import math
import jax, jax.numpy as jnp
from jax import lax
import numpy as np

D_MODEL = 1024
BATCH = 8
SEQ = 2048
DEPTH = 1
DEC_BATCH = 16
DEC_SEQ = 64
PAST_LEN = 4096

CHUNK = 64
D_MIX = D_MODEL
D_SSM = D_MIX // 2
SSM_GROUP = 16
N_SSM_GROUPS = D_SSM // SSM_GROUP
SSM_STATE = 64
D_ATT = D_MIX - D_SSM
N_HEADS = 4
HEAD_DIM = D_ATT // (2 * N_HEADS)
V_DIM = 2 * HEAD_DIM
D_IN = D_SSM + 3 * D_ATT
N_MEM = 256
N_MEM_HEADS = 4
MEM_HEAD_DIM = D_MODEL // N_MEM_HEADS
D_FF = 2816
Q_BLOCK = 128
EPS = 1e-6

kernel_name = "hymba_s5_diffattn_streaming_step"

F32 = jnp.float32


def rmsnorm(x, g):
    xf = x.astype(F32)
    y = xf * lax.rsqrt(jnp.mean(xf * xf, axis=-1, keepdims=True) + EPS)
    return (y * g.astype(F32)).astype(x.dtype)


def swiglu(x, w_gu, w_d):
    gate, up = jnp.split(x @ w_gu, 2, axis=-1)
    return (jax.nn.silu(gate) * up) @ w_d


def ffn_half(x, g, w_gu, w_d):
    return x + 0.5 * swiglu(rmsnorm(x, g), w_gu, w_d)


def mix_inputs(h, g_mix, w_in):
    z = rmsnorm(h, g_mix) @ w_in
    bsz, L, _ = z.shape
    u = z[..., :D_SSM]
    q = z[..., D_SSM:D_SSM + D_ATT].reshape(bsz, L, N_HEADS, 2, HEAD_DIM)
    k = z[..., D_SSM + D_ATT:D_SSM + 2 * D_ATT].reshape(bsz, L, N_HEADS, 2 * HEAD_DIM)
    v = z[..., D_SSM + 2 * D_ATT:].reshape(bsz, L, N_HEADS, V_DIM)
    return u, q, k, v


def s5_mixer(u, h0_re, h0_im, a_re, a_im, log_dt, b_re, b_im, c_re, c_im, d_skip, w_glu, b_glu):
    bsz, L, _ = u.shape
    lam = lax.complex(a_re.astype(F32), a_im.astype(F32))
    dt = jnp.exp(log_dt.astype(F32))[:, None]
    a_bar = jnp.exp(lam * dt)
    b = lax.complex(b_re.astype(F32), b_im.astype(F32))
    b_bar = ((a_bar - 1.0) / lam)[..., None] * b
    ug = u.astype(F32).reshape(bsz, L, N_SSM_GROUPS, SSM_GROUP)
    bu = jnp.einsum('gph,blgh->blgp', b_bar, ug)
    h0 = lax.complex(h0_re.astype(F32), h0_im.astype(F32))
    bu = bu.at[:, 0].add(a_bar * h0)
    a_seq = jnp.broadcast_to(a_bar, bu.shape)

    def combine(left, right):
        a_l, b_l = left
        a_r, b_r = right
        return a_r * a_l, a_r * b_l + b_r

    _, hs = lax.associative_scan(combine, (a_seq, bu), axis=1)
    c = lax.complex(c_re.astype(F32), c_im.astype(F32))
    y = jnp.einsum('ghp,blgp->blgh', c, hs).real + d_skip.astype(F32).reshape(N_SSM_GROUPS, SSM_GROUP) * ug
    y = jax.nn.gelu(y.reshape(bsz, L, D_SSM))
    y = y * jax.nn.sigmoid(y @ w_glu.astype(F32) + b_glu.astype(F32))
    h_last = hs[:, -1]
    return y.astype(u.dtype), h_last.real, h_last.imag


def alibi_slopes():
    return 2.0 ** (-8.0 * jnp.arange(1, N_HEADS + 1, dtype=F32) / N_HEADS)


def diff_lambda(lq, lk, lam_init):
    lq = lq.astype(F32)
    lk = lk.astype(F32)
    return jnp.exp(jnp.sum(lq[0] * lk[0])) - jnp.exp(jnp.sum(lq[1] * lk[1])) + lam_init


def diff_attend(q, k, v, qpos, kpos, lam, mask):
    kk = k.reshape(k.shape[:3] + (2, HEAD_DIM))
    s = jnp.einsum('bqhme,bkhme->bhmqk', q.astype(F32), kk.astype(F32)) * (HEAD_DIM ** -0.5)
    dist = jnp.abs(qpos[:, None] - kpos[None, :]).astype(F32)
    s = s - alibi_slopes()[None, :, None, None, None] * dist
    if mask is not None:
        s = jnp.where(mask, s, -jnp.inf)
    p = jax.nn.softmax(s, axis=-1)
    w = p[:, :, 0] - lam * p[:, :, 1]
    return jnp.einsum('bhqk,bkhv->bqhv', w, v.astype(F32))


def subln(o, g_subln, lam_init):
    return rmsnorm(o, g_subln) * (1.0 - lam_init)


def diff_attn_prompt(q, k, v, lam, lam_init, g_subln):
    bsz, L = q.shape[:2]
    nblk = L // Q_BLOCK
    kpos = jnp.arange(L)
    qb = q.reshape(bsz, nblk, Q_BLOCK, N_HEADS, 2, HEAD_DIM).swapaxes(0, 1)

    def block(args):
        qi, i = args
        qpos = i * Q_BLOCK + jnp.arange(Q_BLOCK)
        mask = (kpos[None, :] // CHUNK) <= (qpos[:, None] // CHUNK)
        return diff_attend(qi, k, v, qpos, kpos, lam, mask)

    o = lax.map(block, (qb, jnp.arange(nblk)))
    o = o.swapaxes(0, 1).reshape(bsz, L, N_HEADS, V_DIM)
    return subln(o, g_subln, lam_init)


def diff_attn_sample(q, k_new, v_new, k_cache, v_cache, lam, lam_init, g_subln):
    past = k_cache.shape[1]
    n_new = q.shape[1]
    k_all = jnp.concatenate([k_cache, k_new.astype(k_cache.dtype)], axis=1)
    v_all = jnp.concatenate([v_cache, v_new.astype(v_cache.dtype)], axis=1)
    qpos = past + jnp.arange(n_new)
    kpos = jnp.arange(past + n_new)
    o = diff_attend(q, k_all, v_all, qpos, kpos, lam, None)
    return subln(o, g_subln, lam_init)


def merge_groups(y_ssm, y_att, w_out):
    bsz, L = y_ssm.shape[:2]
    cat = jnp.concatenate([y_ssm, y_att.reshape(bsz, L, D_ATT).astype(y_ssm.dtype)], axis=-1)
    return cat @ w_out


def mem_kv(mem, g_mem, w_ck, w_cv):
    bsz = mem.shape[0]
    m = rmsnorm(mem, g_mem)
    mk = (m @ w_ck).reshape(bsz, N_MEM, N_MEM_HEADS, MEM_HEAD_DIM)
    mv = (m @ w_cv).reshape(bsz, N_MEM, N_MEM_HEADS, MEM_HEAD_DIM)
    return mk, mv


def cross_attend(h, g_cross, w_cq, w_co, mk, mv):
    bsz, L, _ = h.shape
    q = (rmsnorm(h, g_cross) @ w_cq).reshape(bsz, L, N_MEM_HEADS, MEM_HEAD_DIM)
    s = jnp.einsum('bqhd,bkhd->bhqk', q.astype(F32), mk.astype(F32)) * (MEM_HEAD_DIM ** -0.5)
    p = jax.nn.softmax(s, axis=-1)
    o = jnp.einsum('bhqk,bkhd->bqhd', p, mv.astype(F32)).reshape(bsz, L, D_MODEL)
    return o.astype(h.dtype) @ w_co


def setup_inputs(seed: int = 0) -> dict:
    key = jax.random.key(seed)
    keys = iter(jax.random.split(key, 48))

    def nrm(shape, scale=1.0):
        return scale * jax.random.normal(next(keys), shape, F32)

    def gain(shape):
        return 1.0 + nrm(shape, 0.02)

    G, P, H = N_SSM_GROUPS, SSM_STATE, SSM_GROUP
    n_idx = jnp.arange(P, dtype=F32)
    inp = {}
    inp["x_prompt"] = nrm((BATCH, SEQ, D_MODEL))
    inp["x_sample"] = nrm((DEC_BATCH, DEC_SEQ, D_MODEL))
    inp["cache_attn_k"] = nrm((DEPTH, DEC_BATCH, PAST_LEN, N_HEADS, 2 * HEAD_DIM))
    inp["cache_attn_v"] = nrm((DEPTH, DEC_BATCH, PAST_LEN, N_HEADS, V_DIM))
    inp["state_s5_re"] = nrm((DEPTH, DEC_BATCH, G, P), 0.5)
    inp["state_s5_im"] = nrm((DEPTH, DEC_BATCH, G, P), 0.5)
    inp["cache_mem_k"] = nrm((DEPTH, DEC_BATCH, N_MEM, N_MEM_HEADS, MEM_HEAD_DIM))
    inp["cache_mem_v"] = nrm((DEPTH, DEC_BATCH, N_MEM, N_MEM_HEADS, MEM_HEAD_DIM))
    inp["mem_prompt"] = nrm((BATCH, N_MEM, D_MODEL))
    inp["g_ffn1"] = gain((DEPTH, D_MODEL))
    inp["w_ffn1_gu"] = nrm((DEPTH, D_MODEL, 2 * D_FF), D_MODEL ** -0.5)
    inp["w_ffn1_d"] = nrm((DEPTH, D_FF, D_MODEL), D_FF ** -0.5)
    inp["g_mix"] = gain((DEPTH, D_MODEL))
    inp["w_in"] = nrm((DEPTH, D_MODEL, D_IN), D_MODEL ** -0.5)
    inp["ssm_a_re"] = -0.5 + nrm((DEPTH, G, P), 0.01)
    inp["ssm_a_im"] = math.pi * n_idx + nrm((DEPTH, G, P), 0.01)
    inp["ssm_log_dt"] = jax.random.uniform(next(keys), (DEPTH, G), F32, math.log(1e-3), math.log(1e-1))
    inp["ssm_b_re"] = nrm((DEPTH, G, P, H), (2 * H) ** -0.5)
    inp["ssm_b_im"] = nrm((DEPTH, G, P, H), (2 * H) ** -0.5)
    inp["ssm_c_re"] = nrm((DEPTH, G, H, P), (2 * P) ** -0.5)
    inp["ssm_c_im"] = nrm((DEPTH, G, H, P), (2 * P) ** -0.5)
    inp["ssm_d"] = nrm((DEPTH, D_SSM))
    inp["w_glu"] = nrm((DEPTH, D_SSM, D_SSM), D_SSM ** -0.5)
    inp["b_glu"] = nrm((DEPTH, D_SSM), 0.01)
    inp["lambda_q"] = nrm((DEPTH, 2, HEAD_DIM), 0.1)
    inp["lambda_k"] = nrm((DEPTH, 2, HEAD_DIM), 0.1)
    inp["g_subln"] = gain((DEPTH, V_DIM))
    inp["w_out"] = nrm((DEPTH, D_MIX, D_MODEL), D_MIX ** -0.5)
    inp["g_mem"] = gain((DEPTH, D_MODEL))
    inp["g_cross"] = gain((DEPTH, D_MODEL))
    inp["w_cq"] = nrm((DEPTH, D_MODEL, D_MODEL), D_MODEL ** -0.5)
    inp["w_ck"] = nrm((DEPTH, D_MODEL, D_MODEL), D_MODEL ** -0.5)
    inp["w_cv"] = nrm((DEPTH, D_MODEL, D_MODEL), D_MODEL ** -0.5)
    inp["w_co"] = nrm((DEPTH, D_MODEL, D_MODEL), D_MODEL ** -0.5)
    inp["g_ffn2"] = gain((DEPTH, D_MODEL))
    inp["w_ffn2_gu"] = nrm((DEPTH, D_MODEL, 2 * D_FF), D_MODEL ** -0.5)
    inp["w_ffn2_d"] = nrm((DEPTH, D_FF, D_MODEL), D_FF ** -0.5)
    inp["g_final"] = gain((D_MODEL,))
    return inp


def reference(x_prompt, x_sample, cache_attn_k, cache_attn_v, state_s5_re, state_s5_im,
              cache_mem_k, cache_mem_v, mem_prompt,
              g_ffn1, w_ffn1_gu, w_ffn1_d, g_mix, w_in,
              ssm_a_re, ssm_a_im, ssm_log_dt, ssm_b_re, ssm_b_im, ssm_c_re, ssm_c_im, ssm_d,
              w_glu, b_glu, lambda_q, lambda_k, g_subln, w_out,
              g_mem, g_cross, w_cq, w_ck, w_cv, w_co,
              g_ffn2, w_ffn2_gu, w_ffn2_d, g_final):
    xp, xs = x_prompt, x_sample
    kp_l, vp_l, rep_l, imp_l, mkp_l, mvp_l = [], [], [], [], [], []
    ks_l, vs_l, res_l, ims_l = [], [], [], []
    for l in range(DEPTH):
        lam_init = 0.8 - 0.6 * math.exp(-0.3 * l)
        lam = diff_lambda(lambda_q[l], lambda_k[l], lam_init)
        ssm = (ssm_a_re[l], ssm_a_im[l], ssm_log_dt[l], ssm_b_re[l], ssm_b_im[l],
               ssm_c_re[l], ssm_c_im[l], ssm_d[l], w_glu[l], b_glu[l])

        hp = ffn_half(xp, g_ffn1[l], w_ffn1_gu[l], w_ffn1_d[l])
        up, qp, kp, vp = mix_inputs(hp, g_mix[l], w_in[l])
        zero_state = jnp.zeros((xp.shape[0], N_SSM_GROUPS, SSM_STATE), F32)
        yp_ssm, rep, imp = s5_mixer(up, zero_state, zero_state, *ssm)
        yp_att = diff_attn_prompt(qp, kp, vp, lam, lam_init, g_subln[l])
        hp = hp + merge_groups(yp_ssm, yp_att, w_out[l])
        mkp, mvp = mem_kv(mem_prompt, g_mem[l], w_ck[l], w_cv[l])
        hp = hp + cross_attend(hp, g_cross[l], w_cq[l], w_co[l], mkp, mvp)
        xp = ffn_half(hp, g_ffn2[l], w_ffn2_gu[l], w_ffn2_d[l])

        hs = ffn_half(xs, g_ffn1[l], w_ffn1_gu[l], w_ffn1_d[l])
        us, qs, ks, vs = mix_inputs(hs, g_mix[l], w_in[l])
        ys_ssm, res, ims = s5_mixer(us, state_s5_re[l], state_s5_im[l], *ssm)
        ys_att = diff_attn_sample(qs, ks, vs, cache_attn_k[l], cache_attn_v[l], lam, lam_init, g_subln[l])
        hs = hs + merge_groups(ys_ssm, ys_att, w_out[l])
        hs = hs + cross_attend(hs, g_cross[l], w_cq[l], w_co[l], cache_mem_k[l], cache_mem_v[l])
        xs = ffn_half(hs, g_ffn2[l], w_ffn2_gu[l], w_ffn2_d[l])

        kp_l.append(kp); vp_l.append(vp); rep_l.append(rep); imp_l.append(imp)
        mkp_l.append(mkp); mvp_l.append(mvp)
        ks_l.append(ks); vs_l.append(vs); res_l.append(res); ims_l.append(ims)

    y_prompt = rmsnorm(xp, g_final)
    y_sample = rmsnorm(xs, g_final)
    return (y_prompt, y_sample,
            jnp.stack(kp_l), jnp.stack(vp_l), jnp.stack(rep_l), jnp.stack(imp_l),
            jnp.stack(mkp_l), jnp.stack(mvp_l),
            jnp.stack(ks_l), jnp.stack(vs_l), jnp.stack(res_l), jnp.stack(ims_l))
```

```python
import math
from contextlib import ExitStack
import numpy as np
import concourse.bass as bass
import concourse.mybir as mybir
from concourse.bass_utils import run_bass_kernel_spmd

F32 = mybir.dt.float32
BF16 = mybir.dt.bfloat16
AF = mybir.ActivationFunctionType
ALU = mybir.AluOpType

NCORES = 8
D = 1024
KC = 8
TP = 2048
TS = 128
T = TP + TS
DFF = 2816
NHC = 22
EPS = 1e-6
PAST = 4096
NMEM = 256
TB = [(0, 512), (512, 1024), (1024, 1536), (1536, 2048), (2048, 2176)]
ARENA_WORDS = 52480


class Buf:
    __slots__ = ("name", "writer", "readers", "inherit", "sem", "excl", "semq")

    def __init__(self, name, excl=False):
        self.name = name
        self.excl = excl
        self.semq = None
        self.writer = None
        self.readers = {}
        self.inherit = ()
        self.sem = None


class Op:
    __slots__ = ("eng", "fn", "cdeps", "ddeps", "signal", "kind", "sem", "val")


class Prog:
    ENG = ("pe", "act", "dve", "pool", "sp")

    def __init__(self, nc, es):
        self.nc = nc
        self.ops = []
        self.byeng = {e: [] for e in self.ENG}
        self.esem = {e: es.enter_context(nc.semaphore("e_" + e)) for e in self.ENG}
        self.dsems = [es.enter_context(nc.semaphore("d%d" % i)) for i in range(90)]
        self.dnext = 0
        self.semcount = {}
        self.dfree = {"sp": [], "pool": [], "act": []}
        self.defer = False
        self.deferred = []

    def _hz(self, h, cdeps, ddeps):
        if h is None:
            return
        if h[0] == 'c':
            cdeps.add(h[1])
        else:
            ddeps[h[1]] = 16 * self.semcount[h[1]]

    def _deps(self, reads, writes):
        cdeps, ddeps = set(), {}
        for b in reads:
            self._hz(b.writer, cdeps, ddeps)
            for h in b.inherit:
                self._hz(h, cdeps, ddeps)
        for b in writes:
            self._hz(b.writer, cdeps, ddeps)
            for h in b.readers.values():
                self._hz(h, cdeps, ddeps)
            for h in b.inherit:
                self._hz(h, cdeps, ddeps)
        return cdeps, ddeps

    def _commit(self, op, reads, writes, tag):
        oid = len(self.ops)
        self.ops.append(op)
        self.byeng[op.eng].append(oid)
        for d in op.cdeps:
            if not (op.eng == "pe" and self.ops[d].eng == "pe"):
                self.ops[d].signal = True
        h = ('c', oid) if op.kind == 'c' else ('d', op.sem)
        key = op.eng if op.kind == 'c' else ('d', id(op.sem))
        for b in reads:
            b.readers[key] = h
        for b in writes:
            b.writer = h
            b.readers = {}
            b.inherit = ()
        return oid

    def replay(self, n):
        q = self.deferred
        while n > 0 and q:
            it = q.pop(0)
            if it[0] == "op":
                self.op(*it[1:])
            else:
                self.dma(*it[1:6], owner=it[6], nc_ok=it[7])
            n -= 1

    def op(self, eng, fn, reads=(), writes=()):
        if self.defer:
            self.deferred.append(("op", eng, fn, list(reads), list(writes)))
            return
        ex = [b for b in reads if b.excl]
        if ex:
            writes = list(writes) + ex
        op = Op()
        op.eng, op.fn, op.kind, op.signal, op.sem, op.val = eng, fn, 'c', False, None, 0
        op.cdeps, op.ddeps = self._deps(reads, writes)
        return self._commit(op, reads, writes, None)

    def dma(self, q, out, in_, reads=(), writes=(), owner=None, nc_ok=False):
        if self.defer:
            self.deferred.append(("dma", q, out, in_, list(reads), list(writes), owner, nc_ok))
            return
        owner = owner or (writes[0] if writes else reads[0])
        if owner.sem is None:
            if self.dfree[q]:
                owner.sem = self.dfree[q].pop()
            else:
                owner.sem = self.dsems[self.dnext]
                self.dnext += 1
            owner.semq = q
        assert owner.semq == q, "a buffer's DMA semaphore is bound to one queue"
        sem = owner.sem
        op = Op()
        op.eng, op.kind, op.signal, op.sem = q, 'd', True, sem
        if nc_ok:
            op.fn = lambda e: e.dma_start(out=out, in_=in_, allow_slow_non_contiguous=True)
        else:
            op.fn = lambda e: e.dma_start(out=out, in_=in_)
        op.cdeps, op.ddeps = self._deps(reads, writes)
        prev = self.semcount.get(sem, 0)
        if prev:
            op.ddeps[sem] = max(op.ddeps.get(sem, 0), 16 * prev)
        self.semcount[sem] = prev + 1
        op.val = 16 * self.semcount[sem]
        return self._commit(op, reads, writes, None)

    def emit(self, block):
        for e in self.ENG:
            c = 0
            for oid in self.byeng[e]:
                op = self.ops[oid]
                if op.kind == 'c' and op.signal:
                    c += 1
                    op.val = c
        ops = self.ops
        esem = self.esem
        final = dict(self.semcount)

        def run(ename, eng):
            waited = {}
            for oid in self.byeng[ename]:
                op = ops[oid]
                waits = {}
                for d in op.cdeps:
                    dop = ops[d]
                    if dop.eng == ename and ename == "pe":
                        continue
                    s = esem[dop.eng]
                    if waits.get(s, 0) < dop.val:
                        waits[s] = dop.val
                for s, v in op.ddeps.items():
                    if waits.get(s, 0) < v:
                        waits[s] = v
                for s, v in waits.items():
                    if waited.get(s, 0) < v:
                        eng.wait_ge(s, v)
                        waited[s] = v
                ins = op.fn(eng)
                if op.kind == 'd':
                    ins.then_inc(op.sem, 16)
                elif op.signal:
                    ins.then_inc(esem[ename], 1)
            if ename == "sp":
                for s, c in final.items():
                    if waited.get(s, 0) < 16 * c:
                        eng.wait_ge(s, 16 * c)

        @block.sync
        def _(e):
            run("sp", e)

        @block.gpsimd
        def _(e):
            run("pool", e)

        @block.tensor
        def _(e):
            run("pe", e)

        @block.scalar
        def _(e):
            run("act", e)

        @block.vector
        def _(e):
            run("dve", e)


class Arena:
    def __init__(self, ap, words, prog=None):
        self.prog = prog
        self.ap = ap
        self.free = [(0, words)]
        self.ghosts = []

    def alloc(self, words, name, nbufs=1, top=False):
        words = (words + 1) // 2 * 2
        order = list(enumerate(self.free))
        if top:
            order = order[::-1]
        for i, (s, e) in order:
            if e - s >= words:
                if top:
                    self.free[i] = (s, e - words)
                    s = e - words
                else:
                    self.free[i] = (s + words, e)
                t = Tile(self, s, words, name, nbufs)
                inh = set()
                keep = []
                for (gs, ge, hz) in self.ghosts:
                    if gs < s + words and ge > s:
                        inh |= hz
                        if gs >= s and ge <= s + words:
                            continue
                    keep.append((gs, ge, hz))
                self.ghosts = keep
                inh = frozenset(inh)
                for b in t.bufs:
                    b.inherit = inh
                return t
        raise RuntimeError("arena full allocating %s (%d words); free=%s" % (name, words, self.free))

    def release(self, t):
        hz = set()
        for b in t.bufs:
            if b.sem is not None and self.prog is not None:
                self.prog.dfree[b.semq].append(b.sem)
                b.sem = None
            if b.writer is not None:
                hz.add(b.writer)
            hz.update(b.readers.values())
            hz.update(b.inherit)
        self.ghosts.append((t.off, t.off + t.words, hz))
        self.free.append((t.off, t.off + t.words))
        self.free.sort()
        m = []
        for s, e in self.free:
            if m and m[-1][1] == s:
                m[-1] = (m[-1][0], e)
            else:
                m.append((s, e))
        self.free = m


class Tile:
    def __init__(self, arena, off, words, name, nbufs=1):
        self.arena, self.off, self.words, self.name = arena, off, words, name
        self.bufs = [Buf("%s.%d" % (name, i)) for i in range(nbufs)]

    @property
    def b(self):
        return self.bufs[0]

    def f32(self, c0=0, c1=None):
        c1 = self.words if c1 is None else c1
        return self.arena.ap[:, self.off + c0:self.off + c1]

    def bf(self, c0=0, c1=None):
        v = self.arena.ap[:, self.off:self.off + self.words].bitcast(BF16)
        c1 = 2 * self.words if c1 is None else c1
        return v[:, c0:c1]

    def free(self):
        self.arena.release(self)


class Pipe:
    def __init__(self, depth):
        self.depth = depth
        self.q = []

    def push(self, a, b):
        if a is not None:
            a()
        self.q.append(b)
        while len(self.q) > self.depth:
            self.q.pop(0)()

    def flush(self):
        while self.q:
            self.q.pop(0)()


def blk_of(c0, c1):
    return [i for i, (s, e) in enumerate(TB) if s < c1 and e > c0]


class FMat:
    def __init__(self, arena, nch, dtype, name, lo=0, hi=T):
        self.nch, self.dtype, self.ncols, self.lo = nch, dtype, hi - lo, lo
        wpc = self.ncols if dtype == F32 else self.ncols // 2
        self.t = arena.alloc(nch * wpc, name, nbufs=nch * len(TB))

    def ap(self, kc, c0, c1):
        c0, c1 = c0 - self.lo, c1 - self.lo
        if self.dtype == F32:
            return self.t.f32(kc * self.ncols + c0, kc * self.ncols + c1)
        return self.t.bf(kc * self.ncols + c0, kc * self.ncols + c1)

    def ap3(self, k0, k1, c0, c1):
        if self.dtype == F32:
            v = self.t.f32()
        else:
            v = self.t.bf()
        v = v.rearrange("p (k t) -> p k t", t=self.ncols)
        return v[:, k0:k1, c0 - self.lo:c1 - self.lo]

    def bufs(self, kc, c0, c1):
        return [self.t.bufs[kc * len(TB) + i] for i in blk_of(c0, c1)]

    def free(self):
        self.t.free()


class Builder:
    def __init__(self, stage=99):
        self.stage = stage
        self.nc = bass.Bass("TRN2", target_bir_lowering=False)
        self.es = ExitStack()
        nc = self.nc
        self.din = {}
        self.dout = {}

    def dram_in(self, name, shape):
        self.din[name] = self.nc.dram_tensor(name, list(shape), F32, kind="ExternalInput").ap()
        return self.din[name]

    def dram_out(self, name, shape):
        self.dout[name] = self.nc.dram_tensor(name, list(shape), F32, kind="ExternalOutput").ap()
        return self.dout[name]

    def ps(self, i):
        return self.psum[i], self.psb[i]

    def build(self):
        nc, es = self.nc, self.es
        I, O = self.dram_in, self.dram_out
        xp = I("xp", (TP, D)); xs = I("xs", (TS, D))
        ck = I("ck", (2, PAST, 512)); cv = I("cv", (2, PAST, 512))
        s5r = I("s5r", (2, 2048)); s5i = I("s5i", (2, 2048))
        cmk = I("cmk", (2, NMEM, D)); cmv = I("cmv", (2, NMEM, D))
        memp = I("memp", (NMEM, D))
        W = {}
        for nm, sh in [("g_ffn1", (D,)), ("w_ffn1_gu", (D, 2 * DFF)), ("w_ffn1_d", (DFF, D)),
                       ("g_mix", (D,)), ("w_in", (D, 2048)),
                       ("ssm_a_re", (2048,)), ("ssm_a_im", (2048,)), ("ssm_log_dt", (32,)),
                       ("ssm_b_re", (2048, 16)), ("ssm_b_im", (2048, 16)),
                       ("ssm_c_re", (512, 64)), ("ssm_c_im", (512, 64)), ("ssm_d", (512,)),
                       ("w_glu", (512, 512)), ("b_glu", (512,)),
                       ("lambda_q", (128,)), ("lambda_k", (128,)), ("g_subln", (128,)),
                       ("w_out", (D, D)), ("g_mem", (D,)), ("g_cross", (D,)),
                       ("w_cq", (D, D)), ("w_ck", (D, D)), ("w_cv", (D, D)), ("w_co", (D, D)),
                       ("g_ffn2", (D,)), ("w_ffn2_gu", (D, 2 * DFF)), ("w_ffn2_d", (DFF, D)),
                       ("g_final", (D,))]:
            W[nm] = I(nm, sh)
        self.W = W
        yp = O("yp", (TP, D)); ys = O("ys", (TS, D))
        okp = O("kp", (TP, 512)); ovp = O("vp", (TP, 512))
        orep = O("rep", (2048,)); oimp = O("imp", (2048,))
        omkp = O("mkp", (NMEM, D)); omvp = O("mvp", (NMEM, D))
        oks = O("ks", (TS, 512)); ovs = O("vs", (TS, 512))
        ores = O("res", (2, 2048)); oims = O("ims", (2, 2048))

        arena_t = es.enter_context(nc.sbuf_tensor("arena", [128, ARENA_WORDS], F32))
        self.arena_t = arena_t
        self.arena_bf = arena_t.bitcast(BF16)
        self.A = Arena(arena_t[:, :], ARENA_WORDS)
        self.psum = [es.enter_context(nc.psum_tensor("ps%d" % i, [128, 512], F32)) for i in range(8)]
        self.psb = [Buf("ps%d" % i, excl=True) for i in range(8)]
        self.P = Prog(nc, es)
        self.A.prog = self.P
        P, A = self.P, self.A

        self.consts()
        if self.stage >= 4:
            P.defer = True
            self.s5_params(s5r, s5i)
            P.defer = False
        self.hT = FMat(A, KC, F32, "hT")
        self.load_x(xp, xs)
        self.ffn(W["w_ffn1_gu"], W["w_ffn1_d"], self.gcol["g_ffn1"])
        def mid():
            if self.stage >= 4:
                self.s5(s5r, s5i, orep, oimp, ores, oims)
                if self.stage == 4:
                    for k_ in range(4):
                        self.dump("dbg_ys%d" % k_, self.yssT.ap(k_, 0, T), T, [b for b in self.yssT.t.bufs])
            else:
                self.uD.free()
        P.replay(1 << 30)
        if self.stage >= 2:
            self.mix(okp, ovp, oks, ovs, mid)
            if self.stage >= 4:
                self.proj_add(W["w_out"], 0, 4, self.yssT)
                self.yssT.free()
        if self.stage >= 3:
            self.attention(ck, cv)
            if self.stage == 3:
                for h in range(4):
                    self.dump("dbg_ya%d" % h, self.yaT.ap(h, 0, T), T, [b for b in self.yaT.t.bufs])
            self.proj_add(W["w_out"], 512, 4, self.yaT)
            self.yaT.free()
            self.qT.free(); self.kT.free(); self.vtok.free()
            for t_ in self.vts:
                t_.free()
        if self.stage >= 5:
            self.mem_sets(memp, cmk, cmv, omkp, omvp)
            self.cross_attn()
        if self.stage >= 6:
            self.ffn(W["w_ffn2_gu"], W["w_ffn2_d"], self.gcol["g_ffn2"],
                     on_block_done=lambda c0, c1: self.final_out(yp, ys, only=(c0, c1)))
            block = es.enter_context(nc.Block())
            P.emit(block)
            es.close()
            return nc
        self.final_out(yp, ys)

        block = es.enter_context(nc.Block())
        P.emit(block)
        es.close()
        return nc

    def mm(self, out, lhsT, rhs, start, stop, reads, writes, **kw):
        self.P.op("pe", lambda e: e.matmul(out, lhsT=lhsT, rhs=rhs, start=start, stop=stop, **kw), reads, writes)

    def tr(self, out, in_, ident, reads, writes):
        self.P.op("pe", lambda e: e.transpose(out, in_, ident), reads, writes)

    def act(self, out, in_, func, reads, writes, bias=None, scale=None, accum_out=None):
        kw = {}
        if bias is not None:
            kw["bias"] = bias
        if scale is not None:
            kw["scale"] = scale
        if accum_out is not None:
            kw["accum_out"] = accum_out
        self.P.op("act", lambda e: e.activation(out=out, in_=in_, func=func, **kw), reads, writes)

    def tt(self, out, in0, in1, op, reads, writes, eng="dve"):
        self.P.op(eng, lambda e: e.tensor_tensor(out=out, in0=in0, in1=in1, op=op), reads, writes)

    def stt(self, out, in0, scalar, in1, op0, op1, reads, writes):
        self.P.op("dve", lambda e: e.scalar_tensor_tensor(out=out, in0=in0, scalar=scalar, in1=in1,
                                                          op0=op0, op1=op1), reads, writes)

    def ts(self, out, in0, s1, s2, op0, op1, reads, writes, eng="dve"):
        if op1 is None:
            self.P.op(eng, lambda e: e.tensor_scalar(out=out, in0=in0, scalar1=s1, scalar2=None, op0=op0),
                      reads, writes)
        else:
            self.P.op(eng, lambda e: e.tensor_scalar(out=out, in0=in0, scalar1=s1, scalar2=s2, op0=op0, op1=op1),
                      reads, writes)

    def cp(self, out, in_, reads, writes, eng="dve"):
        if eng == "act":
            self.P.op("act", lambda e: e.activation(out=out, in_=in_, func=AF.Copy), reads, writes)
        else:
            self.P.op(eng, lambda e: e.tensor_copy(out=out, in_=in_), reads, writes)

    def recip(self, out, in_, reads, writes):
        self.P.op("dve", lambda e: e.reciprocal(out=out, in_=in_), reads, writes)

    def rpow(self, out, in_, reads, writes, power=-1.0, scale=None, bias=None, bias_bufs=()):
        self.act(out, in_, AF.Ln, list(reads) + list(bias_bufs), writes, bias=bias, scale=scale)
        self.act(out, out, AF.Exp, writes, writes, scale=power)

    def mset(self, out, val, reads, writes, eng="dve"):
        self.P.op(eng, lambda e: e.memset(out, val), reads, writes)

    def consts(self):
        P, A, nc = self.P, self.A, self.nc
        c2 = A.alloc(128, "ones_f")
        self.ones_f = c2.f32(0, 128)
        self.cb2 = c2.b
        self.mset(self.ones_f, 1.0, [], [c2.b])
        c = A.alloc(256, "consts")
        self.cb = c.b
        self.ident_f = c.f32(0, 128)
        self.ident_b = c.bf(256, 384)
        self.ones_b = c.bf(384, 512)
        idf = self.ident_f
        P.op("pool", lambda e: e.iota(idf, pattern=[[1, 128]], base=0, channel_multiplier=-1,
                                      allow_small_or_imprecise_dtypes=True), writes=[c.b])
        self.ts(idf, idf, 0.0, None, ALU.is_equal, None, [c.b], [c.b])
        self.cp(self.ident_b, idf, [c.b], [c.b])
        self.mset(self.ones_b, 1.0, [c.b], [c.b])
        self.gcol = {}
        gt = A.alloc(8 * 6, "gcols")
        self.gt = gt
        for i, nm in enumerate(["g_ffn1", "g_mix", "g_cross", "g_ffn2", "g_final", "g_mem"]):
            ap = gt.f32(8 * i, 8 * i + 8)
            self.gcol[nm] = ap
            P.dma("sp", ap, self.W[nm].rearrange("(k p) -> p k", p=128), writes=[gt.b], nc_ok=True)
        ec = A.alloc(2, "epscol")
        self.eps_col = ec.f32(0, 1)
        self.mset(ec.f32(0, 2), EPS, [], [ec.b])
        self.epsb = ec.b

    def load_x(self, xp, xs):
        P, A = self.P, self.A
        stg = [A.alloc(1024, "xstg%d" % i) for i in range(2)]
        for i in range(T // 128):
            st = stg[i % 2]
            src = xp[i * 128:(i + 1) * 128, :] if i < 16 else xs[:, :]
            P.dma("sp", st.f32(), src, writes=[st.b])
            c0 = i * 128
            for half in range(2):
                pst, psb = self.ps(half)
                for j in range(4):
                    kc = half * 4 + j
                    self.tr(pst[:, j * 128:(j + 1) * 128], st.f32(kc * 128, (kc + 1) * 128), self.ident_f,
                            [st.b, self.cb], [psb])
                dst = self.hT.ap3(half * 4, half * 4 + 4, c0, c0 + 128)
                wb = []
                for kc in range(half * 4, half * 4 + 4):
                    wb += self.hT.bufs(kc, c0, c0 + 128)
                self.cp(dst, pst[:, :].rearrange("p (k t) -> p k t", t=128), [psb], wb,
                        eng=("dve" if half == 0 else "act"))
        for s in stg:
            s.free()

    def rmsnorm_to(self, src, gcol, dst, c0, c1, psi=7):
        self.rmsnorm_gen(lambda kc: (src.ap(kc, c0, c1), src.bufs(kc, c0, c1)), gcol,
                         lambda kc: (dst.ap(kc, c0, c1), dst.bufs(kc, c0, c1)), c1 - c0, psi)

    def rmsnorm_gen(self, src_fn, gcol, dst_fn, w, psi=7):
        P, A = self.P, self.A
        pst, psb = self.ps(psi)
        sq = [A.alloc(256, "sq%d" % i) for i in range(2)]
        for kc in range(KC):
            s = sq[kc % 2]
            sap, sbufs = src_fn(kc)
            self.act(s.bf(0, w), sap, AF.Square, sbufs, [s.b])
            self.mm(pst[:, 0:w], self.ones_b, s.bf(0, w), kc == 0, kc == KC - 1, [s.b, self.cb], [psb])
        rs = A.alloc(512, "rstd")
        self.rpow(rs.f32(0, w), pst[:, 0:w], [psb], [rs.b], power=-0.5, scale=1.0 / D, bias=self.eps_col,
                  bias_bufs=[self.epsb])
        for kc in range(KC):
            sap, sbufs = src_fn(kc)
            dap, dbufs = dst_fn(kc)
            self.stt(dap, sap, gcol[:, kc:kc + 1], rs.f32(0, w), ALU.mult, ALU.mult,
                     sbufs + [rs.b, self.gt.b], dbufs)
        for s in sq:
            s.free()
        rs.free()

    def ffn(self, w_gu, w_d, gcol, on_block_done=None):
        P, A = self.P, self.A
        wgu_v = w_gu.rearrange("(k p) n -> p k n", p=128)
        wd_v = w_d.rearrange("(h p) n -> p h n", p=128)
        cnt = 0
        sbs = [(0, 1024), (1024, T)]
        xn_next = None
        for si, (s0, s1) in enumerate(sbs):
            blocks = [(c0, c1) for (c0, c1) in TB if s0 <= c0 < s1]
            if xn_next is None:
                xn = FMat(A, KC, BF16, "xn", s0, s1)
                for (c0, c1) in blocks:
                    self.rmsnorm_to(self.hT, gcol, xn, c0, c1)
            else:
                xn = xn_next
                xn_next = None
            h1 = FMat(A, NHC, BF16, "h1", s0, s1)
            wg = [A.alloc(1024, "wg%d" % i) for i in range(2)]
            wu = [A.alloc(1024, "wu%d" % i) for i in range(2)]
            sg = [A.alloc(256, "sg%d" % i) for i in range(2)]
            wdA = A.alloc(NHC * 256, "wdA")
            for hp in range(NHC // 2):
                s = hp % 2
                gv = wg[s].bf().rearrange("p (k n) -> p k n", n=256)
                uv = wu[s].bf().rearrange("p (k n) -> p k n", n=256)
                P.dma("pool", gv, wgu_v[:, :, hp * 256:(hp + 1) * 256], writes=[wg[s].b])
                P.dma("pool", uv, wgu_v[:, :, DFF + hp * 256:DFF + (hp + 1) * 256], writes=[wu[s].b])
                if hp == 1:
                    P.dma("pool", wdA.bf().rearrange("p (h n) -> p h n", n=512), wd_v[:, :, 0:512], writes=[wdA.b])
                for j in range(2):
                    hc = 2 * hp + j
                    for (c0, c1) in blocks:
                        w = c1 - c0
                        r = cnt % 2
                        cnt += 1
                        pg, pgb = self.ps(2 * r)
                        pu, pub = self.ps(2 * r + 1)
                        for kc in range(KC):
                            self.mm(pg[:, 0:w], gv[:, kc, j * 128:(j + 1) * 128], xn.ap(kc, c0, c1),
                                    kc == 0, kc == KC - 1, [wg[s].b] + xn.bufs(kc, c0, c1), [pgb])
                        for kc in range(KC):
                            self.mm(pu[:, 0:w], uv[:, kc, j * 128:(j + 1) * 128], xn.ap(kc, c0, c1),
                                    kc == 0, kc == KC - 1, [wu[s].b] + xn.bufs(kc, c0, c1), [pub])
                        sgt = sg[r]
                        self.act(sgt.bf(0, w), pg[:, 0:w], AF.Silu, [pgb], [sgt.b])
                        self.tt(h1.ap(hc, c0, c1), pu[:, 0:w], sgt.bf(0, w), ALU.mult, [pub, sgt.b],
                                h1.bufs(hc, c0, c1))
                        P.replay(4)
            xn.free()
            for t_ in wg + wu + sg:
                t_.free()
            wdB = A.alloc(NHC * 256, "wdB")
            P.dma("pool", wdB.bf().rearrange("p (h n) -> p h n", n=512), wd_v[:, :, 512:1024], writes=[wdB.b])
            if si + 1 < len(sbs) and on_block_done is None:
                n0, n1 = sbs[si + 1]
                try:
                    xn_next = FMat(A, KC, BF16, "xn", n0, n1)
                except RuntimeError:
                    xn_next = None
                if xn_next is not None:
                    for (c0, c1) in [(c0, c1) for (c0, c1) in TB if n0 <= c0 < n1]:
                        self.rmsnorm_to(self.hT, gcol, xn_next, c0, c1)
            nb = len(blocks)
            items = [(0, 0)]
            for bi in range(1, nb):
                items += [(0, bi), (1, bi - 1)]
            items += [(1, nb - 1)]
            for half, bi in items:
                wd = (wdA, wdB)[half]
                wv = wd.bf().rearrange("p (h n) -> p h n", n=512)
                for (c0, c1) in [blocks[bi]]:
                    w = c1 - c0
                    for o in range(4):
                        oc = half * 4 + o
                        pt, ptb = self.ps(4 + cnt % 2)
                        cnt += 1
                        for hc in range(NHC):
                            self.mm(pt[:, 0:w], wv[:, hc, o * 128:(o + 1) * 128], h1.ap(hc, c0, c1),
                                    hc == 0, hc == NHC - 1, [wd.b] + h1.bufs(hc, c0, c1), [ptb])
                        hb = self.hT.bufs(oc, c0, c1)
                        self.stt(self.hT.ap(oc, c0, c1), pt[:, 0:w], 0.5, self.hT.ap(oc, c0, c1), ALU.mult, ALU.add,
                                 [ptb] + hb, hb)
                if half == 1 and on_block_done is not None:
                    on_block_done(*blocks[bi])
            h1.free()
            wdA.free()
            wdB.free()

    def final_out(self, yp, ys, only=None):
        P, A = self.P, self.A
        ost = [A.alloc(1024, "ostg%d" % i) for i in range(2)]
        cnt = 0
        for (c0, c1) in (TB if only is None else [only]):
            yn = FMat(A, KC, F32, "yn", c0, c1)
            self.rmsnorm_to(self.hT, self.gcol["g_final"], yn, c0, c1)
            for tt in range((c1 - c0) // 128):
                t0 = c0 + tt * 128
                o = ost[cnt % 2]
                cnt += 1
                for half in range(2):
                    pst, psb = self.ps(half)
                    for j in range(4):
                        kc = half * 4 + j
                        self.tr(pst[:, j * 128:(j + 1) * 128], yn.ap(kc, t0, t0 + 128), self.ident_f,
                                yn.bufs(kc, t0, t0 + 128) + [self.cb], [psb])
                    self.cp(o.f32(half * 512, half * 512 + 512), pst[:, :], [psb], [o.b],
                            eng=("dve" if half == 0 else "act"))
                dst = yp[t0:t0 + 128, :] if t0 < TP else ys[:, :]
                P.dma("sp", dst, o.f32(), reads=[o.b])
            yn.free()
        for o in ost:
            o.free()


    def dump(self, name, ap, ncols, reads, npart=128):
        P, A = self.P, self.A
        d = self.dram_out(name, (npart, ncols))
        t = A.alloc(ncols, "dump")
        self.cp(t.f32(0, ncols)[0:npart, :], ap, reads, [t.b])
        P.dma("sp", d, t.f32(0, ncols)[0:npart, :], reads=[t.b])
        t.free()

    def mix(self, okp, ovp, oks, ovs, mid=None):
        P, A, W = self.P, self.A, self.W
        xnm = FMat(A, KC, BF16, "xnm")
        for (c0, c1) in TB:
            self.rmsnorm_to(self.hT, self.gcol["g_mix"], xnm, c0, c1)
        win_v = W["w_in"].rearrange("(k p) n -> p k n", p=128)
        wi = [A.alloc(2048, "wi%d" % i) for i in range(2)]
        wiv = [t.bf().rearrange("p (k n) -> p k n", n=512) for t in wi]
        self.uD = A.alloc(4 * T // 2, "uD", nbufs=4)
        cnt = [0]

        def fm_pass(slot, colblk, dst):
            P.dma("pool", wiv[slot], win_v[:, :, colblk * 512:(colblk + 1) * 512], writes=[wi[slot].b])
            for oc in range(4):
                for (c0, c1) in TB:
                    w = c1 - c0
                    pt, ptb = self.ps(cnt[0] % 2)
                    for kc in range(KC):
                        self.mm(pt[:, 0:w], wiv[slot][:, kc, oc * 128:(oc + 1) * 128], xnm.ap(kc, c0, c1),
                                kc == 0, kc == KC - 1, [wi[slot].b] + xnm.bufs(kc, c0, c1), [ptb])
                    if dst is None:
                        d_ap = self.raw(BF16, 0, 128, self.uD.off * 2 + oc * T + c0 // 8, [[1, w // 8], [T // 8, 8]])
                        self.cp(d_ap, pt[:, 0:w].rearrange("p (c s) -> p c s", s=8), [ptb], [self.uD.bufs[oc]],
                                eng=("dve" if cnt[0] % 2 == 0 else "act"))
                    else:
                        self.cp(dst.ap(oc, c0, c1), pt[:, 0:w], [ptb], dst.bufs(oc, c0, c1),
                                eng=("dve" if cnt[0] % 2 == 0 else "act"))
                    cnt[0] += 1

        import os
        ms = int(os.environ.get("MIXSTOP", "9"))
        fm_pass(0, 0, None)
        if ms <= 1:
            return
        if mid is not None:
            for t_ in wi:
                t_.free()
            xnm.free()
            mid()
            xnm = FMat(A, KC, BF16, "xnm")
            for (c0, c1) in TB:
                self.rmsnorm_to(self.hT, self.gcol["g_mix"], xnm, c0, c1)
            wi[:] = [A.alloc(2048, "wi%d" % i) for i in range(2)]
            wiv[:] = [t.bf().rearrange("p (k n) -> p k n", n=512) for t in wi]
        self.qT = FMat(A, 4, BF16, "qT")
        self.kT = FMat(A, 4, BF16, "kT")
        fm_pass(1, 1, self.qT)
        fm_pass(0, 2, self.kT)
        if ms <= 2:
            return
        P.dma("pool", wiv[1], win_v[:, :, 1536:2048], writes=[wi[1].b])
        self.vtok = A.alloc(16 * 256, "vtok", nbufs=16)
        self.vts = [A.alloc(256, "vts%d" % s) for s in range(2)]
        stg = [A.alloc(512, "kvstg%d" % i) for i in range(4)]
        tiles = [(128 * i, 128, okp[128 * i:128 * i + 128, :], ovp[128 * i:128 * i + 128, :]) for i in range(16)]
        tiles += [(TP + 64 * s, 64, oks[64 * s:64 * s + 64, :], ovs[64 * s:64 * s + 64, :]) for s in range(2)]
        sc = 0
        mkv = int(os.environ.get("MIXKV", "0"))
        if mkv == 1:
            tiles = tiles[:16]
        for ti, (col0, nt, dk, dv) in enumerate(tiles):
            for which in range(2):
                pt, ptb = self.ps(2 + cnt[0] % 2)
                cnt[0] += 1
                for kc in range(KC):
                    self.mm(pt[0:nt, :], xnm.ap(kc, col0, col0 + nt), wiv[which][:, kc, :],
                            kc == 0, kc == KC - 1, [wi[which].b] + xnm.bufs(kc, col0, col0 + nt), [ptb])
                st = stg[sc % 4]
                sc += 1
                self.cp(st.f32(0, 512)[0:nt, :], pt[0:nt, :], [ptb], [st.b], eng="act")
                if mkv != 3:
                    P.dma("sp", dk if which == 0 else dv, st.f32(0, 512)[0:nt, :], reads=[st.b])
                if which == 1 and mkv != 2:
                    if ti < 16:
                        self.cp(self.vtok.bf(ti * 512, (ti + 1) * 512), st.f32(0, 512), [st.b],
                                [self.vtok.bufs[ti]])
                    else:
                        v = self.vts[ti - 16]
                        self.cp(v.bf(0, 512)[0:64, :], st.f32(0, 512)[0:64, :], [st.b], [v.b])
        xnm.free()
        for t_ in wi + stg:
            t_.free()

    def attn_consts_prompt(self):
        P, A, W = self.P, self.A, self.W
        slopes = self.slopes
        it = A.alloc(512, "iota_qk")
        P.op("pool", lambda e: e.iota(it.f32(), pattern=[[1, 512]], base=0, channel_multiplier=-1,
                                      allow_small_or_imprecise_dtypes=True), writes=[it.b])
        self.Nh = A.alloc(4 * 512, "alibiN")
        self.Wh = A.alloc(4 * 512, "alibiW")
        ita = A.alloc(512, "iota_abs")
        self.ts(ita.f32(), it.f32(), -1.0, None, ALU.mult, None, [it.b], [ita.b])
        self.tt(ita.f32(), ita.f32(), it.f32(), ALU.max, [ita.b, it.b], [ita.b])
        for h in range(4):
            self.ts(self.Nh.f32(h * 512, (h + 1) * 512), it.f32(), -slopes[h], None, ALU.mult, None,
                    [it.b], [self.Nh.b])
            self.ts(self.Wh.f32(h * 512, (h + 1) * 512), ita.f32(), -slopes[h], None, ALU.mult, None,
                    [ita.b], [self.Wh.b])
            self.mset(self.Wh.f32(h * 512, h * 512 + 64)[64:128, :], -30000.0, [self.Wh.b], [self.Wh.b])
        self.EB = A.alloc(64, "expbias")
        i16 = A.alloc(16, "iota16")
        P.op("pool", lambda e: e.iota(i16.f32(), pattern=[[1, 16]], base=0, channel_multiplier=0,
                                      allow_small_or_imprecise_dtypes=True), writes=[i16.b])
        for h in range(4):
            self.ts(self.EB.f32(h * 16, h * 16 + 16), i16.f32(), -slopes[h] * 128.0, None, ALU.mult, None,
                    [i16.b], [self.EB.b])
        for t_ in (it, ita, i16):
            t_.free()
        xt = A.alloc(128 * 5, "al_x")
        self.mset(xt.f32(), 0.0, [], [xt.b])
        ip = A.alloc(2, "al_ip")
        P.op("pool", lambda e: e.iota(ip.f32(0, 1), pattern=[[0, 1]], base=0, channel_multiplier=1,
                                      allow_small_or_imprecise_dtypes=True), writes=[ip.b])
        for base in (0,):
            self.mset(xt.f32(base, base + 2), 1.0, [xt.b], [xt.b])
            self.cp(xt.f32(base + 2, base + 3), ip.f32(0, 1), [ip.b, xt.b], [xt.b])
            for b in range(4):
                o = 128 * (1 + b) + base
                self.mset(xt.f32(o, o + 1), -256.0 * (b // 2), [xt.b], [xt.b])
                self.ts(xt.f32(o + 1, o + 2), ip.f32(0, 1), -1.0, -128.0 * (b % 2), ALU.mult, ALU.add, [ip.b, xt.b], [xt.b])
                self.mset(xt.f32(o + 2, o + 3), 1.0, [xt.b], [xt.b])
        self.ALl = A.alloc(64, "al_l")
        self.ALr = A.alloc(4 * 256, "al_r")
        ptl, ptlb = self.ps(6)
        self.tr(ptl[:, 0:128], xt.f32(0, 128), self.ident_f, [xt.b, self.cb], [ptlb])
        self.cp(self.ALl.bf(0, 128), ptl[:, 0:128], [ptlb], [self.ALl.b])
        ptr_, ptrb = self.ps(5)
        for b in range(4):
            self.tr(ptr_[:, b * 128:(b + 1) * 128], xt.f32(128 * (1 + b), 128 * (2 + b)), self.ident_f,
                    [xt.b, self.cb], [ptrb])
        for h in range(4):
            self.ts(self.ALr.bf(h * 512, (h + 1) * 512), ptr_[:, 0:512], 8.0 * slopes[h], None, ALU.mult, None,
                    [ptrb], [self.ALr.b])
        xt.free(); ip.free()

    def attn_consts(self):
        P, A, W = self.P, self.A, self.W
        slopes = [2.0 ** (-2.0 * (h + 1)) for h in range(4)]
        self.slopes = slopes
        self.SB = A.alloc(128, "sbias")
        i32 = A.alloc(32, "iota32")
        P.op("pool", lambda e: e.iota(i32.f32(), pattern=[[128, 32]], base=-PAST, channel_multiplier=1,
                                      allow_small_or_imprecise_dtypes=True), writes=[i32.b])
        for h in range(4):
            self.ts(self.SB.f32(h * 32, h * 32 + 32), i32.f32(), slopes[h], None, ALU.mult, None,
                    [i32.b], [self.SB.b])
        self.DN = A.alloc(4 * 128, "dnew")
        ia = A.alloc(64, "iota_a")
        ib = A.alloc(64, "iota_b")
        P.op("pool", lambda e: e.iota(ia.f32(), pattern=[[1, 64]], base=0, channel_multiplier=-1,
                                      allow_small_or_imprecise_dtypes=True), writes=[ia.b])
        P.op("pool", lambda e: e.iota(ib.f32(), pattern=[[1, 64]], base=0, channel_multiplier=0,
                                      allow_small_or_imprecise_dtypes=True), writes=[ib.b])
        ic = A.alloc(64, "iota_c")
        self.ts(ic.f32(), ia.f32(), -1.0, None, ALU.mult, None, [ia.b], [ic.b])
        self.tt(ia.f32(), ia.f32(), ic.f32(), ALU.max, [ia.b, ic.b], [ia.b])
        self.tt(ia.f32(), ib.f32(), ia.f32(), ALU.subtract, [ia.b, ib.b], [ia.b])
        ic.free()
        for h in range(4):
            for m in range(2):
                self.ts(self.DN.f32(h * 128 + m * 64, h * 128 + m * 64 + 64), ia.f32(), slopes[h], None,
                        ALU.mult, None, [ia.b], [self.DN.b])
        for t_ in (i32, ia, ib):
            t_.free()
        lt = A.alloc(8, "lam")
        self.lamt = lt
        self.mset(lt.f32(0, 8), 0.0, [], [lt.b])
        P.dma("sp", lt.f32(0, 1), W["lambda_q"].rearrange("(p o) -> p o", o=1), writes=[lt.b], nc_ok=True)
        P.dma("sp", lt.f32(1, 2), W["lambda_k"].rearrange("(p o) -> p o", o=1), writes=[lt.b], nc_ok=True)
        P.dma("sp", lt.f32(6, 7), W["g_subln"].rearrange("(p o) -> p o", o=1), writes=[lt.b], nc_ok=True)
        self.tt(lt.f32(2, 3), lt.f32(0, 1), lt.f32(1, 2), ALU.mult, [lt.b], [lt.b])
        pt, ptb = self.ps(6)
        pt2, ptb2 = self.ps(5)
        self.mm(pt[:, 0:2], self.ones_f[0:64, :], lt.f32(2, 4)[0:64, :], True, True, [lt.b, self.cb2], [ptb])
        self.mm(pt2[:, 0:2], self.ones_f[64:128, :], lt.f32(2, 4)[64:128, :], True, True, [lt.b, self.cb2], [ptb2])
        self.act(lt.f32(3, 4), pt[:, 0:1], AF.Exp, [ptb], [lt.b])
        self.act(lt.f32(4, 5), pt2[:, 0:1], AF.Exp, [ptb2], [lt.b])
        lam_init = 0.8 - 0.6 * math.exp(-0.3 * 0)
        self.lam_init = lam_init
        self.stt(lt.f32(5, 6), lt.f32(4, 5), -lam_init, lt.f32(3, 4), ALU.add, ALU.subtract, [lt.b], [lt.b])
        self.neglam = lt.f32(5, 6)
        self.ts(lt.f32(6, 7), lt.f32(6, 7), 1.0 - lam_init, None, ALU.mult, None, [lt.b], [lt.b])
        self.gsub = lt.f32(6, 7)

    def attn_finish(self, O, S, ncols, qsl, dst_ap, dst_bufs, wk):
        A = self.A
        on = []
        for m in range(2):
            (po, pob), (psm, psmb) = O[m], S[m]
            r = wk["R"][m]
            self.rpow(r.f32(0, ncols), qsl(psm), [psmb], [r.b])
            o = wk["On"][m]
            self.tt(o.f32(0, ncols), qsl(po), r.f32(0, ncols), ALU.mult, [pob, r.b], [o.b])
            on.append(o)
        od = wk["Od"]
        self.stt(od.f32(0, ncols), on[1].f32(0, ncols), self.neglam, on[0].f32(0, ncols), ALU.mult, ALU.add,
                 [on[0].b, on[1].b, self.lamt.b], [od.b])
        sq = wk["sq"]
        self.tt(sq.bf(0, ncols), od.f32(0, ncols), od.f32(0, ncols), ALU.mult, [od.b], [sq.b])
        pt, ptb = self.ps(6)
        self.mm(pt[:, 0:ncols], self.ones_b, sq.bf(0, ncols), True, True, [sq.b, self.cb], [ptb])
        rs = wk["rs"]
        self.rpow(rs.f32(0, ncols), pt[:, 0:ncols], [ptb], [rs.b], power=-0.5, scale=1.0 / 128, bias=self.eps_col,
                  bias_bufs=[self.epsb])
        self.stt(dst_ap, od.f32(0, ncols), self.gsub, rs.f32(0, ncols), ALU.mult, ALU.mult,
                 [od.b, rs.b, self.lamt.b], dst_bufs)

    def attention(self, ck, cv):
        P, A = self.P, self.A
        self.attn_consts()
        self.yaT = FMat(A, 4, BF16, "yattT")
        wk = {"R": [A.alloc(512, "R%d" % m) for m in range(2)],
              "On": [A.alloc(512, "On%d" % m) for m in range(2)],
              "Od": A.alloc(512, "Od"), "sq": A.alloc(256, "sq"), "rs": A.alloc(512, "rs")}
        sb = [A.alloc(512, "sbias%d" % i) for i in range(3)]
        pT = [A.alloc(256, "pT%d" % i) for i in range(6)]
        qT, kT = self.qT, self.kT
        cnt = 0
        pcnt = 0
        HK = 2048
        pipe = Pipe(5)
        sbanks = (0, 1, 6, 7)
        qz = [A.alloc(64, "qz%d" % i) for i in range(2)]
        csets = [(A.alloc(1024, "ktok%d" % i), A.alloc(1024, "vtokc%d" % i), A.alloc(1024, "kTc%d" % i))
                 for i in range(2)]
        ci_ = 0
        for s in range(2):
            qc0 = TP + 64 * s
            for h in range(4):
                Ob, Sb = self.ps(2), self.ps(3)
                qzt = qz[(s * 4 + h) % 2]
                self.mset(qzt.bf(0, 128), 0.0, [], [qzt.b])
                for m in range(2):
                    self.cp(qzt.bf(m * 64, m * 64 + 64)[64 * m:64 * m + 64, :],
                            qT.ap(h, qc0, qc0 + 64)[64 * m:64 * m + 64, :], qT.bufs(h, qc0, qc0 + 64), [qzt.b])
                first = [True]
                for half in range(2):
                    ktok, vtc, kTc = csets[ci_ % 2]
                    ci_ += 1
                    kv = ktok.bf().rearrange("p (t c) -> p t c", c=128)
                    vv = vtc.bf().rearrange("p (t c) -> p t c", c=128)
                    P.dma("pool", kv, ck[s, half * HK:(half + 1) * HK, h * 128:(h + 1) * 128].rearrange(
                        "(t p) c -> p t c", p=128), writes=[ktok.b])
                    P.dma("pool", vv, cv[s, half * HK:(half + 1) * HK, h * 128:(h + 1) * 128].rearrange(
                        "(t p) c -> p t c", p=128), writes=[vtc.b])
                    for g in range(2):
                        pt, ptb = self.ps(4 + g)
                        ptv = pt[:, :].bitcast(BF16)
                        for j in range(8):
                            self.tr(ptv[:, j * 128:(j + 1) * 128], kv[:, g * 8 + j, :], self.ident_b,
                                    [ktok.b, self.cb], [ptb])
                        self.cp(kTc.bf(g * 1024, (g + 1) * 1024), ptv, [ptb], [kTc.b], eng=("dve" if g == 0 else "act"))
                    for kt in range(16):
                        ktg = half * 16 + kt
                        pt, ptb = self.ps(sbanks[cnt % 4])
                        cnt += 1
                        p_ = pT[pcnt % 6]
                        pcnt += 1

                        def stA(pt=pt, ptb=ptb, p_=p_, kt=kt, ktg=ktg, kTc=kTc, qzt=qzt, h=h):
                            self.mm(pt[:, 0:128], kTc.bf(kt * 128, (kt + 1) * 128), qzt.bf(0, 128), True, True,
                                    [kTc.b, qzt.b], [ptb])
                            self.act(p_.bf(0, 128), pt[:, 0:128], AF.Exp, [ptb, self.SB.b], [p_.b],
                                     bias=self.SB.f32(h * 32 + ktg, h * 32 + ktg + 1), scale=0.125)

                        def stB(p_=p_, kt=kt, vv=vv, vtc=vtc, Ob=Ob, Sb=Sb, first=first):
                            self.mm(Ob[0][:, 0:128], vv[:, kt, :], p_.bf(0, 128), first[0], False, [vtc.b, p_.b], [Ob[1]])
                            self.mm(Sb[0][:, 0:128], self.ones_b, p_.bf(0, 128), first[0], False, [p_.b, self.cb], [Sb[1]])
                            first[0] = False
                        pipe.push(stA, stB)
                sbt = sb[cnt % 3]
                p_ = pT[pcnt % 6]
                pcnt += 1
                ptA, ptAb = self.ps(sbanks[cnt % 4])
                cnt += 1

                def stA2(ptA=ptA, ptAb=ptAb, sbt=sbt, p_=p_, h=h, qc0=qc0, qzt=qzt):
                    self.mm(ptA[0:64, 0:128], kT.ap(h, qc0, qc0 + 64), qzt.bf(0, 128), True, True,
                            kT.bufs(h, qc0, qc0 + 64) + [qzt.b], [ptAb])
                    self.stt(sbt.f32(0, 128)[0:64, :], ptA[0:64, 0:128], 0.125,
                             self.DN.f32(h * 128, h * 128 + 128)[0:64, :], ALU.mult, ALU.add, [ptAb, self.DN.b], [sbt.b])
                    self.act(p_.bf(0, 128)[0:64, :], sbt.f32(0, 128)[0:64, :], AF.Exp, [sbt.b], [p_.b])

                def stB2(p_=p_, s=s, h=h, Ob=Ob, Sb=Sb, qc0=qc0):
                    self.mm(Ob[0][:, 0:128], self.vts[s].bf(h * 128, (h + 1) * 128)[0:64, :], p_.bf(0, 128)[0:64, :],
                            False, True, [self.vts[s].b, p_.b], [Ob[1]])
                    self.mm(Sb[0][:, 0:128], self.ones_b[0:64, :], p_.bf(0, 128)[0:64, :], False, True,
                            [p_.b, self.cb], [Sb[1]])
                    self.attn_finish_s(Ob, Sb, self.yaT.ap(h, qc0, qc0 + 64), self.yaT.bufs(h, qc0, qc0 + 64), wk)
                pipe.push(stA2, stB2)
        pipe.flush()
        for t_ in qz + [x for cs in csets for x in cs] + [self.SB, self.DN]:
            t_.free()
        self.attn_consts_prompt()
        pipe = Pipe(5)
        pbanks = (0, 1, 6, 7)
        qzp = [A.alloc(512, "qzp%d" % i) for i in range(2)]
        for h in range(4):
            for j in range(4):
                q0 = 512 * j
                O, S = [], []
                qz_ = qzp[(h * 4 + j) % 2]
                self.mset(qz_.bf(0, 1024), 0.0, [], [qz_.b], eng="pool")
                for m in range(2):
                    self.cp(qz_.bf(m * 512, (m + 1) * 512)[64 * m:64 * m + 64, :],
                            qT.ap(h, q0, q0 + 512)[64 * m:64 * m + 64, :], qT.bufs(h, q0, q0 + 512), [qz_.b],
                            eng=("dve" if m == 0 else "act"))
                for m in range(2):
                    Ob, Sb = self.ps(2 + 2 * m), self.ps(3 + 2 * m)
                    nkt = 4 * j + 4
                    for kt in range(nkt):
                        delta = 4 * j - kt
                        lo = 0 if delta >= 1 else 128 * (kt - 4 * j)
                        n = 512 - lo
                        pt, ptb = self.ps(pbanks[cnt % 4])
                        sbt = sb[cnt % 3]
                        cnt += 1
                        p_ = pT[pcnt % 6]
                        pcnt += 1

                        def stA(pt=pt, ptb=ptb, sbt=sbt, p_=p_, kt=kt, delta=delta, lo=lo, n=n, h=h, m=m, q0=q0, qz_=qz_):
                            self.mm(pt[:, 0:n], kT.ap(h, kt * 128, (kt + 1) * 128),
                                    qz_.bf(m * 512 + lo, (m + 1) * 512), True, delta < 1,
                                    kT.bufs(h, kt * 128, (kt + 1) * 128) + [qz_.b], [ptb])
                            if delta >= 1:
                                self.mm(pt[:, 0:512], self.ALl.bf(0, 128), self.ALr.bf(h * 512, (h + 1) * 512), False, True,
                                        [self.ALl.b, self.ALr.b], [ptb])
                                self.act(p_.bf(0, n), pt[:, 0:n], AF.Exp, [ptb], [p_.b],
                                         bias=float(-self.slopes[h] * 128.0 * delta), scale=0.125)
                            else:
                                self.stt(sbt.f32(0, n), pt[:, 0:n], 0.125, self.Wh.f32(h * 512, h * 512 + n),
                                         ALU.mult, ALU.add, [ptb, self.Wh.b], [sbt.b])
                                self.act(p_.bf(0, n), sbt.f32(0, n), AF.Exp, [sbt.b], [p_.b])

                        def stB(p_=p_, kt=kt, nkt=nkt, lo=lo, n=n, h=h, Ob=Ob, Sb=Sb):
                            self.mm(Ob[0][:, lo:512], self.vtok.bf(kt * 512 + h * 128, kt * 512 + (h + 1) * 128),
                                    p_.bf(0, n), kt == 0, kt == nkt - 1, [self.vtok.bufs[kt], p_.b], [Ob[1]])
                            self.mm(Sb[0][:, lo:512], self.ones_b, p_.bf(0, n), kt == 0, kt == nkt - 1,
                                    [p_.b, self.cb], [Sb[1]])
                        pipe.push(stA, stB)
                    O.append(Ob)
                    S.append(Sb)

                def fin(O=O, S=S, h=h, q0=q0):
                    self.attn_finish(O, S, 512, lambda t: t[:, 0:512], self.yaT.ap(h, q0, q0 + 512),
                                     self.yaT.bufs(h, q0, q0 + 512), wk)
                pipe.push(None, fin)
        pipe.flush()
        for t_ in wk["R"] + wk["On"] + [wk["Od"], wk["sq"], wk["rs"]] + sb + pT:
            t_.free()
        for t_ in (self.Nh, self.Wh, self.EB, self.ALl, self.ALr) + tuple(qzp):
            t_.free()

    def attn_finish_s(self, Ob, Sb, dst_ap, dst_bufs, wk):
        r = wk["R"][0]
        self.rpow(r.f32(0, 128), Sb[0][:, 0:128], [Sb[1]], [r.b])
        o = wk["On"][0]
        self.tt(o.f32(0, 128), Ob[0][:, 0:128], r.f32(0, 128), ALU.mult, [Ob[1], r.b], [o.b])
        od = wk["Od"]
        self.stt(od.f32(0, 64), o.f32(64, 128), self.neglam, o.f32(0, 64), ALU.mult, ALU.add,
                 [o.b, self.lamt.b], [od.b])
        sq = wk["sq"]
        self.act(sq.bf(0, 64), od.f32(0, 64), AF.Square, [od.b], [sq.b])
        pt, ptb = self.ps(5)
        self.mm(pt[:, 0:64], self.ones_b, sq.bf(0, 64), True, True, [sq.b, self.cb], [ptb])
        rs = wk["rs"]
        self.rpow(rs.f32(0, 64), pt[:, 0:64], [ptb], [rs.b], power=-0.5, scale=1.0 / 128, bias=self.eps_col,
                  bias_bufs=[self.epsb])
        self.stt(dst_ap, od.f32(0, 64), self.gsub, rs.f32(0, 64), ALU.mult, ALU.mult,
                 [od.b, rs.b, self.lamt.b], dst_bufs)

    def proj_add(self, w_dram, row0, nkc, srcT, scale=1.0):
        P, A = self.P, self.A
        wv = w_dram[row0:row0 + nkc * 128, :].rearrange("(k p) n -> p k n", p=128)
        wt = [A.alloc(nkc * 256, "pw%d" % i) for i in range(2)]
        cnt = 0
        for half in range(2):
            t = wt[half]
            tv = t.bf().rearrange("p (k n) -> p k n", n=512)
            P.dma("pool", tv, wv[:, :, half * 512:(half + 1) * 512], writes=[t.b])
            for o in range(4):
                oc = half * 4 + o
                for (c0, c1) in TB:
                    w = c1 - c0
                    pt, ptb = self.ps(cnt % 2)
                    cnt += 1
                    for kc in range(nkc):
                        self.mm(pt[:, 0:w], tv[:, kc, o * 128:(o + 1) * 128], srcT.ap(kc, c0, c1),
                                kc == 0, kc == nkc - 1, [t.b] + srcT.bufs(kc, c0, c1), [ptb])
                    hb = self.hT.bufs(oc, c0, c1)
                    self.stt(self.hT.ap(oc, c0, c1), pt[:, 0:w], scale, self.hT.ap(oc, c0, c1), ALU.mult, ALU.add,
                             [ptb] + hb, hb)
        for t in wt:
            t.free()


    def raw(self, dtype, p0, npart, off, dims):
        if dtype == F32:
            return bass.AP(tensor=self.arena_t, offset=p0 * ARENA_WORDS + off,
                           ap=[[ARENA_WORDS, npart]] + [list(d) for d in dims])
        return bass.AP(tensor=self.arena_bf, offset=p0 * 2 * ARENA_WORDS + off,
                       ap=[[2 * ARENA_WORDS, npart]] + [list(d) for d in dims])

    def cmul_small(self, o_r, o_i, a_r, a_i, b_r, b_i, t1, t2, bufs):
        self.tt(t1, a_r, b_r, ALU.mult, bufs, bufs)
        self.tt(t2, a_i, b_i, ALU.mult, bufs, bufs)
        self.tt(o_r, t1, t2, ALU.subtract, bufs, bufs)
        self.tt(t1, a_r, b_i, ALU.mult, bufs, bufs)
        self.tt(t2, a_i, b_r, ALU.mult, bufs, bufs)
        self.tt(o_i, t1, t2, ALU.add, bufs, bufs)

    def s5_params(self, s5r, s5i):
        P, A, W = self.P, self.A, self.W
        NP = 16
        prm = A.alloc(128 * 3, "s5prm_in", top=True)
        pb = [prm.b]
        P.dma("sp", prm.f32(0, 128)[0:16, :], W["ssm_a_re"].rearrange("(r p) -> r p", p=128), writes=pb)
        P.dma("sp", prm.f32(0, 128)[16:32, :], W["ssm_a_im"].rearrange("(r p) -> r p", p=128), writes=pb)
        P.dma("sp", prm.f32(128, 256)[0:32, :], s5r.rearrange("s (r p) -> (s r) p", p=128), writes=pb)
        P.dma("sp", prm.f32(256, 384)[0:32, :], s5i.rearrange("s (r p) -> (s r) p", p=128), writes=pb)
        sp_ = A.alloc(32 * 24, "s5small", top=True)
        sb_ = [sp_.b]
        S = lambda i, n=16: sp_.f32(32 * i, 32 * i + n)
        pt, ptb = self.ps(6)
        for j in range(3):
            self.tr(pt[:, j * 32:(j + 1) * 32], prm.f32(j * 128, (j + 1) * 128)[0:32, :], self.ident_f[0:32, 0:32],
                    pb + [self.cb], [ptb])
        AR, AI = S(0), S(1)
        self.cp(sp_.f32(0, 16), pt[:, 0:16], [ptb], sb_)
        self.cp(sp_.f32(32, 48), pt[:, 16:32], [ptb], sb_)
        H0R, H0I = S(2, 32), S(3, 32)
        self.cp(H0R, pt[:, 32:64], [ptb], sb_)
        self.cp(H0I, pt[:, 64:96], [ptb], sb_)
        DT = S(4)
        ldt = W["ssm_log_dt"]
        for gs in range(2):
            src = bass.AP(tensor=ldt.tensor, offset=gs, ap=[[0, 64], [2, 16]])
            P.dma("sp", DT[64 * gs:64 * gs + 64, :], src, writes=sb_, nc_ok=True)
        self.act(DT, DT, AF.Exp, sb_, sb_)
        X, PH = S(5), S(6)
        self.tt(X, AR, DT, ALU.mult, sb_, sb_)
        self.tt(PH, AI, DT, ALU.mult, sb_, sb_)
        RHO = S(7)
        self.act(RHO, X, AF.Exp, sb_, sb_)
        TWO_PI = 2.0 * math.pi
        K_, R_, C_ = S(8), S(9), S(10)
        ki = sp_.f32(32 * 11, 32 * 11 + 16).bitcast(mybir.dt.int32)
        SIN, COS = S(12), S(13)

        def reduce_sin(dst, shift):
            self.ts(R_, PH, shift, None, ALU.add, None, sb_, sb_)
            self.ts(K_, R_, 1.0 / TWO_PI, None, ALU.mult, None, sb_, sb_)
            self.cp(ki, K_, sb_, sb_)
            self.cp(K_, ki, sb_, sb_)
            self.stt(R_, K_, -TWO_PI, R_, ALU.mult, ALU.add, sb_, sb_)
            self.ts(C_, R_, math.pi, TWO_PI, ALU.is_gt, ALU.mult, sb_, sb_)
            self.tt(R_, R_, C_, ALU.subtract, sb_, sb_)
            self.ts(C_, R_, -math.pi, TWO_PI, ALU.is_lt, ALU.mult, sb_, sb_)
            self.tt(R_, R_, C_, ALU.add, sb_, sb_)
            self.ts(R_, R_, 3.141592, -3.141592, ALU.min, ALU.max, sb_, sb_)
            self.act(dst, R_, AF.Sin, sb_, sb_)

        reduce_sin(SIN, 0.0)
        reduce_sin(COS, math.pi / 2.0)
        pw = A.alloc(32 * 24, "s5pow", top=True)
        pwb = [pw.b]
        PWR = lambda k: pw.f32(32 * k, 32 * k + 16)
        PWI = lambda k: pw.f32(32 * (12 + k), 32 * (12 + k) + 16)
        both = sb_ + pwb
        self.tt(PWR(0), RHO, COS, ALU.mult, both, both)
        self.tt(PWI(0), RHO, SIN, ALU.mult, both, both)
        T1, T2 = S(14), S(15)
        for k in range(1, 11):
            self.cmul_small(PWR(k), PWI(k), PWR(k - 1), PWI(k - 1), PWR(k - 1), PWI(k - 1), T1, T2, both)
        NPI = S(16, 16)
        npw = A.alloc(32 * 12, "s5npow", top=True)
        NPWI = lambda k: npw.f32(32 * k, 32 * k + 16)
        for k in range(11):
            self.ts(NPWI(k), PWI(k), -1.0, None, ALU.mult, None, pwb, [npw.b])
        powb = pwb + [npw.b]
        NR, DEN, CR, CI = S(17), S(18), S(19), S(20)
        self.ts(NR, PWR(0), -1.0, None, ALU.add, None, both, both)
        self.tt(DEN, AR, AR, ALU.mult, sb_, sb_)
        self.tt(T1, AI, AI, ALU.mult, sb_, sb_)
        self.tt(DEN, DEN, T1, ALU.add, sb_, sb_)
        self.recip(DEN, DEN, sb_, sb_)
        self.tt(T1, NR, AR, ALU.mult, sb_, sb_)
        self.tt(T2, PWI(0), AI, ALU.mult, both, both)
        self.tt(CR, T1, T2, ALU.add, sb_, sb_)
        self.tt(CR, CR, DEN, ALU.mult, sb_, sb_)
        self.tt(T1, PWI(0), AR, ALU.mult, both, both)
        self.tt(T2, NR, AI, ALU.mult, sb_, sb_)
        self.tt(CI, T1, T2, ALU.subtract, sb_, sb_)
        self.tt(CI, CI, DEN, ALU.mult, sb_, sb_)
        G0R, G0I = S(21, 32), S(22, 32)
        for s in range(2):
            sl = slice(16 * s, 16 * s + 16)
            self.cmul_small(G0R[:, sl], G0I[:, sl], PWR(3), PWI(3), H0R[:, sl], H0I[:, sl], T1, T2, both)
        pk = A.alloc(32 * 27, "s5pk", top=True)
        pkb = [pk.b]
        PKR = lambda k: pk.f32(32 * k, 32 * k + 16)
        PKI = lambda k: pk.f32(32 * (9 + k), 32 * (9 + k) + 16)
        NPKI = lambda k: pk.f32(32 * (18 + k), 32 * (18 + k) + 16)
        allb = sb_ + pwb + pkb
        self.mset(PKR(0), 1.0, [], pkb)
        self.mset(PKI(0), 0.0, [], pkb)
        self.cp(PKR(1), PWR(0), allb, pkb)
        self.cp(PKI(1), PWI(0), allb, pkb)
        for k in range(2, 9):
            self.cmul_small(PKR(k), PKI(k), PKR(k - 1), PKI(k - 1), PWR(0), PWI(0), T1, T2, allb)
        for k in range(9):
            self.ts(NPKI(k), PKI(k), -1.0, None, ALU.mult, None, pkb, pkb)
        self._s5 = dict(locals())

    def s5(self, s5r, s5i, orep, oimp, ores, oims):
        P, A, W = self.P, self.A, self.W
        L = self._s5
        (NP, sp_, sb_, S, H0R, H0I, G0R, G0I, pw, pwb, PWR, PWI, npw, NPWI, powb, pk, pkb, PKR, PKI, NPKI, T1, T2,
         both, allb) = (L[k_] for k_ in ("NP", "sp_", "sb_", "S", "H0R", "H0I", "G0R", "G0I", "pw", "pwb", "PWR", "PWI",
                                         "npw", "NPWI", "powb", "pk", "pkb", "PKR", "PKI", "NPKI", "T1", "T2", "both",
                                         "allb"))
        braw = A.alloc(2 * 256, "s5braw")
        bb_ = [braw.b]
        BRw, BIw = braw.f32(0, 256), braw.f32(256, 512)
        P.dma("sp", BRw.rearrange("p (r c) -> p r c", c=16), W["ssm_b_re"].rearrange("(r p) c -> p r c", p=128),
              writes=bb_)
        P.dma("sp", BIw.rearrange("p (r c) -> p r c", c=16), W["ssm_b_im"].rearrange("(r p) c -> p r c", p=128),
              writes=bb_)
        bbar = A.alloc(10 * 256, "s5bbar")
        bbb = [bbar.b]
        BBR, BBI, U1, U2, BKR, BKI = (bbar.f32(256 * i, 256 * (i + 1)) for i in range(6))
        bset = [[bbar.f32(256 * (2 + 4 * j + i), 256 * (3 + 4 * j + i)) for i in range(4)] for j in range(2)]
        bc = lambda t_, slot: self.raw(F32, 0, 128, t_.off + 32 * slot, [[1, 16], [0, 16]])
        v3 = lambda ap: ap.rearrange("p (r c) -> p r c", c=16)
        rb = sb_ + bb_ + bbb + pkb
        CRb, CIb = bc(sp_, 19), bc(sp_, 20)
        self.tt(v3(U1), v3(BRw), CRb, ALU.mult, rb, bbb)
        self.tt(v3(U2), v3(BIw), CIb, ALU.mult, rb, bbb)
        self.tt(BBR, U1, U2, ALU.subtract, bbb, bbb)
        self.tt(v3(U1), v3(BIw), CRb, ALU.mult, rb, bbb)
        self.tt(v3(U2), v3(BRw), CIb, ALU.mult, rb, bbb)
        self.tt(BBI, U1, U2, ALU.add, bbb, bbb)
        tin = [A.alloc(512, "s5tin%d" % i) for i in range(2)]
        self.BBk = A.alloc(16 * 512, "s5BBk")
        tcnt = 0
        for k in range(8):
            if k == 0:
                srcs = (BBR, BBI)
            else:
                U1, U2, BKR, BKI = bset[k % 2]
                kr, ki = bc(pk, k), bc(pk, 9 + k)
                self.tt(v3(U1), v3(BBR), kr, ALU.mult, rb, bbb)
                self.tt(v3(U2), v3(BBI), ki, ALU.mult, rb, bbb)
                self.tt(BKR, U1, U2, ALU.subtract, bbb, bbb)
                self.tt(v3(U1), v3(BBI), kr, ALU.mult, rb, bbb)
                self.tt(v3(U2), v3(BBR), ki, ALU.mult, rb, bbb)
                self.tt(BKI, U1, U2, ALU.add, bbb, bbb)
                srcs = (BKR, BKI)
            for ri, srcw in enumerate(srcs):
                tn = tin[tcnt % 2]
                self.mset(tn.f32(), 0.0, [], [tn.b], eng="pool")
                s_off = bbar.off + (0 if k == 0 else (4 + 4 * (k % 2)) * 256) + ri * 256
                for gs in range(2):
                    for half in range(2):
                        s_ap = self.raw(F32, 64 * gs, 64, s_off + half * 32, [[64, 4], [16, 2], [1, 16]])
                        d_ap = self.raw(BF16, 64 * gs, 64, tn.off * 2 + half * 64 + gs * 16, [[256, 4], [160, 2], [1, 16]])
                        self.cp(d_ap, s_ap, bbb, [tn.b], eng=("dve" if (gs + half) % 2 == 0 else "act"))
                ptt, pttb = self.ps(1 + tcnt % 2)
                ptv = ptt[:, :].bitcast(BF16)
                for tix in range(8):
                    self.tr(ptv[:, tix * 128:(tix + 1) * 128], tn.bf(tix * 128, (tix + 1) * 128), self.ident_b,
                            [tn.b, self.cb], [pttb])
                o_ = (ri * 8 + k) * 1024
                self.cp(self.BBk.bf(o_, o_ + 1024), ptv, [pttb], [self.BBk.b], eng=("dve" if tcnt % 2 == 0 else "act"))
                tcnt += 1
        for t_ in tin:
            t_.free()
        bbar.free(); braw.free(); L["prm"].free()
        dcol = A.alloc(4, "s5d")
        P.dma("sp", dcol.f32(0, 4), W["ssm_d"].rearrange("(k p) -> p k", p=128), writes=[dcol.b], nc_ok=True)
        PADP, PADS = 128, 4
        SW = PADP + 256 + 2 * (PADS + 8)
        scan = [A.alloc(SW, "s5scan%d" % i) for i in range(4)]
        for t_ in scan:
            self.mset(t_.f32(), 0.0, [], [t_.b])
        def sc_p(t_, shift=0, n=256):
            return t_.f32(PADP - shift, PADP - shift + n)
        def sc_s(t_, shift=0, n=8, c0=0):
            return self.raw(F32, 0, 128, t_.off + PADP + 256 + PADS - shift + c0, [[PADS + 8, 2], [1, n]])
        hpb = [A.alloc(T // 16, "s5hpb%d" % i) for i in range(2)]
        loc = [A.alloc(T // 2, "s5loc%d" % i) for i in range(2)]
        Et = [A.alloc(1024, "s5E%d" % i) for i in range(2)]
        Etmp = [A.alloc(1024, "s5Et%d" % i) for i in range(2)]
        yacc = A.alloc(T, "s5yacc")
        self.ygT = FMat(A, 4, BF16, "ygT")
        outs = A.alloc(2 * 64, "s5outs")
        crw = A.alloc(4 * 128, "s5craw")
        cc = A.alloc(2 * 512, "s5cc")
        c_src = [W["ssm_c_re"], W["ssm_c_im"]]
        ev = 0
        NCH = T // 8
        ccb = A.alloc(2 * 256, "s5ccb")
        evc = [0]
        fins = {}

        def geo(pr):
            qd, q = pr // 4, pr % 4
            half, sub = q // 2, q % 2
            return qd, q, half, sub, qd * 2 + sub

        def usl_(pr):
            qd, q, half, sub, tix = geo(pr)
            return [self.uD.bf(qd * T + s_ * NCH, qd * T + (s_ + 1) * NCH)[64 * half:64 * half + 64, :]
                    for s_ in range(8)]

        def wk_(pr, ri, k):
            qd, q, half, sub, tix = geo(pr)
            return self.BBk.bf((ri * 8 + k) * 1024 + tix * 128,
                               (ri * 8 + k) * 1024 + (tix + 1) * 128)[64 * half:64 * half + 64, :]

        def emit_G(pr):
            qd = pr // 4
            ub = [self.uD.bufs[qd]]
            usl = usl_(pr)
            if pr > 0:
                for t_ in scan:
                    self.mset(self.raw(F32, 0, 128, t_.off + PADP + 256, [[PADS + 8, 2], [1, PADS]]), 0.0,
                              [t_.b], [t_.b])
            for ri in range(2):
                ptt, pttb = self.ps(4 + ri)
                for s_ in range(8):
                    self.mm(ptt[:, 0:NCH], wk_(pr, ri, 7 - s_), usl[s_], s_ == 0, s_ == 7, [self.BBk.b] + ub, [pttb])
                dst_t = scan[ri]
                self.cp(sc_p(dst_t), ptt[:, 0:256], [pttb], [dst_t.b], eng="act")
                self.cp(sc_s(dst_t), ptt[:, 256:272].rearrange("p (s c) -> p s c", c=8), [pttb], [dst_t.b], eng="act")
                g0 = (G0R, G0I)[ri]
                for s in range(2):
                    col = dst_t.f32(PADP + 256 + s * (PADS + 8) + PADS, PADP + 256 + s * (PADS + 8) + PADS + 1)
                    self.tt(col, col, g0[:, 16 * s + pr:16 * s + pr + 1], ALU.add, [dst_t.b] + sb_, [dst_t.b])

        def emit_scan(pr):
            a_, b_ = (scan[0], scan[1]), (scan[2], scan[3])
            for k in range(8):
                d = 1 << k
                mr = pw.f32(32 * (3 + k) + pr, 32 * (3 + k) + pr + 1)
                mi = pw.f32(32 * (12 + 3 + k) + pr, 32 * (12 + 3 + k) + pr + 1)
                nmi = npw.f32(32 * (3 + k) + pr, 32 * (3 + k) + pr + 1)
                views = [sc_p]
                if k <= 2:
                    views.append(sc_s)
                for vf in views:
                    rr = [a_[0].b, a_[1].b] + powb
                    self.stt(vf(b_[0]), vf(a_[0], d), mr, vf(a_[0]), ALU.mult, ALU.add, rr, [b_[0].b])
                    self.stt(vf(b_[1]), vf(a_[1], d), mr, vf(a_[1]), ALU.mult, ALU.add, rr, [b_[1].b])
                    self.stt(vf(b_[0]), vf(a_[1], d), nmi, vf(b_[0]), ALU.mult, ALU.add, rr + [b_[0].b], [b_[0].b])
                    self.stt(vf(b_[1]), vf(a_[0], d), mi, vf(b_[1]), ALU.mult, ALU.add, rr + [b_[1].b], [b_[1].b])
                if k > 2:
                    for c in range(2):
                        self.cp(sc_s(b_[c]), sc_s(a_[c]), [a_[c].b], [b_[c].b], eng="act")
                a_, b_ = b_, a_
            fin = a_
            for c, h0 in enumerate((H0R, H0I)):
                for s in range(2):
                    col = fin[c].f32(PADP + 256 + s * (PADS + 8) + PADS - 1, PADP + 256 + s * (PADS + 8) + PADS)
                    self.cp(col, h0[:, 16 * s + pr:16 * s + pr + 1], sb_, [fin[c].b], eng="act")
            for ri in range(2):
                self.cp(hpb[ri].bf(0, 256), sc_p(fin[ri], 1), [fin[ri].b], [hpb[ri].b], eng="act")
                self.cp(hpb[ri].bf(256, 272).rearrange("p (s c) -> p s c", c=8), sc_s(fin[ri], 1), [fin[ri].b],
                        [hpb[ri].b], eng="act")
                d0 = ri * 48 + pr * 3
                self.cp(outs.f32(d0, d0 + 1), fin[ri].f32(PADP + 255, PADP + 256), [fin[ri].b], [outs.b], eng="act")
                self.cp(outs.f32(d0 + 1, d0 + 3),
                        self.raw(F32, 0, 128, fin[ri].off + PADP + 256 + PADS + 7, [[PADS + 8, 2]]),
                        [fin[ri].b], [outs.b], eng="act")

        def emit_local(pr):
            qd = pr // 4
            ub = [self.uD.bufs[qd]]
            usl = usl_(pr)
            for st in range(8):
                for ri in range(2):
                    ptt, pttb = self.ps(evc[0] % 4)
                    evc[0] += 1
                    for s2 in range(st + 1):
                        self.mm(ptt[:, 0:NCH], wk_(pr, ri, st - s2), usl[s2], s2 == 0, s2 == st, [self.BBk.b] + ub, [pttb])
                    self.cp(loc[ri].bf(st * NCH, (st + 1) * NCH), ptt[:, 0:NCH], [pttb], [loc[ri].b], eng="act")

        def emit_E(pr):
            qd, q, half, sub, tix = geo(pr)
            if q == 0:
                for ri in range(2):
                    for dup in range(2):
                        P.dma("sp", crw.f32(ri * 256 + dup * 64, ri * 256 + dup * 64 + 64),
                              c_src[ri][qd * 128:(qd + 1) * 128, :], writes=[crw.b])
                self.mset(cc.f32(), 0.0, [], [cc.b])
                ptt, pttb = self.ps(6)
                for ri in range(2):
                    self.tr(ptt[:, ri * 128:(ri + 1) * 128], crw.f32(ri * 256, ri * 256 + 128), self.ident_f,
                            [crw.b, self.cb], [pttb])
                for ri in range(2):
                    for gs in range(2):
                        for j in range(4):
                            s_ap = ptt[64 * gs:64 * gs + 64, ri * 128 + (2 * j + gs) * 16:ri * 128 + (2 * j + gs) * 16 + 16]
                            d0 = ri * 512 + j * 128 + j * 32 + gs * 16
                            d_ap = cc.f32(d0, d0 + 16)[64 * gs:64 * gs + 64, :]
                            if ri == 0:
                                self.cp(d_ap, s_ap, [pttb], [cc.b])
                            else:
                                self.ts(d_ap, s_ap, -1.0, None, ALU.mult, None, [pttb], [cc.b])
                self.cp(ccb.bf(0, 1024), cc.f32(0, 1024), [cc.b], [ccb.b], eng="act")
            E = Et[pr % 2]
            c0b = self.raw(F32, 0, 128, cc.off + q * 128, [[0, 8], [1, 128]])
            c1b = self.raw(F32, 0, 128, cc.off + 512 + q * 128, [[0, 8], [1, 128]])
            krb = self.raw(F32, 0, 128, pk.off + 32 * 1 + pr, [[32, 8], [0, 128]])
            kib = self.raw(F32, 0, 128, pk.off + 32 * 10 + pr, [[32, 8], [0, 128]])
            t1 = Etmp[0].f32().rearrange("p (k n) -> p k n", n=128)
            t2 = Etmp[1].f32().rearrange("p (k n) -> p k n", n=128)
            er = E.bf(0, 1024).rearrange("p (k n) -> p k n", n=128)
            ei = E.bf(1024, 2048).rearrange("p (k n) -> p k n", n=128)
            rb_ = [cc.b] + pkb
            tb = [Etmp[0].b, Etmp[1].b]
            self.tt(t1, c0b, krb, ALU.mult, rb_, [Etmp[0].b])
            self.tt(t2, c1b, kib, ALU.mult, rb_, [Etmp[1].b])
            self.tt(er, t1, t2, ALU.add, tb, [E.b])
            self.tt(t1, c1b, krb, ALU.mult, rb_ + [E.b], [Etmp[0].b])
            self.tt(t2, c0b, kib, ALU.mult, rb_ + [E.b], [Etmp[1].b])
            self.tt(ei, t1, t2, ALU.subtract, tb, [E.b])

        def emit_y(pr):
            qd, q, half, sub, tix = geo(pr)
            E = Et[pr % 2]
            for st in range(8):
                ptt, pttb = self.ps(4 + evc[0] % 4)
                evc[0] += 1
                sl = (st * NCH, (st + 1) * NCH)
                self.mm(ptt[:, 0:NCH], ccb.bf(q * 128, (q + 1) * 128), loc[0].bf(*sl), True, False,
                        [ccb.b, loc[0].b], [pttb])
                self.mm(ptt[:, 0:NCH], ccb.bf(512 + q * 128, 512 + (q + 1) * 128), loc[1].bf(*sl), False, False,
                        [ccb.b, loc[1].b], [pttb])
                self.mm(ptt[:, 0:NCH], E.bf(st * 128, (st + 1) * 128), hpb[0].bf(0, NCH), False, False,
                        [E.b, hpb[0].b], [pttb])
                self.mm(ptt[:, 0:NCH], E.bf(1024 + st * 128, 1024 + (st + 1) * 128), hpb[1].bf(0, NCH), False, True,
                        [E.b, hpb[1].b], [pttb])
                if q == 0:
                    self.cp(yacc.f32(*sl), ptt[:, 0:NCH], [pttb], [yacc.b], eng="act")
                else:
                    self.tt(yacc.f32(*sl), ptt[:, 0:NCH], yacc.f32(*sl), ALU.add, [pttb, yacc.b], [yacc.b])
            if q == 3:
                for (c0, c1) in TB:
                    self.stt(yacc.f32(c0, c1), self.uD.bf(qd * T + c0, qd * T + c1), dcol.f32(qd, qd + 1),
                             yacc.f32(c0, c1), ALU.mult, ALU.add, [self.uD.bufs[qd], dcol.b, yacc.b], [yacc.b])
                for st in range(8):
                    d_ap = self.raw(BF16, 0, 128, self.ygT.t.off * 2 + qd * T + st, [[8, NCH]])
                    self.act(d_ap, yacc.f32(st * NCH, (st + 1) * NCH), AF.Gelu_apprx_tanh, [yacc.b],
                             self.ygT.bufs(qd, 0, T))

        emit_G(0)
        for pr in range(NP):
            emit_E(pr)
            emit_local(pr)
            emit_scan(pr)
            if pr + 1 < NP:
                emit_G(pr + 1)
            emit_y(pr)
        for t_ in scan + hpb + loc + Et + Etmp:
            t_.free()
        ptt, pttb = self.ps(0)
        ostt = A.alloc(128, "s5ostg")
        for ri in range(2):
            for wch in range(3):
                s_ap = self.raw(F32, 0, 128, outs.off + ri * 48 + wch, [[3, 16]])
                col = (ri * 3 + wch) * 16
                self.tr(ptt[0:16, col * 8:col * 8 + 128] if False else ptt[0:16, (ri * 3 + wch) * 128:(ri * 3 + wch) * 128 + 128]
                        if (ri * 3 + wch) < 4 else self.psum[1][0:16, (ri * 3 + wch - 4) * 128:(ri * 3 + wch - 4) * 128 + 128],
                        s_ap, self.ident_f, [outs.b, self.cb], [pttb if (ri * 3 + wch) < 4 else self.psb[1]])
        ost2 = A.alloc(6 * 128, "s5ostg2")
        self.cp(ost2.f32(0, 512)[0:16, :], ptt[0:16, :], [pttb], [ost2.b])
        self.cp(ost2.f32(512, 768)[0:16, :], self.psum[1][0:16, 0:256], [self.psb[1]], [ost2.b])
        dsts = [[orep.rearrange("(r p) -> r p", p=128), ores[0].rearrange("(r p) -> r p", p=128),
                 ores[1].rearrange("(r p) -> r p", p=128)],
                [oimp.rearrange("(r p) -> r p", p=128), oims[0].rearrange("(r p) -> r p", p=128),
                 oims[1].rearrange("(r p) -> r p", p=128)]]
        for ri in range(2):
            for wch in range(3):
                k = ri * 3 + wch
                P.dma("sp", dsts[ri][wch], ost2.f32(k * 128, (k + 1) * 128)[0:16, :], reads=[ost2.b])
        for t_ in [ccb, yacc, outs, crw, cc, ostt, ost2, dcol, pw, npw, sp_, pk, self.BBk]:
            t_.free()
        self.uD.free()
        self.yssT = FMat(A, 4, BF16, "yssT")
        wg_ = A.alloc(1024, "wglu")
        wgv = wg_.bf().rearrange("p (k n) -> p k n", n=512)
        P.dma("pool", wgv, W["w_glu"].rearrange("(k p) n -> p k n", p=128), writes=[wg_.b])
        bg = A.alloc(4, "bglu")
        P.dma("sp", bg.f32(0, 4), W["b_glu"].rearrange("(k p) -> p k", p=128), writes=[bg.b], nc_ok=True)
        gt_ = [A.alloc(256, "gate%d" % i) for i in range(2)]
        cnt = 0
        for oc in range(4):
            for (c0, c1) in TB:
                w = c1 - c0
                ptt, pttb = self.ps(cnt % 2)
                g_ = gt_[cnt % 2]
                cnt += 1
                for kc in range(4):
                    self.mm(ptt[:, 0:w], wgv[:, kc, oc * 128:(oc + 1) * 128], self.ygT.ap(kc, c0, c1),
                            kc == 0, kc == 3, [wg_.b] + self.ygT.bufs(kc, c0, c1), [pttb])
                self.act(g_.bf(0, w), ptt[:, 0:w], AF.Sigmoid, [pttb, bg.b], [g_.b], bias=bg.f32(oc, oc + 1))
                self.tt(self.yssT.ap(oc, c0, c1), self.ygT.ap(oc, c0, c1), g_.bf(0, w), ALU.mult,
                        self.ygT.bufs(oc, c0, c1) + [g_.b], self.yssT.bufs(oc, c0, c1))
        for t_ in gt_ + [wg_, bg]:
            t_.free()
        self.ygT.free()


    def proj_to(self, w_dram, nkc, srcT, dstT):
        P, A = self.P, self.A
        wv = w_dram.rearrange("(k p) n -> p k n", p=128)
        wt = [A.alloc(nkc * 256, "pw%d" % i) for i in range(2)]
        cnt = 0
        for half in range(2):
            t = wt[half]
            tv = t.bf().rearrange("p (k n) -> p k n", n=512)
            P.dma("pool", tv, wv[:, :, half * 512:(half + 1) * 512], writes=[t.b])
            for o in range(4):
                oc = half * 4 + o
                for (c0, c1) in TB:
                    w = c1 - c0
                    pt, ptb = self.ps(cnt % 2)
                    for kc in range(nkc):
                        self.mm(pt[:, 0:w], tv[:, kc, o * 128:(o + 1) * 128], srcT.ap(kc, c0, c1),
                                kc == 0, kc == nkc - 1, [t.b] + srcT.bufs(kc, c0, c1), [ptb])
                    self.cp(dstT.ap(oc, c0, c1), pt[:, 0:w], [ptb], dstT.bufs(oc, c0, c1),
                            eng=("dve" if cnt % 2 == 0 else "act"))
                    cnt += 1
        for t in wt:
            t.free()

    def mem_sets(self, memp, cmk, cmv, omkp, omvp):
        P, A, W = self.P, self.A, self.W
        self.mkT = [A.alloc(1024, "mkT%d" % i) for i in range(3)]
        self.mvt = [A.alloc(1024, "mvt%d" % i) for i in range(3)]
        mraw = A.alloc(2048, "mraw")
        P.dma("sp", mraw.f32().rearrange("p (t c) -> p t c", c=1024), memp.rearrange("(t p) c -> p t c", p=128),
              writes=[mraw.b])
        mT = A.alloc(KC * 256, "mT")
        for tt in range(2):
            for half in range(2):
                pst, psb = self.ps(half)
                for j in range(4):
                    kc = half * 4 + j
                    self.tr(pst[:, j * 128:(j + 1) * 128], mraw.f32(tt * 1024 + kc * 128, tt * 1024 + (kc + 1) * 128),
                            self.ident_f, [mraw.b, self.cb], [psb])
                dst = mT.f32().rearrange("p (k t) -> p k t", t=256)[:, half * 4:half * 4 + 4, tt * 128:(tt + 1) * 128]
                self.cp(dst, pst[:, :].rearrange("p (k t) -> p k t", t=128), [psb], [mT.b],
                        eng=("dve" if half == 0 else "act"))
        mraw.free()
        mn = A.alloc(KC * 128, "mnT")
        self.rmsnorm_gen(lambda kc: (mT.f32(kc * 256, (kc + 1) * 256), [mT.b]), self.gcol["g_mem"],
                         lambda kc: (mn.bf(kc * 256, (kc + 1) * 256), [mn.b]), 256)
        mT.free()
        wt = [A.alloc(2048, "mw%d" % i) for i in range(2)]
        stg = [A.alloc(512, "mstg%d" % i) for i in range(2)]
        cnt = 0
        for wi_, (wd_, odst) in enumerate(((W["w_ck"], omkp), (W["w_cv"], omvp))):
            wv = wd_.rearrange("(k p) n -> p k n", p=128)
            for half in range(2):
                t = wt[cnt % 2]
                tv = t.bf().rearrange("p (k n) -> p k n", n=512)
                P.dma("pool", tv, wv[:, :, half * 512:(half + 1) * 512], writes=[t.b])
                if wi_ == 0:
                    for o in range(4):
                        oc = half * 4 + o
                        pt, ptb = self.ps(2 + cnt % 2)
                        for kc in range(KC):
                            self.mm(pt[:, 0:256], tv[:, kc, o * 128:(o + 1) * 128], mn.bf(kc * 256, (kc + 1) * 256),
                                    kc == 0, kc == KC - 1, [t.b, mn.b], [ptb])
                        self.cp(self.mkT[0].bf(oc * 256, (oc + 1) * 256), pt[:, 0:256], [ptb], [self.mkT[0].b])
                for tt in range(2):
                    pt, ptb = self.ps(4 + (cnt + tt) % 2)
                    for kc in range(KC):
                        self.mm(pt[:, :], mn.bf(kc * 256 + tt * 128, kc * 256 + (tt + 1) * 128), tv[:, kc, :],
                                kc == 0, kc == KC - 1, [t.b, mn.b], [ptb])
                    st = stg[(cnt + tt) % 2]
                    self.cp(st.f32(0, 512), pt[:, :], [ptb], [st.b], eng="act")
                    P.dma("sp", odst[tt * 128:(tt + 1) * 128, half * 512:(half + 1) * 512], st.f32(0, 512), reads=[st.b])
                    if wi_ == 1:
                        self.cp(self.mvt[0].bf(tt * 1024 + half * 512, tt * 1024 + (half + 1) * 512), st.f32(0, 512),
                                [st.b], [self.mvt[0].b])
                cnt += 1
        for t_ in wt + stg + [mn]:
            t_.free()
        for s in range(2):
            kraw = A.alloc(1024, "ckraw")
            P.dma("pool", kraw.bf().rearrange("p (t c) -> p t c", c=1024), cmk[s].rearrange("(t p) c -> p t c", p=128),
                  writes=[kraw.b])
            P.dma("pool", self.mvt[1 + s].bf().rearrange("p (t c) -> p t c", c=1024),
                  cmv[s].rearrange("(t p) c -> p t c", p=128), writes=[self.mvt[1 + s].b])
            for tt in range(2):
                pt, ptb = self.ps(tt)
                ptv = pt[:, :].bitcast(BF16)
                for kc in range(KC):
                    self.tr(ptv[:, kc * 128:(kc + 1) * 128], kraw.bf(tt * 1024 + kc * 128, tt * 1024 + (kc + 1) * 128),
                            self.ident_b, [kraw.b, self.cb], [ptb])
                dst = self.mkT[1 + s].bf().rearrange("p (k t) -> p k t", t=256)[:, :, tt * 128:(tt + 1) * 128]
                self.cp(dst, ptv.rearrange("p (k t) -> p k t", t=128), [ptb], [self.mkT[1 + s].b],
                        eng=("dve" if tt == 0 else "act"))
            kraw.free()

    def cross_attn(self):
        P, A, W = self.P, self.A, self.W
        xnc = FMat(A, KC, BF16, "xnc")
        for (c0, c1) in TB:
            self.rmsnorm_to(self.hT, self.gcol["g_cross"], xnc, c0, c1)
        qc = FMat(A, KC, BF16, "qcT")
        self.proj_to(W["w_cq"], KC, xnc, qc)
        xnc.free()
        oc_ = FMat(A, KC, BF16, "ocT")
        pT = [A.alloc(256, "cpT%d" % i) for i in range(5)]
        R = [A.alloc(512, "cR%d" % i) for i in range(2)]
        qsets = [(0, c0, c1) for (c0, c1) in TB[:4]] + [(1, TP, TP + 64), (2, TP + 64, T)]
        cnt = 0
        it = 0
        pipe = Pipe(4)
        for (ms, c0, c1) in qsets:
            w = c1 - c0
            kT_ = self.mkT[ms].bf().rearrange("p (k t) -> p k t", t=256)
            vt_ = self.mvt[ms].bf().rearrange("p (t c) -> p t c", c=1024)
            for hh in range(4):
                banks = (2, 3, 4) if it % 2 == 0 else (5, 6, 7)
                O0, O1, Sm = self.ps(banks[0]), self.ps(banks[1]), self.ps(banks[2])
                r = R[it % 2]
                it += 1
                for kt in range(2):
                    pt, ptb = self.ps(cnt % 2)
                    p_ = pT[cnt % 5]
                    cnt += 1

                    def stA(pt=pt, ptb=ptb, p_=p_, kt=kt, hh=hh, ms=ms, c0=c0, c1=c1, w=w, kT_=kT_):
                        for dc in range(2):
                            self.mm(pt[:, 0:w], kT_[:, 2 * hh + dc, kt * 128:(kt + 1) * 128], qc.ap(2 * hh + dc, c0, c1),
                                    dc == 0, dc == 1, [self.mkT[ms].b] + qc.bufs(2 * hh + dc, c0, c1), [ptb])
                        self.act(p_.bf(0, w), pt[:, 0:w], AF.Exp, [ptb], [p_.b], scale=1.0 / 16.0)

                    def stB(p_=p_, kt=kt, hh=hh, ms=ms, c0=c0, c1=c1, w=w, vt_=vt_, O0=O0, O1=O1, Sm=Sm, r=r):
                        for dc, Ob in enumerate((O0, O1)):
                            self.mm(Ob[0][:, 0:w], vt_[:, kt, (2 * hh + dc) * 128:(2 * hh + dc + 1) * 128], p_.bf(0, w),
                                    kt == 0, kt == 1, [self.mvt[ms].b, p_.b], [Ob[1]])
                        self.mm(Sm[0][:, 0:w], self.ones_b, p_.bf(0, w), kt == 0, kt == 1, [p_.b, self.cb], [Sm[1]])
                        if kt == 1:
                            self.rpow(r.f32(0, w), Sm[0][:, 0:w], [Sm[1]], [r.b])
                            for dc, Ob in enumerate((O0, O1)):
                                self.tt(oc_.ap(2 * hh + dc, c0, c1), Ob[0][:, 0:w], r.f32(0, w), ALU.mult, [Ob[1], r.b],
                                        oc_.bufs(2 * hh + dc, c0, c1))
                    pipe.push(stA, stB)
        pipe.flush()
        qc.free()
        for t_ in pT + R + self.mkT + self.mvt:
            t_.free()
        self.proj_add(W["w_co"], 0, KC, oc_)
        oc_.free()


_CACHE = {}


def _get_nc(stage=99):
    if stage not in _CACHE:
        _CACHE[stage] = Builder(stage).build()
    return _CACHE[stage]


def _in_maps(inp):
    f = lambda a: np.ascontiguousarray(np.asarray(a, dtype=np.float32))
    maps = []
    shared = {}
    for nm in ["g_ffn1", "w_ffn1_gu", "w_ffn1_d", "g_mix", "w_in", "ssm_a_re", "ssm_a_im", "ssm_log_dt",
               "ssm_b_re", "ssm_b_im", "ssm_c_re", "ssm_c_im", "ssm_d", "w_glu", "b_glu", "lambda_q",
               "lambda_k", "g_subln", "w_out", "g_mem", "g_cross", "w_cq", "w_ck", "w_cv", "w_co",
               "g_ffn2", "w_ffn2_gu", "w_ffn2_d"]:
        shared[nm] = f(inp[nm])[0]
    shared["g_final"] = f(inp["g_final"])
    for nm in ["ssm_a_re", "ssm_a_im"]:
        shared[nm] = shared[nm].reshape(2048)
    for nm in ["ssm_b_re", "ssm_b_im"]:
        shared[nm] = shared[nm].reshape(2048, 16)
    for nm in ["ssm_c_re", "ssm_c_im"]:
        shared[nm] = shared[nm].reshape(512, 64)
    for nm in ["lambda_q", "lambda_k"]:
        shared[nm] = shared[nm].reshape(128)
    xp = f(inp["x_prompt"]); xs = f(inp["x_sample"])
    ck = f(inp["cache_attn_k"])[0]; cv = f(inp["cache_attn_v"])[0]
    sr = f(inp["state_s5_re"])[0]; si = f(inp["state_s5_im"])[0]
    cmk = f(inp["cache_mem_k"])[0]; cmv = f(inp["cache_mem_v"])[0]
    mp = f(inp["mem_prompt"])
    for c in range(NCORES):
        m = dict(shared)
        m["xp"] = xp[c]
        m["xs"] = xs[2 * c:2 * c + 2].reshape(TS, D)
        m["ck"] = ck[2 * c:2 * c + 2].reshape(2, PAST, 512)
        m["cv"] = cv[2 * c:2 * c + 2].reshape(2, PAST, 512)
        m["s5r"] = sr[2 * c:2 * c + 2].reshape(2, 2048)
        m["s5i"] = si[2 * c:2 * c + 2].reshape(2, 2048)
        m["cmk"] = cmk[2 * c:2 * c + 2].reshape(2, NMEM, D)
        m["cmv"] = cmv[2 * c:2 * c + 2].reshape(2, NMEM, D)
        m["memp"] = mp[c]
        maps.append(m)
    return maps


def kernel(**inp):
    nc = _get_nc()
    res = run_bass_kernel_spmd(nc, _in_maps(inp), core_ids=list(range(NCORES)))
    R = res.results
    cat = lambda k: np.stack([np.asarray(r[k], dtype=np.float32) for r in R], axis=0)
    y_prompt = cat("yp").reshape(8, TP, D)
    y_sample = cat("ys").reshape(16, 64, D)
    kp = cat("kp").reshape(1, 8, TP, 4, 128)
    vp = cat("vp").reshape(1, 8, TP, 4, 128)
    rep = cat("rep").reshape(1, 8, 32, 64)
    imp = cat("imp").reshape(1, 8, 32, 64)
    mkp = cat("mkp").reshape(1, 8, NMEM, 4, 256)
    mvp = cat("mvp").reshape(1, 8, NMEM, 4, 256)
    ks = cat("ks").reshape(1, 16, 64, 4, 128)
    vs = cat("vs").reshape(1, 16, 64, 4, 128)
    res_ = cat("res").reshape(1, 16, 32, 64)
    ims_ = cat("ims").reshape(1, 16, 32, 64)
    return (y_prompt, y_sample, kp, vp, rep, imp, mkp, mvp, ks, vs, res_, ims_)
```

```python
import math
from contextlib import ExitStack
import numpy as np
import concourse.bass as bass
import concourse.mybir as mybir
from concourse.bass_utils import run_bass_kernel_spmd

F32 = mybir.dt.float32
BF16 = mybir.dt.bfloat16
AF = mybir.ActivationFunctionType
ALU = mybir.AluOpType

NCORES = 8
D = 1024
KC = 8
TP = 2048
TS = 128
T = TP + TS
DFF = 2816
NHC = 22
EPS = 1e-6
PAST = 4096
NMEM = 256
TB = [(0, 512), (512, 1024), (1024, 1536), (1536, 2048), (2048, 2176)]
ARENA_WORDS = 52480


class Buf:
    __slots__ = ("name", "writer", "readers", "inherit", "sem", "excl", "semq")

    def __init__(self, name, excl=False):
        self.name = name
        self.excl = excl
        self.semq = None
        self.writer = None
        self.readers = {}
        self.inherit = ()
        self.sem = None


class Op:
    __slots__ = ("eng", "fn", "cdeps", "ddeps", "signal", "kind", "sem", "val")


class Prog:
    ENG = ("pe", "act", "dve", "pool", "sp")

    def __init__(self, nc, es):
        self.nc = nc
        self.ops = []
        self.byeng = {e: [] for e in self.ENG}
        self.esem = {e: es.enter_context(nc.semaphore("e_" + e)) for e in self.ENG}
        self.dsems = [es.enter_context(nc.semaphore("d%d" % i)) for i in range(90)]
        self.dnext = 0
        self.semcount = {}
        self.dfree = {"sp": [], "pool": [], "act": []}
        self.defer = False
        self.deferred = []

    def _hz(self, h, cdeps, ddeps):
        if h is None:
            return
        if h[0] == 'c':
            cdeps.add(h[1])
        else:
            ddeps[h[1]] = 16 * self.semcount[h[1]]

    def _deps(self, reads, writes):
        cdeps, ddeps = set(), {}
        for b in reads:
            self._hz(b.writer, cdeps, ddeps)
            for h in b.inherit:
                self._hz(h, cdeps, ddeps)
        for b in writes:
            self._hz(b.writer, cdeps, ddeps)
            for h in b.readers.values():
                self._hz(h, cdeps, ddeps)
            for h in b.inherit:
                self._hz(h, cdeps, ddeps)
        return cdeps, ddeps

    def _commit(self, op, reads, writes, tag):
        oid = len(self.ops)
        self.ops.append(op)
        self.byeng[op.eng].append(oid)
        for d in op.cdeps:
            if not (op.eng == "pe" and self.ops[d].eng == "pe"):
                self.ops[d].signal = True
        h = ('c', oid) if op.kind == 'c' else ('d', op.sem)
        key = op.eng if op.kind == 'c' else ('d', id(op.sem))
        for b in reads:
            b.readers[key] = h
        for b in writes:
            b.writer = h
            b.readers = {}
            b.inherit = ()
        return oid

    def replay(self, n):
        q = self.deferred
        while n > 0 and q:
            it = q.pop(0)
            if it[0] == "op":
                self.op(*it[1:])
            else:
                self.dma(*it[1:6], owner=it[6], nc_ok=it[7])
            n -= 1

    def op(self, eng, fn, reads=(), writes=()):
        if self.defer:
            self.deferred.append(("op", eng, fn, list(reads), list(writes)))
            return
        ex = [b for b in reads if b.excl]
        if ex:
            writes = list(writes) + ex
        op = Op()
        op.eng, op.fn, op.kind, op.signal, op.sem, op.val = eng, fn, 'c', False, None, 0
        op.cdeps, op.ddeps = self._deps(reads, writes)
        return self._commit(op, reads, writes, None)

    def dma(self, q, out, in_, reads=(), writes=(), owner=None, nc_ok=False):
        if self.defer:
            self.deferred.append(("dma", q, out, in_, list(reads), list(writes), owner, nc_ok))
            return
        owner = owner or (writes[0] if writes else reads[0])
        if owner.sem is None:
            if self.dfree[q]:
                owner.sem = self.dfree[q].pop()
            else:
                owner.sem = self.dsems[self.dnext]
                self.dnext += 1
            owner.semq = q
        assert owner.semq == q, "a buffer's DMA semaphore is bound to one queue"
        sem = owner.sem
        op = Op()
        op.eng, op.kind, op.signal, op.sem = q, 'd', True, sem
        if nc_ok:
            op.fn = lambda e: e.dma_start(out=out, in_=in_, allow_slow_non_contiguous=True)
        else:
            op.fn = lambda e: e.dma_start(out=out, in_=in_)
        op.cdeps, op.ddeps = self._deps(reads, writes)
        prev = self.semcount.get(sem, 0)
        if prev:
            op.ddeps[sem] = max(op.ddeps.get(sem, 0), 16 * prev)
        self.semcount[sem] = prev + 1
        op.val = 16 * self.semcount[sem]
        return self._commit(op, reads, writes, None)

    def emit(self, block):
        for e in self.ENG:
            c = 0
            for oid in self.byeng[e]:
                op = self.ops[oid]
                if op.kind == 'c' and op.signal:
                    c += 1
                    op.val = c
        ops = self.ops
        esem = self.esem
        final = dict(self.semcount)

        def run(ename, eng):
            waited = {}
            for oid in self.byeng[ename]:
                op = ops[oid]
                waits = {}
                for d in op.cdeps:
                    dop = ops[d]
                    if dop.eng == ename and ename == "pe":
                        continue
                    s = esem[dop.eng]
                    if waits.get(s, 0) < dop.val:
                        waits[s] = dop.val
                for s, v in op.ddeps.items():
                    if waits.get(s, 0) < v:
                        waits[s] = v
                for s, v in waits.items():
                    if waited.get(s, 0) < v:
                        eng.wait_ge(s, v)
                        waited[s] = v
                ins = op.fn(eng)
                if op.kind == 'd':
                    ins.then_inc(op.sem, 16)
                elif op.signal:
                    ins.then_inc(esem[ename], 1)
            if ename == "sp":
                for s, c in final.items():
                    if waited.get(s, 0) < 16 * c:
                        eng.wait_ge(s, 16 * c)

        @block.sync
        def _(e):
            run("sp", e)

        @block.gpsimd
        def _(e):
            run("pool", e)

        @block.tensor
        def _(e):
            run("pe", e)

        @block.scalar
        def _(e):
            run("act", e)

        @block.vector
        def _(e):
            run("dve", e)


class Arena:
    def __init__(self, ap, words, prog=None):
        self.prog = prog
        self.ap = ap
        self.free = [(0, words)]
        self.ghosts = []

    def alloc(self, words, name, nbufs=1, top=False):
        words = (words + 1) // 2 * 2
        order = list(enumerate(self.free))
        if top:
            order = order[::-1]
        for i, (s, e) in order:
            if e - s >= words:
                if top:
                    self.free[i] = (s, e - words)
                    s = e - words
                else:
                    self.free[i] = (s + words, e)
                t = Tile(self, s, words, name, nbufs)
                inh = set()
                keep = []
                for (gs, ge, hz) in self.ghosts:
                    if gs < s + words and ge > s:
                        inh |= hz
                        if gs >= s and ge <= s + words:
                            continue
                    keep.append((gs, ge, hz))
                self.ghosts = keep
                inh = frozenset(inh)
                for b in t.bufs:
                    b.inherit = inh
                return t
        raise RuntimeError("arena full allocating %s (%d words); free=%s" % (name, words, self.free))

    def release(self, t):
        hz = set()
        for b in t.bufs:
            if b.sem is not None and self.prog is not None:
                self.prog.dfree[b.semq].append(b.sem)
                b.sem = None
            if b.writer is not None:
                hz.add(b.writer)
            hz.update(b.readers.values())
            hz.update(b.inherit)
        self.ghosts.append((t.off, t.off + t.words, hz))
        self.free.append((t.off, t.off + t.words))
        self.free.sort()
        m = []
        for s, e in self.free:
            if m and m[-1][1] == s:
                m[-1] = (m[-1][0], e)
            else:
                m.append((s, e))
        self.free = m


class Tile:
    def __init__(self, arena, off, words, name, nbufs=1):
        self.arena, self.off, self.words, self.name = arena, off, words, name
        self.bufs = [Buf("%s.%d" % (name, i)) for i in range(nbufs)]

    @property
    def b(self):
        return self.bufs[0]

    def f32(self, c0=0, c1=None):
        c1 = self.words if c1 is None else c1
        return self.arena.ap[:, self.off + c0:self.off + c1]

    def bf(self, c0=0, c1=None):
        v = self.arena.ap[:, self.off:self.off + self.words].bitcast(BF16)
        c1 = 2 * self.words if c1 is None else c1
        return v[:, c0:c1]

    def free(self):
        self.arena.release(self)


class Pipe:
    def __init__(self, depth):
        self.depth = depth
        self.q = []

    def push(self, a, b):
        if a is not None:
            a()
        self.q.append(b)
        while len(self.q) > self.depth:
            self.q.pop(0)()

    def flush(self):
        while self.q:
            self.q.pop(0)()


def blk_of(c0, c1):
    return [i for i, (s, e) in enumerate(TB) if s < c1 and e > c0]


class FMat:
    def __init__(self, arena, nch, dtype, name, lo=0, hi=T):
        self.nch, self.dtype, self.ncols, self.lo = nch, dtype, hi - lo, lo
        wpc = self.ncols if dtype == F32 else self.ncols // 2
        self.t = arena.alloc(nch * wpc, name, nbufs=nch * len(TB))

    def ap(self, kc, c0, c1):
        c0, c1 = c0 - self.lo, c1 - self.lo
        if self.dtype == F32:
            return self.t.f32(kc * self.ncols + c0, kc * self.ncols + c1)
        return self.t.bf(kc * self.ncols + c0, kc * self.ncols + c1)

    def ap3(self, k0, k1, c0, c1):
        if self.dtype == F32:
            v = self.t.f32()
        else:
            v = self.t.bf()
        v = v.rearrange("p (k t) -> p k t", t=self.ncols)
        return v[:, k0:k1, c0 - self.lo:c1 - self.lo]

    def bufs(self, kc, c0, c1):
        return [self.t.bufs[kc * len(TB) + i] for i in blk_of(c0, c1)]

    def free(self):
        self.t.free()


class Builder:
    def __init__(self, stage=99):
        self.stage = stage
        self.nc = bass.Bass("TRN2", target_bir_lowering=False)
        self.es = ExitStack()
        nc = self.nc
        self.din = {}
        self.dout = {}

    def dram_in(self, name, shape):
        self.din[name] = self.nc.dram_tensor(name, list(shape), F32, kind="ExternalInput").ap()
        return self.din[name]

    def dram_out(self, name, shape):
        self.dout[name] = self.nc.dram_tensor(name, list(shape), F32, kind="ExternalOutput").ap()
        return self.dout[name]

    def ps(self, i):
        return self.psum[i], self.psb[i]

    def build(self):
        nc, es = self.nc, self.es
        I, O = self.dram_in, self.dram_out
        xp = I("xp", (TP, D)); xs = I("xs", (TS, D))
        ck = I("ck", (2, PAST, 512)); cv = I("cv", (2, PAST, 512))
        s5r = I("s5r", (2, 2048)); s5i = I("s5i", (2, 2048))
        cmk = I("cmk", (2, NMEM, D)); cmv = I("cmv", (2, NMEM, D))
        memp = I("memp", (NMEM, D))
        W = {}
        for nm, sh in [("g_ffn1", (D,)), ("w_ffn1_gu", (D, 2 * DFF)), ("w_ffn1_d", (DFF, D)),
                       ("g_mix", (D,)), ("w_in", (D, 2048)),
                       ("ssm_a_re", (2048,)), ("ssm_a_im", (2048,)), ("ssm_log_dt", (32,)),
                       ("ssm_b_re", (2048, 16)), ("ssm_b_im", (2048, 16)),
                       ("ssm_c_re", (512, 64)), ("ssm_c_im", (512, 64)), ("ssm_d", (512,)),
                       ("w_glu", (512, 512)), ("b_glu", (512,)),
                       ("lambda_q", (128,)), ("lambda_k", (128,)), ("g_subln", (128,)),
                       ("w_out", (D, D)), ("g_mem", (D,)), ("g_cross", (D,)),
                       ("w_cq", (D, D)), ("w_ck", (D, D)), ("w_cv", (D, D)), ("w_co", (D, D)),
                       ("g_ffn2", (D,)), ("w_ffn2_gu", (D, 2 * DFF)), ("w_ffn2_d", (DFF, D)),
                       ("g_final", (D,))]:
            W[nm] = I(nm, sh)
        self.W = W
        yp = O("yp", (TP, D)); ys = O("ys", (TS, D))
        okp = O("kp", (TP, 512)); ovp = O("vp", (TP, 512))
        orep = O("rep", (2048,)); oimp = O("imp", (2048,))
        omkp = O("mkp", (NMEM, D)); omvp = O("mvp", (NMEM, D))
        oks = O("ks", (TS, 512)); ovs = O("vs", (TS, 512))
        ores = O("res", (2, 2048)); oims = O("ims", (2, 2048))

        arena_t = es.enter_context(nc.sbuf_tensor("arena", [128, ARENA_WORDS], F32))
        self.arena_t = arena_t
        self.arena_bf = arena_t.bitcast(BF16)
        self.A = Arena(arena_t[:, :], ARENA_WORDS)
        self.psum = [es.enter_context(nc.psum_tensor("ps%d" % i, [128, 512], F32)) for i in range(8)]
        self.psb = [Buf("ps%d" % i, excl=True) for i in range(8)]
        self.P = Prog(nc, es)
        self.A.prog = self.P
        P, A = self.P, self.A

        self.consts()
        if self.stage >= 4:
            P.defer = True
            self.s5_params(s5r, s5i)
            P.defer = False
        self.hT = FMat(A, KC, F32, "hT")
        self.load_x(xp, xs)
        self.ffn(W["w_ffn1_gu"], W["w_ffn1_d"], self.gcol["g_ffn1"])
        def mid():
            if self.stage >= 4:
                self.s5(s5r, s5i, orep, oimp, ores, oims)
                if self.stage == 4:
                    for k_ in range(4):
                        self.dump("dbg_ys%d" % k_, self.yssT.ap(k_, 0, T), T, [b for b in self.yssT.t.bufs])
            else:
                self.uD.free()
        P.replay(1 << 30)
        if self.stage >= 2:
            self.mix(okp, ovp, oks, ovs, mid)
            if self.stage >= 4:
                self.proj_add(W["w_out"], 0, 4, self.yssT)
                self.yssT.free()
        if self.stage >= 3:
            self.attention(ck, cv)
            if self.stage == 3:
                for h in range(4):
                    self.dump("dbg_ya%d" % h, self.yaT.ap(h, 0, T), T, [b for b in self.yaT.t.bufs])
            self.proj_add(W["w_out"], 512, 4, self.yaT)
            self.yaT.free()
            self.qT.free(); self.kT.free(); self.vtok.free()
            for t_ in self.vts:
                t_.free()
        if self.stage >= 5:
            self.mem_sets(memp, cmk, cmv, omkp, omvp)
            self.cross_attn()
        if self.stage >= 6:
            self.ffn(W["w_ffn2_gu"], W["w_ffn2_d"], self.gcol["g_ffn2"],
                     on_block_done=lambda c0, c1: self.final_out(yp, ys, only=(c0, c1)))
            block = es.enter_context(nc.Block())
            P.emit(block)
            es.close()
            return nc
        self.final_out(yp, ys)

        block = es.enter_context(nc.Block())
        P.emit(block)
        es.close()
        return nc

    def mm(self, out, lhsT, rhs, start, stop, reads, writes, **kw):
        self.P.op("pe", lambda e: e.matmul(out, lhsT=lhsT, rhs=rhs, start=start, stop=stop, **kw), reads, writes)

    def tr(self, out, in_, ident, reads, writes):
        self.P.op("pe", lambda e: e.transpose(out, in_, ident), reads, writes)

    def act(self, out, in_, func, reads, writes, bias=None, scale=None, accum_out=None):
        kw = {}
        if bias is not None:
            kw["bias"] = bias
        if scale is not None:
            kw["scale"] = scale
        if accum_out is not None:
            kw["accum_out"] = accum_out
        self.P.op("act", lambda e: e.activation(out=out, in_=in_, func=func, **kw), reads, writes)

    def tt(self, out, in0, in1, op, reads, writes, eng="dve"):
        self.P.op(eng, lambda e: e.tensor_tensor(out=out, in0=in0, in1=in1, op=op), reads, writes)

    def stt(self, out, in0, scalar, in1, op0, op1, reads, writes):
        self.P.op("dve", lambda e: e.scalar_tensor_tensor(out=out, in0=in0, scalar=scalar, in1=in1,
                                                          op0=op0, op1=op1), reads, writes)

    def ts(self, out, in0, s1, s2, op0, op1, reads, writes, eng="dve"):
        if op1 is None:
            self.P.op(eng, lambda e: e.tensor_scalar(out=out, in0=in0, scalar1=s1, scalar2=None, op0=op0),
                      reads, writes)
        else:
            self.P.op(eng, lambda e: e.tensor_scalar(out=out, in0=in0, scalar1=s1, scalar2=s2, op0=op0, op1=op1),
                      reads, writes)

    def cp(self, out, in_, reads, writes, eng="dve"):
        if eng == "act":
            self.P.op("act", lambda e: e.activation(out=out, in_=in_, func=AF.Copy), reads, writes)
        else:
            self.P.op(eng, lambda e: e.tensor_copy(out=out, in_=in_), reads, writes)

    def recip(self, out, in_, reads, writes):
        self.P.op("dve", lambda e: e.reciprocal(out=out, in_=in_), reads, writes)

    def rpow(self, out, in_, reads, writes, power=-1.0, scale=None, bias=None, bias_bufs=()):
        self.act(out, in_, AF.Ln, list(reads) + list(bias_bufs), writes, bias=bias, scale=scale)
        self.act(out, out, AF.Exp, writes, writes, scale=power)

    def mset(self, out, val, reads, writes, eng="dve"):
        self.P.op(eng, lambda e: e.memset(out, val), reads, writes)

    def consts(self):
        P, A, nc = self.P, self.A, self.nc
        c2 = A.alloc(128, "ones_f")
        self.ones_f = c2.f32(0, 128)
        self.cb2 = c2.b
        self.mset(self.ones_f, 1.0, [], [c2.b])
        c = A.alloc(256, "consts")
        self.cb = c.b
        self.ident_f = c.f32(0, 128)
        self.ident_b = c.bf(256, 384)
        self.ones_b = c.bf(384, 512)
        idf = self.ident_f
        P.op("pool", lambda e: e.iota(idf, pattern=[[1, 128]], base=0, channel_multiplier=-1,
                                      allow_small_or_imprecise_dtypes=True), writes=[c.b])
        self.ts(idf, idf, 0.0, None, ALU.is_equal, None, [c.b], [c.b])
        self.cp(self.ident_b, idf, [c.b], [c.b])
        self.mset(self.ones_b, 1.0, [c.b], [c.b])
        self.gcol = {}
        gt = A.alloc(8 * 6, "gcols")
        self.gt = gt
        for i, nm in enumerate(["g_ffn1", "g_mix", "g_cross", "g_ffn2", "g_final", "g_mem"]):
            ap = gt.f32(8 * i, 8 * i + 8)
            self.gcol[nm] = ap
            P.dma("sp", ap, self.W[nm].rearrange("(k p) -> p k", p=128), writes=[gt.b], nc_ok=True)
        ec = A.alloc(2, "epscol")
        self.eps_col = ec.f32(0, 1)
        self.mset(ec.f32(0, 2), EPS, [], [ec.b])
        self.epsb = ec.b

    def load_x(self, xp, xs):
        P, A = self.P, self.A
        stg = [A.alloc(1024, "xstg%d" % i) for i in range(2)]
        for i in range(T // 128):
            st = stg[i % 2]
            src = xp[i * 128:(i + 1) * 128, :] if i < 16 else xs[:, :]
            P.dma("sp", st.f32(), src, writes=[st.b])
            c0 = i * 128
            for half in range(2):
                pst, psb = self.ps(half)
                for j in range(4):
                    kc = half * 4 + j
                    self.tr(pst[:, j * 128:(j + 1) * 128], st.f32(kc * 128, (kc + 1) * 128), self.ident_f,
                            [st.b, self.cb], [psb])
                dst = self.hT.ap3(half * 4, half * 4 + 4, c0, c0 + 128)
                wb = []
                for kc in range(half * 4, half * 4 + 4):
                    wb += self.hT.bufs(kc, c0, c0 + 128)
                self.cp(dst, pst[:, :].rearrange("p (k t) -> p k t", t=128), [psb], wb,
                        eng=("dve" if half == 0 else "act"))
        for s in stg:
            s.free()

    def rmsnorm_to(self, src, gcol, dst, c0, c1, psi=7):
        self.rmsnorm_gen(lambda kc: (src.ap(kc, c0, c1), src.bufs(kc, c0, c1)), gcol,
                         lambda kc: (dst.ap(kc, c0, c1), dst.bufs(kc, c0, c1)), c1 - c0, psi)

    def rmsnorm_gen(self, src_fn, gcol, dst_fn, w, psi=7):
        P, A = self.P, self.A
        pst, psb = self.ps(psi)
        sq = [A.alloc(256, "sq%d" % i) for i in range(2)]
        for kc in range(KC):
            s = sq[kc % 2]
            sap, sbufs = src_fn(kc)
            self.act(s.bf(0, w), sap, AF.Square, sbufs, [s.b])
            self.mm(pst[:, 0:w], self.ones_b, s.bf(0, w), kc == 0, kc == KC - 1, [s.b, self.cb], [psb])
        rs = A.alloc(512, "rstd")
        self.rpow(rs.f32(0, w), pst[:, 0:w], [psb], [rs.b], power=-0.5, scale=1.0 / D, bias=self.eps_col,
                  bias_bufs=[self.epsb])
        for kc in range(KC):
            sap, sbufs = src_fn(kc)
            dap, dbufs = dst_fn(kc)
            self.stt(dap, sap, gcol[:, kc:kc + 1], rs.f32(0, w), ALU.mult, ALU.mult,
                     sbufs + [rs.b, self.gt.b], dbufs)
        for s in sq:
            s.free()
        rs.free()

    def ffn(self, w_gu, w_d, gcol, on_block_done=None):
        P, A = self.P, self.A
        wgu_v = w_gu.rearrange("(k p) n -> p k n", p=128)
        wd_v = w_d.rearrange("(h p) n -> p h n", p=128)
        cnt = 0
        sbs = [(0, 1024), (1024, T)]
        xn_next = None
        for si, (s0, s1) in enumerate(sbs):
            blocks = [(c0, c1) for (c0, c1) in TB if s0 <= c0 < s1]
            if xn_next is None:
                xn = FMat(A, KC, BF16, "xn", s0, s1)
                for (c0, c1) in blocks:
                    self.rmsnorm_to(self.hT, gcol, xn, c0, c1)
            else:
                xn = xn_next
                xn_next = None
            h1 = FMat(A, NHC, BF16, "h1", s0, s1)
            wg = [A.alloc(1024, "wg%d" % i) for i in range(2)]
            wu = [A.alloc(1024, "wu%d" % i) for i in range(2)]
            sg = [A.alloc(256, "sg%d" % i) for i in range(2)]
            wdA = A.alloc(NHC * 256, "wdA")
            for hp in range(NHC // 2):
                s = hp % 2
                gv = wg[s].bf().rearrange("p (k n) -> p k n", n=256)
                uv = wu[s].bf().rearrange("p (k n) -> p k n", n=256)
                P.dma("pool", gv, wgu_v[:, :, hp * 256:(hp + 1) * 256], writes=[wg[s].b])
                P.dma("pool", uv, wgu_v[:, :, DFF + hp * 256:DFF + (hp + 1) * 256], writes=[wu[s].b])
                if hp == 1:
                    P.dma("pool", wdA.bf().rearrange("p (h n) -> p h n", n=512), wd_v[:, :, 0:512], writes=[wdA.b])
                for j in range(2):
                    hc = 2 * hp + j
                    for (c0, c1) in blocks:
                        w = c1 - c0
                        r = cnt % 2
                        cnt += 1
                        pg, pgb = self.ps(2 * r)
                        pu, pub = self.ps(2 * r + 1)
                        for kc in range(KC):
                            self.mm(pg[:, 0:w], gv[:, kc, j * 128:(j + 1) * 128], xn.ap(kc, c0, c1),
                                    kc == 0, kc == KC - 1, [wg[s].b] + xn.bufs(kc, c0, c1), [pgb])
                        for kc in range(KC):
                            self.mm(pu[:, 0:w], uv[:, kc, j * 128:(j + 1) * 128], xn.ap(kc, c0, c1),
                                    kc == 0, kc == KC - 1, [wu[s].b] + xn.bufs(kc, c0, c1), [pub])
                        sgt = sg[r]
                        self.act(sgt.bf(0, w), pg[:, 0:w], AF.Silu, [pgb], [sgt.b])
                        self.tt(h1.ap(hc, c0, c1), pu[:, 0:w], sgt.bf(0, w), ALU.mult, [pub, sgt.b],
                                h1.bufs(hc, c0, c1))
                        P.replay(4)
            xn.free()
            for t_ in wg + wu + sg:
                t_.free()
            wdB = A.alloc(NHC * 256, "wdB")
            P.dma("pool", wdB.bf().rearrange("p (h n) -> p h n", n=512), wd_v[:, :, 512:1024], writes=[wdB.b])
            if si + 1 < len(sbs) and on_block_done is None:
                n0, n1 = sbs[si + 1]
                try:
                    xn_next = FMat(A, KC, BF16, "xn", n0, n1)
                except RuntimeError:
                    xn_next = None
                if xn_next is not None:
                    for (c0, c1) in [(c0, c1) for (c0, c1) in TB if n0 <= c0 < n1]:
                        self.rmsnorm_to(self.hT, gcol, xn_next, c0, c1)
            nb = len(blocks)
            items = [(0, 0)]
            for bi in range(1, nb):
                items += [(0, bi), (1, bi - 1)]
            items += [(1, nb - 1)]
            for half, bi in items:
                wd = (wdA, wdB)[half]
                wv = wd.bf().rearrange("p (h n) -> p h n", n=512)
                for (c0, c1) in [blocks[bi]]:
                    w = c1 - c0
                    for o in range(4):
                        oc = half * 4 + o
                        pt, ptb = self.ps(4 + cnt % 2)
                        cnt += 1
                        for hc in range(NHC):
                            self.mm(pt[:, 0:w], wv[:, hc, o * 128:(o + 1) * 128], h1.ap(hc, c0, c1),
                                    hc == 0, hc == NHC - 1, [wd.b] + h1.bufs(hc, c0, c1), [ptb])
                        hb = self.hT.bufs(oc, c0, c1)
                        self.stt(self.hT.ap(oc, c0, c1), pt[:, 0:w], 0.5, self.hT.ap(oc, c0, c1), ALU.mult, ALU.add,
                                 [ptb] + hb, hb)
                if half == 1 and on_block_done is not None:
                    on_block_done(*blocks[bi])
            h1.free()
            wdA.free()
            wdB.free()

    def final_out(self, yp, ys, only=None):
        P, A = self.P, self.A
        ost = [A.alloc(1024, "ostg%d" % i) for i in range(2)]
        cnt = 0
        for (c0, c1) in (TB if only is None else [only]):
            yn = FMat(A, KC, F32, "yn", c0, c1)
            self.rmsnorm_to(self.hT, self.gcol["g_final"], yn, c0, c1)
            for tt in range((c1 - c0) // 128):
                t0 = c0 + tt * 128
                o = ost[cnt % 2]
                cnt += 1
                for half in range(2):
                    pst, psb = self.ps(half)
                    for j in range(4):
                        kc = half * 4 + j
                        self.tr(pst[:, j * 128:(j + 1) * 128], yn.ap(kc, t0, t0 + 128), self.ident_f,
                                yn.bufs(kc, t0, t0 + 128) + [self.cb], [psb])
                    self.cp(o.f32(half * 512, half * 512 + 512), pst[:, :], [psb], [o.b],
                            eng=("dve" if half == 0 else "act"))
                dst = yp[t0:t0 + 128, :] if t0 < TP else ys[:, :]
                P.dma("sp", dst, o.f32(), reads=[o.b])
            yn.free()
        for o in ost:
            o.free()


    def dump(self, name, ap, ncols, reads, npart=128):
        P, A = self.P, self.A
        d = self.dram_out(name, (npart, ncols))
        t = A.alloc(ncols, "dump")
        self.cp(t.f32(0, ncols)[0:npart, :], ap, reads, [t.b])
        P.dma("sp", d, t.f32(0, ncols)[0:npart, :], reads=[t.b])
        t.free()

    def mix(self, okp, ovp, oks, ovs, mid=None):
        P, A, W = self.P, self.A, self.W
        xnm = FMat(A, KC, BF16, "xnm")
        for (c0, c1) in TB:
            self.rmsnorm_to(self.hT, self.gcol["g_mix"], xnm, c0, c1)
        win_v = W["w_in"].rearrange("(k p) n -> p k n", p=128)
        wi = [A.alloc(2048, "wi%d" % i) for i in range(2)]
        wiv = [t.bf().rearrange("p (k n) -> p k n", n=512) for t in wi]
        self.uD = A.alloc(4 * T // 2, "uD", nbufs=4)
        cnt = [0]

        def fm_pass(slot, colblk, dst):
            P.dma("pool", wiv[slot], win_v[:, :, colblk * 512:(colblk + 1) * 512], writes=[wi[slot].b])
            for oc in range(4):
                for (c0, c1) in TB:
                    w = c1 - c0
                    pt, ptb = self.ps(cnt[0] % 2)
                    for kc in range(KC):
                        self.mm(pt[:, 0:w], wiv[slot][:, kc, oc * 128:(oc + 1) * 128], xnm.ap(kc, c0, c1),
                                kc == 0, kc == KC - 1, [wi[slot].b] + xnm.bufs(kc, c0, c1), [ptb])
                    if dst is None:
                        d_ap = self.raw(BF16, 0, 128, self.uD.off * 2 + oc * T + c0 // 8, [[1, w // 8], [T // 8, 8]])
                        self.cp(d_ap, pt[:, 0:w].rearrange("p (c s) -> p c s", s=8), [ptb], [self.uD.bufs[oc]],
                                eng=("dve" if cnt[0] % 2 == 0 else "act"))
                    else:
                        self.cp(dst.ap(oc, c0, c1), pt[:, 0:w], [ptb], dst.bufs(oc, c0, c1),
                                eng=("dve" if cnt[0] % 2 == 0 else "act"))
                    cnt[0] += 1

        import os
        ms = int(os.environ.get("MIXSTOP", "9"))
        fm_pass(0, 0, None)
        if ms <= 1:
            return
        if mid is not None:
            for t_ in wi:
                t_.free()
            xnm.free()
            mid()
            xnm = FMat(A, KC, BF16, "xnm")
            for (c0, c1) in TB:
                self.rmsnorm_to(self.hT, self.gcol["g_mix"], xnm, c0, c1)
            wi[:] = [A.alloc(2048, "wi%d" % i) for i in range(2)]
            wiv[:] = [t.bf().rearrange("p (k n) -> p k n", n=512) for t in wi]
        self.qT = FMat(A, 4, BF16, "qT")
        self.kT = FMat(A, 4, BF16, "kT")
        fm_pass(1, 1, self.qT)
        fm_pass(0, 2, self.kT)
        if ms <= 2:
            return
        P.dma("pool", wiv[1], win_v[:, :, 1536:2048], writes=[wi[1].b])
        self.vtok = A.alloc(16 * 256, "vtok", nbufs=16)
        self.vts = [A.alloc(256, "vts%d" % s) for s in range(2)]
        stg = [A.alloc(512, "kvstg%d" % i) for i in range(4)]
        tiles = [(128 * i, 128, okp[128 * i:128 * i + 128, :], ovp[128 * i:128 * i + 128, :]) for i in range(16)]
        tiles += [(TP + 64 * s, 64, oks[64 * s:64 * s + 64, :], ovs[64 * s:64 * s + 64, :]) for s in range(2)]
        sc = 0
        mkv = int(os.environ.get("MIXKV", "0"))
        if mkv == 1:
            tiles = tiles[:16]
        for ti, (col0, nt, dk, dv) in enumerate(tiles):
            for which in range(2):
                pt, ptb = self.ps(2 + cnt[0] % 2)
                cnt[0] += 1
                for kc in range(KC):
                    self.mm(pt[0:nt, :], xnm.ap(kc, col0, col0 + nt), wiv[which][:, kc, :],
                            kc == 0, kc == KC - 1, [wi[which].b] + xnm.bufs(kc, col0, col0 + nt), [ptb])
                st = stg[sc % 4]
                sc += 1
                self.cp(st.f32(0, 512)[0:nt, :], pt[0:nt, :], [ptb], [st.b], eng="act")
                if mkv != 3:
                    P.dma("sp", dk if which == 0 else dv, st.f32(0, 512)[0:nt, :], reads=[st.b])
                if which == 1 and mkv != 2:
                    if ti < 16:
                        self.cp(self.vtok.bf(ti * 512, (ti + 1) * 512), st.f32(0, 512), [st.b],
                                [self.vtok.bufs[ti]])
                    else:
                        v = self.vts[ti - 16]
                        self.cp(v.bf(0, 512)[0:64, :], st.f32(0, 512)[0:64, :], [st.b], [v.b])
        xnm.free()
        for t_ in wi + stg:
            t_.free()

    def attn_consts_prompt(self):
        P, A, W = self.P, self.A, self.W
        slopes = self.slopes
        it = A.alloc(512, "iota_qk")
        P.op("pool", lambda e: e.iota(it.f32(), pattern=[[1, 512]], base=0, channel_multiplier=-1,
                                      allow_small_or_imprecise_dtypes=True), writes=[it.b])
        self.Nh = A.alloc(4 * 512, "alibiN")
        self.Wh = A.alloc(4 * 512, "alibiW")
        ita = A.alloc(512, "iota_abs")
        self.ts(ita.f32(), it.f32(), -1.0, None, ALU.mult, None, [it.b], [ita.b])
        self.tt(ita.f32(), ita.f32(), it.f32(), ALU.max, [ita.b, it.b], [ita.b])
        for h in range(4):
            self.ts(self.Nh.f32(h * 512, (h + 1) * 512), it.f32(), -slopes[h], None, ALU.mult, None,
                    [it.b], [self.Nh.b])
            self.ts(self.Wh.f32(h * 512, (h + 1) * 512), ita.f32(), -slopes[h], None, ALU.mult, None,
                    [ita.b], [self.Wh.b])
            self.mset(self.Wh.f32(h * 512, h * 512 + 64)[64:128, :], -30000.0, [self.Wh.b], [self.Wh.b])
        self.EB = A.alloc(64, "expbias")
        i16 = A.alloc(16, "iota16")
        P.op("pool", lambda e: e.iota(i16.f32(), pattern=[[1, 16]], base=0, channel_multiplier=0,
                                      allow_small_or_imprecise_dtypes=True), writes=[i16.b])
        for h in range(4):
            self.ts(self.EB.f32(h * 16, h * 16 + 16), i16.f32(), -slopes[h] * 128.0, None, ALU.mult, None,
                    [i16.b], [self.EB.b])
        for t_ in (it, ita, i16):
            t_.free()
        xt = A.alloc(128 * 5, "al_x")
        self.mset(xt.f32(), 0.0, [], [xt.b])
        ip = A.alloc(2, "al_ip")
        P.op("pool", lambda e: e.iota(ip.f32(0, 1), pattern=[[0, 1]], base=0, channel_multiplier=1,
                                      allow_small_or_imprecise_dtypes=True), writes=[ip.b])
        for base in (0,):
            self.mset(xt.f32(base, base + 2), 1.0, [xt.b], [xt.b])
            self.cp(xt.f32(base + 2, base + 3), ip.f32(0, 1), [ip.b, xt.b], [xt.b])
            for b in range(4):
                o = 128 * (1 + b) + base
                self.mset(xt.f32(o, o + 1), -256.0 * (b // 2), [xt.b], [xt.b])
                self.ts(xt.f32(o + 1, o + 2), ip.f32(0, 1), -1.0, -128.0 * (b % 2), ALU.mult, ALU.add, [ip.b, xt.b], [xt.b])
                self.mset(xt.f32(o + 2, o + 3), 1.0, [xt.b], [xt.b])
        self.ALl = A.alloc(64, "al_l")
        self.ALr = A.alloc(4 * 256, "al_r")
        ptl, ptlb = self.ps(6)
        self.tr(ptl[:, 0:128], xt.f32(0, 128), self.ident_f, [xt.b, self.cb], [ptlb])
        self.cp(self.ALl.bf(0, 128), ptl[:, 0:128], [ptlb], [self.ALl.b])
        ptr_, ptrb = self.ps(5)
        for b in range(4):
            self.tr(ptr_[:, b * 128:(b + 1) * 128], xt.f32(128 * (1 + b), 128 * (2 + b)), self.ident_f,
                    [xt.b, self.cb], [ptrb])
        for h in range(4):
            self.ts(self.ALr.bf(h * 512, (h + 1) * 512), ptr_[:, 0:512], 8.0 * slopes[h], None, ALU.mult, None,
                    [ptrb], [self.ALr.b])
        xt.free(); ip.free()

    def attn_consts(self):
        P, A, W = self.P, self.A, self.W
        slopes = [2.0 ** (-2.0 * (h + 1)) for h in range(4)]
        self.slopes = slopes
        self.SB = A.alloc(128, "sbias")
        i32 = A.alloc(32, "iota32")
        P.op("pool", lambda e: e.iota(i32.f32(), pattern=[[128, 32]], base=-PAST, channel_multiplier=1,
                                      allow_small_or_imprecise_dtypes=True), writes=[i32.b])
        for h in range(4):
            self.ts(self.SB.f32(h * 32, h * 32 + 32), i32.f32(), slopes[h], None, ALU.mult, None,
                    [i32.b], [self.SB.b])
        self.DN = A.alloc(4 * 128, "dnew")
        ia = A.alloc(64, "iota_a")
        ib = A.alloc(64, "iota_b")
        P.op("pool", lambda e: e.iota(ia.f32(), pattern=[[1, 64]], base=0, channel_multiplier=-1,
                                      allow_small_or_imprecise_dtypes=True), writes=[ia.b])
        P.op("pool", lambda e: e.iota(ib.f32(), pattern=[[1, 64]], base=0, channel_multiplier=0,
                                      allow_small_or_imprecise_dtypes=True), writes=[ib.b])
        ic = A.alloc(64, "iota_c")
        self.ts(ic.f32(), ia.f32(), -1.0, None, ALU.mult, None, [ia.b], [ic.b])
        self.tt(ia.f32(), ia.f32(), ic.f32(), ALU.max, [ia.b, ic.b], [ia.b])
        self.tt(ia.f32(), ib.f32(), ia.f32(), ALU.subtract, [ia.b, ib.b], [ia.b])
        ic.free()
        for h in range(4):
            for m in range(2):
                self.ts(self.DN.f32(h * 128 + m * 64, h * 128 + m * 64 + 64), ia.f32(), slopes[h], None,
                        ALU.mult, None, [ia.b], [self.DN.b])
        for t_ in (i32, ia, ib):
            t_.free()
        lt = A.alloc(8, "lam")
        self.lamt = lt
        self.mset(lt.f32(0, 8), 0.0, [], [lt.b])
        P.dma("sp", lt.f32(0, 1), W["lambda_q"].rearrange("(p o) -> p o", o=1), writes=[lt.b], nc_ok=True)
        P.dma("sp", lt.f32(1, 2), W["lambda_k"].rearrange("(p o) -> p o", o=1), writes=[lt.b], nc_ok=True)
        P.dma("sp", lt.f32(6, 7), W["g_subln"].rearrange("(p o) -> p o", o=1), writes=[lt.b], nc_ok=True)
        self.tt(lt.f32(2, 3), lt.f32(0, 1), lt.f32(1, 2), ALU.mult, [lt.b], [lt.b])
        pt, ptb = self.ps(6)
        pt2, ptb2 = self.ps(5)
        self.mm(pt[:, 0:2], self.ones_f[0:64, :], lt.f32(2, 4)[0:64, :], True, True, [lt.b, self.cb2], [ptb])
        self.mm(pt2[:, 0:2], self.ones_f[64:128, :], lt.f32(2, 4)[64:128, :], True, True, [lt.b, self.cb2], [ptb2])
        self.act(lt.f32(3, 4), pt[:, 0:1], AF.Exp, [ptb], [lt.b])
        self.act(lt.f32(4, 5), pt2[:, 0:1], AF.Exp, [ptb2], [lt.b])
        lam_init = 0.8 - 0.6 * math.exp(-0.3 * 0)
        self.lam_init = lam_init
        self.stt(lt.f32(5, 6), lt.f32(4, 5), -lam_init, lt.f32(3, 4), ALU.add, ALU.subtract, [lt.b], [lt.b])
        self.neglam = lt.f32(5, 6)
        self.ts(lt.f32(6, 7), lt.f32(6, 7), 1.0 - lam_init, None, ALU.mult, None, [lt.b], [lt.b])
        self.gsub = lt.f32(6, 7)

    def attn_finish(self, O, S, ncols, qsl, dst_ap, dst_bufs, wk):
        A = self.A
        on = []
        for m in range(2):
            (po, pob), (psm, psmb) = O[m], S[m]
            r = wk["R"][m]
            self.rpow(r.f32(0, ncols), qsl(psm), [psmb], [r.b])
            o = wk["On"][m]
            self.tt(o.f32(0, ncols), qsl(po), r.f32(0, ncols), ALU.mult, [pob, r.b], [o.b])
            on.append(o)
        od = wk["Od"]
        self.stt(od.f32(0, ncols), on[1].f32(0, ncols), self.neglam, on[0].f32(0, ncols), ALU.mult, ALU.add,
                 [on[0].b, on[1].b, self.lamt.b], [od.b])
        sq = wk["sq"]
        self.tt(sq.bf(0, ncols), od.f32(0, ncols), od.f32(0, ncols), ALU.mult, [od.b], [sq.b])
        pt, ptb = self.ps(6)
        self.mm(pt[:, 0:ncols], self.ones_b, sq.bf(0, ncols), True, True, [sq.b, self.cb], [ptb])
        rs = wk["rs"]
        self.rpow(rs.f32(0, ncols), pt[:, 0:ncols], [ptb], [rs.b], power=-0.5, scale=1.0 / 128, bias=self.eps_col,
                  bias_bufs=[self.epsb])
        self.stt(dst_ap, od.f32(0, ncols), self.gsub, rs.f32(0, ncols), ALU.mult, ALU.mult,
                 [od.b, rs.b, self.lamt.b], dst_bufs)

    def attention(self, ck, cv):
        P, A = self.P, self.A
        self.attn_consts()
        self.yaT = FMat(A, 4, BF16, "yattT")
        wk = {"R": [A.alloc(512, "R%d" % m) for m in range(2)],
              "On": [A.alloc(512, "On%d" % m) for m in range(2)],
              "Od": A.alloc(512, "Od"), "sq": A.alloc(256, "sq"), "rs": A.alloc(512, "rs")}
        sb = [A.alloc(512, "sbias%d" % i) for i in range(3)]
        pT = [A.alloc(256, "pT%d" % i) for i in range(6)]
        qT, kT = self.qT, self.kT
        cnt = 0
        pcnt = 0
        HK = 2048
        pipe = Pipe(5)
        sbanks = (0, 1, 6, 7)
        qz = [A.alloc(64, "qz%d" % i) for i in range(2)]
        csets = [(A.alloc(1024, "ktok%d" % i), A.alloc(1024, "vtokc%d" % i), A.alloc(1024, "kTc%d" % i))
                 for i in range(2)]
        ci_ = 0
        for s in range(2):
            qc0 = TP + 64 * s
            for h in range(4):
                Ob, Sb = self.ps(2), self.ps(3)
                qzt = qz[(s * 4 + h) % 2]
                self.mset(qzt.bf(0, 128), 0.0, [], [qzt.b])
                for m in range(2):
                    self.cp(qzt.bf(m * 64, m * 64 + 64)[64 * m:64 * m + 64, :],
                            qT.ap(h, qc0, qc0 + 64)[64 * m:64 * m + 64, :], qT.bufs(h, qc0, qc0 + 64), [qzt.b])
                first = [True]
                for half in range(2):
                    ktok, vtc, kTc = csets[ci_ % 2]
                    ci_ += 1
                    kv = ktok.bf().rearrange("p (t c) -> p t c", c=128)
                    vv = vtc.bf().rearrange("p (t c) -> p t c", c=128)
                    P.dma("pool", kv, ck[s, half * HK:(half + 1) * HK, h * 128:(h + 1) * 128].rearrange(
                        "(t p) c -> p t c", p=128), writes=[ktok.b])
                    P.dma("pool", vv, cv[s, half * HK:(half + 1) * HK, h * 128:(h + 1) * 128].rearrange(
                        "(t p) c -> p t c", p=128), writes=[vtc.b])
                    for g in range(2):
                        pt, ptb = self.ps(4 + g)
                        ptv = pt[:, :].bitcast(BF16)
                        for j in range(8):
                            self.tr(ptv[:, j * 128:(j + 1) * 128], kv[:, g * 8 + j, :], self.ident_b,
                                    [ktok.b, self.cb], [ptb])
                        self.cp(kTc.bf(g * 1024, (g + 1) * 1024), ptv, [ptb], [kTc.b], eng=("dve" if g == 0 else "act"))
                    for kt in range(16):
                        ktg = half * 16 + kt
                        pt, ptb = self.ps(sbanks[cnt % 4])
                        cnt += 1
                        p_ = pT[pcnt % 6]
                        pcnt += 1

                        def stA(pt=pt, ptb=ptb, p_=p_, kt=kt, ktg=ktg, kTc=kTc, qzt=qzt, h=h):
                            self.mm(pt[:, 0:128], kTc.bf(kt * 128, (kt + 1) * 128), qzt.bf(0, 128), True, True,
                                    [kTc.b, qzt.b], [ptb])
                            self.act(p_.bf(0, 128), pt[:, 0:128], AF.Exp, [ptb, self.SB.b], [p_.b],
                                     bias=self.SB.f32(h * 32 + ktg, h * 32 + ktg + 1), scale=0.125)

                        def stB(p_=p_, kt=kt, vv=vv, vtc=vtc, Ob=Ob, Sb=Sb, first=first):
                            self.mm(Ob[0][:, 0:128], vv[:, kt, :], p_.bf(0, 128), first[0], False, [vtc.b, p_.b], [Ob[1]])
                            self.mm(Sb[0][:, 0:128], self.ones_b, p_.bf(0, 128), first[0], False, [p_.b, self.cb], [Sb[1]])
                            first[0] = False
                        pipe.push(stA, stB)
                sbt = sb[cnt % 3]
                p_ = pT[pcnt % 6]
                pcnt += 1
                ptA, ptAb = self.ps(sbanks[cnt % 4])
                cnt += 1

                def stA2(ptA=ptA, ptAb=ptAb, sbt=sbt, p_=p_, h=h, qc0=qc0, qzt=qzt):
                    self.mm(ptA[0:64, 0:128], kT.ap(h, qc0, qc0 + 64), qzt.bf(0, 128), True, True,
                            kT.bufs(h, qc0, qc0 + 64) + [qzt.b], [ptAb])
                    self.stt(sbt.f32(0, 128)[0:64, :], ptA[0:64, 0:128], 0.125,
                             self.DN.f32(h * 128, h * 128 + 128)[0:64, :], ALU.mult, ALU.add, [ptAb, self.DN.b], [sbt.b])
                    self.act(p_.bf(0, 128)[0:64, :], sbt.f32(0, 128)[0:64, :], AF.Exp, [sbt.b], [p_.b])

                def stB2(p_=p_, s=s, h=h, Ob=Ob, Sb=Sb, qc0=qc0):
                    self.mm(Ob[0][:, 0:128], self.vts[s].bf(h * 128, (h + 1) * 128)[0:64, :], p_.bf(0, 128)[0:64, :],
                            False, True, [self.vts[s].b, p_.b], [Ob[1]])
                    self.mm(Sb[0][:, 0:128], self.ones_b[0:64, :], p_.bf(0, 128)[0:64, :], False, True,
                            [p_.b, self.cb], [Sb[1]])
                    self.attn_finish_s(Ob, Sb, self.yaT.ap(h, qc0, qc0 + 64), self.yaT.bufs(h, qc0, qc0 + 64), wk)
                pipe.push(stA2, stB2)
        pipe.flush()
        for t_ in qz + [x for cs in csets for x in cs] + [self.SB, self.DN]:
            t_.free()
        self.attn_consts_prompt()
        pipe = Pipe(5)
        pbanks = (0, 1, 6, 7)
        qzp = [A.alloc(512, "qzp%d" % i) for i in range(2)]
        for h in range(4):
            for j in range(4):
                q0 = 512 * j
                O, S = [], []
                qz_ = qzp[(h * 4 + j) % 2]
                self.mset(qz_.bf(0, 1024), 0.0, [], [qz_.b], eng="pool")
                for m in range(2):
                    self.cp(qz_.bf(m * 512, (m + 1) * 512)[64 * m:64 * m + 64, :],
                            qT.ap(h, q0, q0 + 512)[64 * m:64 * m + 64, :], qT.bufs(h, q0, q0 + 512), [qz_.b],
                            eng=("dve" if m == 0 else "act"))
                for m in range(2):
                    Ob, Sb = self.ps(2 + 2 * m), self.ps(3 + 2 * m)
                    nkt = 4 * j + 4
                    for kt in range(nkt):
                        delta = 4 * j - kt
                        lo = 0 if delta >= 1 else 128 * (kt - 4 * j)
                        n = 512 - lo
                        pt, ptb = self.ps(pbanks[cnt % 4])
                        sbt = sb[cnt % 3]
                        cnt += 1
                        p_ = pT[pcnt % 6]
                        pcnt += 1

                        def stA(pt=pt, ptb=ptb, sbt=sbt, p_=p_, kt=kt, delta=delta, lo=lo, n=n, h=h, m=m, q0=q0, qz_=qz_):
                            self.mm(pt[:, 0:n], kT.ap(h, kt * 128, (kt + 1) * 128),
                                    qz_.bf(m * 512 + lo, (m + 1) * 512), True, delta < 1,
                                    kT.bufs(h, kt * 128, (kt + 1) * 128) + [qz_.b], [ptb])
                            if delta >= 1:
                                self.mm(pt[:, 0:512], self.ALl.bf(0, 128), self.ALr.bf(h * 512, (h + 1) * 512), False, True,
                                        [self.ALl.b, self.ALr.b], [ptb])
                                self.act(p_.bf(0, n), pt[:, 0:n], AF.Exp, [ptb], [p_.b],
                                         bias=float(-self.slopes[h] * 128.0 * delta), scale=0.125)
                            else:
                                self.stt(sbt.f32(0, n), pt[:, 0:n], 0.125, self.Wh.f32(h * 512, h * 512 + n),
                                         ALU.mult, ALU.add, [ptb, self.Wh.b], [sbt.b])
                                self.act(p_.bf(0, n), sbt.f32(0, n), AF.Exp, [sbt.b], [p_.b])

                        def stB(p_=p_, kt=kt, nkt=nkt, lo=lo, n=n, h=h, Ob=Ob, Sb=Sb):
                            self.mm(Ob[0][:, lo:512], self.vtok.bf(kt * 512 + h * 128, kt * 512 + (h + 1) * 128),
                                    p_.bf(0, n), kt == 0, kt == nkt - 1, [self.vtok.bufs[kt], p_.b], [Ob[1]])
                            self.mm(Sb[0][:, lo:512], self.ones_b, p_.bf(0, n), kt == 0, kt == nkt - 1,
                                    [p_.b, self.cb], [Sb[1]])
                        pipe.push(stA, stB)
                    O.append(Ob)
                    S.append(Sb)

                def fin(O=O, S=S, h=h, q0=q0):
                    self.attn_finish(O, S, 512, lambda t: t[:, 0:512], self.yaT.ap(h, q0, q0 + 512),
                                     self.yaT.bufs(h, q0, q0 + 512), wk)
                pipe.push(None, fin)
        pipe.flush()
        for t_ in wk["R"] + wk["On"] + [wk["Od"], wk["sq"], wk["rs"]] + sb + pT:
            t_.free()
        for t_ in (self.Nh, self.Wh, self.EB, self.ALl, self.ALr) + tuple(qzp):
            t_.free()

    def attn_finish_s(self, Ob, Sb, dst_ap, dst_bufs, wk):
        r = wk["R"][0]
        self.rpow(r.f32(0, 128), Sb[0][:, 0:128], [Sb[1]], [r.b])
        o = wk["On"][0]
        self.tt(o.f32(0, 128), Ob[0][:, 0:128], r.f32(0, 128), ALU.mult, [Ob[1], r.b], [o.b])
        od = wk["Od"]
        self.stt(od.f32(0, 64), o.f32(64, 128), self.neglam, o.f32(0, 64), ALU.mult, ALU.add,
                 [o.b, self.lamt.b], [od.b])
        sq = wk["sq"]
        self.act(sq.bf(0, 64), od.f32(0, 64), AF.Square, [od.b], [sq.b])
        pt, ptb = self.ps(5)
        self.mm(pt[:, 0:64], self.ones_b, sq.bf(0, 64), True, True, [sq.b, self.cb], [ptb])
        rs = wk["rs"]
        self.rpow(rs.f32(0, 64), pt[:, 0:64], [ptb], [rs.b], power=-0.5, scale=1.0 / 128, bias=self.eps_col,
                  bias_bufs=[self.epsb])
        self.stt(dst_ap, od.f32(0, 64), self.gsub, rs.f32(0, 64), ALU.mult, ALU.mult,
                 [od.b, rs.b, self.lamt.b], dst_bufs)

    def proj_add(self, w_dram, row0, nkc, srcT, scale=1.0):
        P, A = self.P, self.A
        wv = w_dram[row0:row0 + nkc * 128, :].rearrange("(k p) n -> p k n", p=128)
        wt = [A.alloc(nkc * 256, "pw%d" % i) for i in range(2)]
        cnt = 0
        for half in range(2):
            t = wt[half]
            tv = t.bf().rearrange("p (k n) -> p k n", n=512)
            P.dma("pool", tv, wv[:, :, half * 512:(half + 1) * 512], writes=[t.b])
            for o in range(4):
                oc = half * 4 + o
                for (c0, c1) in TB:
                    w = c1 - c0
                    pt, ptb = self.ps(cnt % 2)
                    cnt += 1
                    for kc in range(nkc):
                        self.mm(pt[:, 0:w], tv[:, kc, o * 128:(o + 1) * 128], srcT.ap(kc, c0, c1),
                                kc == 0, kc == nkc - 1, [t.b] + srcT.bufs(kc, c0, c1), [ptb])
                    hb = self.hT.bufs(oc, c0, c1)
                    self.stt(self.hT.ap(oc, c0, c1), pt[:, 0:w], scale, self.hT.ap(oc, c0, c1), ALU.mult, ALU.add,
                             [ptb] + hb, hb)
        for t in wt:
            t.free()


    def raw(self, dtype, p0, npart, off, dims):
        if dtype == F32:
            return bass.AP(tensor=self.arena_t, offset=p0 * ARENA_WORDS + off,
                           ap=[[ARENA_WORDS, npart]] + [list(d) for d in dims])
        return bass.AP(tensor=self.arena_bf, offset=p0 * 2 * ARENA_WORDS + off,
                       ap=[[2 * ARENA_WORDS, npart]] + [list(d) for d in dims])

    def cmul_small(self, o_r, o_i, a_r, a_i, b_r, b_i, t1, t2, bufs):
        self.tt(t1, a_r, b_r, ALU.mult, bufs, bufs)
        self.tt(t2, a_i, b_i, ALU.mult, bufs, bufs)
        self.tt(o_r, t1, t2, ALU.subtract, bufs, bufs)
        self.tt(t1, a_r, b_i, ALU.mult, bufs, bufs)
        self.tt(t2, a_i, b_r, ALU.mult, bufs, bufs)
        self.tt(o_i, t1, t2, ALU.add, bufs, bufs)

    def s5_params(self, s5r, s5i):
        P, A, W = self.P, self.A, self.W
        NP = 16
        prm = A.alloc(128 * 3, "s5prm_in", top=True)
        pb = [prm.b]
        P.dma("sp", prm.f32(0, 128)[0:16, :], W["ssm_a_re"].rearrange("(r p) -> r p", p=128), writes=pb)
        P.dma("sp", prm.f32(0, 128)[16:32, :], W["ssm_a_im"].rearrange("(r p) -> r p", p=128), writes=pb)
        P.dma("sp", prm.f32(128, 256)[0:32, :], s5r.rearrange("s (r p) -> (s r) p", p=128), writes=pb)
        P.dma("sp", prm.f32(256, 384)[0:32, :], s5i.rearrange("s (r p) -> (s r) p", p=128), writes=pb)
        sp_ = A.alloc(32 * 24, "s5small", top=True)
        sb_ = [sp_.b]
        S = lambda i, n=16: sp_.f32(32 * i, 32 * i + n)
        pt, ptb = self.ps(6)
        for j in range(3):
            self.tr(pt[:, j * 32:(j + 1) * 32], prm.f32(j * 128, (j + 1) * 128)[0:32, :], self.ident_f[0:32, 0:32],
                    pb + [self.cb], [ptb])
        AR, AI = S(0), S(1)
        self.cp(sp_.f32(0, 16), pt[:, 0:16], [ptb], sb_)
        self.cp(sp_.f32(32, 48), pt[:, 16:32], [ptb], sb_)
        H0R, H0I = S(2, 32), S(3, 32)
        self.cp(H0R, pt[:, 32:64], [ptb], sb_)
        self.cp(H0I, pt[:, 64:96], [ptb], sb_)
        DT = S(4)
        ldt = W["ssm_log_dt"]
        for gs in range(2):
            src = bass.AP(tensor=ldt.tensor, offset=gs, ap=[[0, 64], [2, 16]])
            P.dma("sp", DT[64 * gs:64 * gs + 64, :], src, writes=sb_, nc_ok=True)
        self.act(DT, DT, AF.Exp, sb_, sb_)
        X, PH = S(5), S(6)
        self.tt(X, AR, DT, ALU.mult, sb_, sb_)
        self.tt(PH, AI, DT, ALU.mult, sb_, sb_)
        RHO = S(7)
        self.act(RHO, X, AF.Exp, sb_, sb_)
        TWO_PI = 2.0 * math.pi
        K_, R_, C_ = S(8), S(9), S(10)
        ki = sp_.f32(32 * 11, 32 * 11 + 16).bitcast(mybir.dt.int32)
        SIN, COS = S(12), S(13)

        def reduce_sin(dst, shift):
            self.ts(R_, PH, shift, None, ALU.add, None, sb_, sb_)
            self.ts(K_, R_, 1.0 / TWO_PI, None, ALU.mult, None, sb_, sb_)
            self.cp(ki, K_, sb_, sb_)
            self.cp(K_, ki, sb_, sb_)
            self.stt(R_, K_, -TWO_PI, R_, ALU.mult, ALU.add, sb_, sb_)
            self.ts(C_, R_, math.pi, TWO_PI, ALU.is_gt, ALU.mult, sb_, sb_)
            self.tt(R_, R_, C_, ALU.subtract, sb_, sb_)
            self.ts(C_, R_, -math.pi, TWO_PI, ALU.is_lt, ALU.mult, sb_, sb_)
            self.tt(R_, R_, C_, ALU.add, sb_, sb_)
            self.ts(R_, R_, 3.141592, -3.141592, ALU.min, ALU.max, sb_, sb_)
            self.act(dst, R_, AF.Sin, sb_, sb_)

        reduce_sin(SIN, 0.0)
        reduce_sin(COS, math.pi / 2.0)
        pw = A.alloc(32 * 24, "s5pow", top=True)
        pwb = [pw.b]
        PWR = lambda k: pw.f32(32 * k, 32 * k + 16)
        PWI = lambda k: pw.f32(32 * (12 + k), 32 * (12 + k) + 16)
        both = sb_ + pwb
        self.tt(PWR(0), RHO, COS, ALU.mult, both, both)
        self.tt(PWI(0), RHO, SIN, ALU.mult, both, both)
        T1, T2 = S(14), S(15)
        for k in range(1, 11):
            self.cmul_small(PWR(k), PWI(k), PWR(k - 1), PWI(k - 1), PWR(k - 1), PWI(k - 1), T1, T2, both)
        NPI = S(16, 16)
        npw = A.alloc(32 * 12, "s5npow", top=True)
        NPWI = lambda k: npw.f32(32 * k, 32 * k + 16)
        for k in range(11):
            self.ts(NPWI(k), PWI(k), -1.0, None, ALU.mult, None, pwb, [npw.b])
        powb = pwb + [npw.b]
        NR, DEN, CR, CI = S(17), S(18), S(19), S(20)
        self.ts(NR, PWR(0), -1.0, None, ALU.add, None, both, both)
        self.tt(DEN, AR, AR, ALU.mult, sb_, sb_)
        self.tt(T1, AI, AI, ALU.mult, sb_, sb_)
        self.tt(DEN, DEN, T1, ALU.add, sb_, sb_)
        self.recip(DEN, DEN, sb_, sb_)
        self.tt(T1, NR, AR, ALU.mult, sb_, sb_)
        self.tt(T2, PWI(0), AI, ALU.mult, both, both)
        self.tt(CR, T1, T2, ALU.add, sb_, sb_)
        self.tt(CR, CR, DEN, ALU.mult, sb_, sb_)
        self.tt(T1, PWI(0), AR, ALU.mult, both, both)
        self.tt(T2, NR, AI, ALU.mult, sb_, sb_)
        self.tt(CI, T1, T2, ALU.subtract, sb_, sb_)
        self.tt(CI, CI, DEN, ALU.mult, sb_, sb_)
        G0R, G0I = S(21, 32), S(22, 32)
        for s in range(2):
            sl = slice(16 * s, 16 * s + 16)
            self.cmul_small(G0R[:, sl], G0I[:, sl], PWR(3), PWI(3), H0R[:, sl], H0I[:, sl], T1, T2, both)
        pk = A.alloc(32 * 27, "s5pk", top=True)
        pkb = [pk.b]
        PKR = lambda k: pk.f32(32 * k, 32 * k + 16)
        PKI = lambda k: pk.f32(32 * (9 + k), 32 * (9 + k) + 16)
        NPKI = lambda k: pk.f32(32 * (18 + k), 32 * (18 + k) + 16)
        allb = sb_ + pwb + pkb
        self.mset(PKR(0), 1.0, [], pkb)
        self.mset(PKI(0), 0.0, [], pkb)
        self.cp(PKR(1), PWR(0), allb, pkb)
        self.cp(PKI(1), PWI(0), allb, pkb)
        for k in range(2, 9):
            self.cmul_small(PKR(k), PKI(k), PKR(k - 1), PKI(k - 1), PWR(0), PWI(0), T1, T2, allb)
        for k in range(9):
            self.ts(NPKI(k), PKI(k), -1.0, None, ALU.mult, None, pkb, pkb)
        self._s5 = dict(locals())

    def s5(self, s5r, s5i, orep, oimp, ores, oims):
        P, A, W = self.P, self.A, self.W
        L = self._s5
        (NP, sp_, sb_, S, H0R, H0I, G0R, G0I, pw, pwb, PWR, PWI, npw, NPWI, powb, pk, pkb, PKR, PKI, NPKI, T1, T2,
         both, allb) = (L[k_] for k_ in ("NP", "sp_", "sb_", "S", "H0R", "H0I", "G0R", "G0I", "pw", "pwb", "PWR", "PWI",
                                         "npw", "NPWI", "powb", "pk", "pkb", "PKR", "PKI", "NPKI", "T1", "T2", "both",
                                         "allb"))
        braw = A.alloc(2 * 256, "s5braw")
        bb_ = [braw.b]
        BRw, BIw = braw.f32(0, 256), braw.f32(256, 512)
        P.dma("sp", BRw.rearrange("p (r c) -> p r c", c=16), W["ssm_b_re"].rearrange("(r p) c -> p r c", p=128),
              writes=bb_)
        P.dma("sp", BIw.rearrange("p (r c) -> p r c", c=16), W["ssm_b_im"].rearrange("(r p) c -> p r c", p=128),
              writes=bb_)
        bbar = A.alloc(10 * 256, "s5bbar")
        bbb = [bbar.b]
        BBR, BBI, U1, U2, BKR, BKI = (bbar.f32(256 * i, 256 * (i + 1)) for i in range(6))
        bset = [[bbar.f32(256 * (2 + 4 * j + i), 256 * (3 + 4 * j + i)) for i in range(4)] for j in range(2)]
        bc = lambda t_, slot: self.raw(F32, 0, 128, t_.off + 32 * slot, [[1, 16], [0, 16]])
        v3 = lambda ap: ap.rearrange("p (r c) -> p r c", c=16)
        rb = sb_ + bb_ + bbb + pkb
        CRb, CIb = bc(sp_, 19), bc(sp_, 20)
        self.tt(v3(U1), v3(BRw), CRb, ALU.mult, rb, bbb)
        self.tt(v3(U2), v3(BIw), CIb, ALU.mult, rb, bbb)
        self.tt(BBR, U1, U2, ALU.subtract, bbb, bbb)
        self.tt(v3(U1), v3(BIw), CRb, ALU.mult, rb, bbb)
        self.tt(v3(U2), v3(BRw), CIb, ALU.mult, rb, bbb)
        self.tt(BBI, U1, U2, ALU.add, bbb, bbb)
        tin = [A.alloc(512, "s5tin%d" % i) for i in range(2)]
        self.BBk = A.alloc(16 * 512, "s5BBk")
        tcnt = 0
        for k in range(8):
            if k == 0:
                srcs = (BBR, BBI)
            else:
                U1, U2, BKR, BKI = bset[k % 2]
                kr, ki = bc(pk, k), bc(pk, 9 + k)
                self.tt(v3(U1), v3(BBR), kr, ALU.mult, rb, bbb)
                self.tt(v3(U2), v3(BBI), ki, ALU.mult, rb, bbb)
                self.tt(BKR, U1, U2, ALU.subtract, bbb, bbb)
                self.tt(v3(U1), v3(BBI), kr, ALU.mult, rb, bbb)
                self.tt(v3(U2), v3(BBR), ki, ALU.mult, rb, bbb)
                self.tt(BKI, U1, U2, ALU.add, bbb, bbb)
                srcs = (BKR, BKI)
            for ri, srcw in enumerate(srcs):
                tn = tin[tcnt % 2]
                self.mset(tn.f32(), 0.0, [], [tn.b], eng="pool")
                s_off = bbar.off + (0 if k == 0 else (4 + 4 * (k % 2)) * 256) + ri * 256
                for gs in range(2):
                    for half in range(2):
                        s_ap = self.raw(F32, 64 * gs, 64, s_off + half * 32, [[64, 4], [16, 2], [1, 16]])
                        d_ap = self.raw(BF16, 64 * gs, 64, tn.off * 2 + half * 64 + gs * 16, [[256, 4], [160, 2], [1, 16]])
                        self.cp(d_ap, s_ap, bbb, [tn.b], eng=("dve" if (gs + half) % 2 == 0 else "act"))
                ptt, pttb = self.ps(1 + tcnt % 2)
                ptv = ptt[:, :].bitcast(BF16)
                for tix in range(8):
                    self.tr(ptv[:, tix * 128:(tix + 1) * 128], tn.bf(tix * 128, (tix + 1) * 128), self.ident_b,
                            [tn.b, self.cb], [pttb])
                o_ = (ri * 8 + k) * 1024
                self.cp(self.BBk.bf(o_, o_ + 1024), ptv, [pttb], [self.BBk.b], eng=("dve" if tcnt % 2 == 0 else "act"))
                tcnt += 1
        for t_ in tin:
            t_.free()
        bbar.free(); braw.free(); L["prm"].free()
        dcol = A.alloc(4, "s5d")
        P.dma("sp", dcol.f32(0, 4), W["ssm_d"].rearrange("(k p) -> p k", p=128), writes=[dcol.b], nc_ok=True)
        PADP, PADS = 128, 4
        SW = PADP + 256 + 2 * (PADS + 8)
        scan = [A.alloc(SW, "s5scan%d" % i) for i in range(4)]
        for t_ in scan:
            self.mset(t_.f32(), 0.0, [], [t_.b])
        def sc_p(t_, shift=0, n=256):
            return t_.f32(PADP - shift, PADP - shift + n)
        def sc_s(t_, shift=0, n=8, c0=0):
            return self.raw(F32, 0, 128, t_.off + PADP + 256 + PADS - shift + c0, [[PADS + 8, 2], [1, n]])
        hpb = [A.alloc(T // 16, "s5hpb%d" % i) for i in range(2)]
        loc = [A.alloc(T // 2, "s5loc%d" % i) for i in range(2)]
        Et = [A.alloc(1024, "s5E%d" % i) for i in range(2)]
        Etmp = [A.alloc(1024, "s5Et%d" % i) for i in range(2)]
        yacc = A.alloc(T, "s5yacc")
        self.ygT = FMat(A, 4, BF16, "ygT")
        outs = A.alloc(2 * 64, "s5outs")
        crw = A.alloc(4 * 128, "s5craw")
        cc = A.alloc(2 * 512, "s5cc")
        c_src = [W["ssm_c_re"], W["ssm_c_im"]]
        ev = 0
        NCH = T // 8
        ccb = A.alloc(2 * 256, "s5ccb")
        evc = [0]
        fins = {}

        def geo(pr):
            qd, q = pr // 4, pr % 4
            half, sub = q // 2, q % 2
            return qd, q, half, sub, qd * 2 + sub

        def usl_(pr):
            qd, q, half, sub, tix = geo(pr)
            return [self.uD.bf(qd * T + s_ * NCH, qd * T + (s_ + 1) * NCH)[64 * half:64 * half + 64, :]
                    for s_ in range(8)]

        def wk_(pr, ri, k):
            qd, q, half, sub, tix = geo(pr)
            return self.BBk.bf((ri * 8 + k) * 1024 + tix * 128,
                               (ri * 8 + k) * 1024 + (tix + 1) * 128)[64 * half:64 * half + 64, :]

        def emit_G(pr):
            qd = pr // 4
            ub = [self.uD.bufs[qd]]
            usl = usl_(pr)
            if pr > 0:
                for t_ in scan:
                    self.mset(self.raw(F32, 0, 128, t_.off + PADP + 256, [[PADS + 8, 2], [1, PADS]]), 0.0,
                              [t_.b], [t_.b])
            for ri in range(2):
                ptt, pttb = self.ps(4 + ri)
                for s_ in range(8):
                    self.mm(ptt[:, 0:NCH], wk_(pr, ri, 7 - s_), usl[s_], s_ == 0, s_ == 7, [self.BBk.b] + ub, [pttb])
                dst_t = scan[ri]
                self.cp(sc_p(dst_t), ptt[:, 0:256], [pttb], [dst_t.b], eng="act")
                self.cp(sc_s(dst_t), ptt[:, 256:272].rearrange("p (s c) -> p s c", c=8), [pttb], [dst_t.b], eng="act")
                g0 = (G0R, G0I)[ri]
                for s in range(2):
                    col = dst_t.f32(PADP + 256 + s * (PADS + 8) + PADS, PADP + 256 + s * (PADS + 8) + PADS + 1)
                    self.tt(col, col, g0[:, 16 * s + pr:16 * s + pr + 1], ALU.add, [dst_t.b] + sb_, [dst_t.b])

        def emit_scan(pr):
            a_, b_ = (scan[0], scan[1]), (scan[2], scan[3])
            for k in range(8):
                d = 1 << k
                mr = pw.f32(32 * (3 + k) + pr, 32 * (3 + k) + pr + 1)
                mi = pw.f32(32 * (12 + 3 + k) + pr, 32 * (12 + 3 + k) + pr + 1)
                nmi = npw.f32(32 * (3 + k) + pr, 32 * (3 + k) + pr + 1)
                views = [sc_p]
                if k <= 2:
                    views.append(sc_s)
                for vf in views:
                    rr = [a_[0].b, a_[1].b] + powb
                    self.stt(vf(b_[0]), vf(a_[0], d), mr, vf(a_[0]), ALU.mult, ALU.add, rr, [b_[0].b])
                    self.stt(vf(b_[1]), vf(a_[1], d), mr, vf(a_[1]), ALU.mult, ALU.add, rr, [b_[1].b])
                    self.stt(vf(b_[0]), vf(a_[1], d), nmi, vf(b_[0]), ALU.mult, ALU.add, rr + [b_[0].b], [b_[0].b])
                    self.stt(vf(b_[1]), vf(a_[0], d), mi, vf(b_[1]), ALU.mult, ALU.add, rr + [b_[1].b], [b_[1].b])
                if k == 3:
                    for c in range(2):
                        self.cp(sc_s(b_[c]), sc_s(a_[c]), [a_[c].b], [b_[c].b], eng="act")
                a_, b_ = b_, a_
            fin = a_
            for c, h0 in enumerate((H0R, H0I)):
                for s in range(2):
                    col = fin[c].f32(PADP + 256 + s * (PADS + 8) + PADS - 1, PADP + 256 + s * (PADS + 8) + PADS)
                    self.cp(col, h0[:, 16 * s + pr:16 * s + pr + 1], sb_, [fin[c].b], eng="act")
            for ri in range(2):
                self.cp(hpb[ri].bf(0, 256), sc_p(fin[ri], 1), [fin[ri].b], [hpb[ri].b], eng="act")
                self.cp(hpb[ri].bf(256, 272).rearrange("p (s c) -> p s c", c=8), sc_s(fin[ri], 1), [fin[ri].b],
                        [hpb[ri].b], eng="act")
                d0 = ri * 48 + pr * 3
                self.cp(outs.f32(d0, d0 + 1), fin[ri].f32(PADP + 255, PADP + 256), [fin[ri].b], [outs.b], eng="act")
                self.cp(outs.f32(d0 + 1, d0 + 3),
                        self.raw(F32, 0, 128, fin[ri].off + PADP + 256 + PADS + 7, [[PADS + 8, 2]]),
                        [fin[ri].b], [outs.b], eng="act")

        def emit_local(pr):
            qd = pr // 4
            ub = [self.uD.bufs[qd]]
            usl = usl_(pr)
            for st in range(8):
                for ri in range(2):
                    ptt, pttb = self.ps(evc[0] % 4)
                    evc[0] += 1
                    for s2 in range(st + 1):
                        self.mm(ptt[:, 0:NCH], wk_(pr, ri, st - s2), usl[s2], s2 == 0, s2 == st, [self.BBk.b] + ub, [pttb])
                    self.cp(loc[ri].bf(st * NCH, (st + 1) * NCH), ptt[:, 0:NCH], [pttb], [loc[ri].b], eng="act")

        def emit_E(pr):
            qd, q, half, sub, tix = geo(pr)
            if q == 0:
                for ri in range(2):
                    for dup in range(2):
                        P.dma("sp", crw.f32(ri * 256 + dup * 64, ri * 256 + dup * 64 + 64),
                              c_src[ri][qd * 128:(qd + 1) * 128, :], writes=[crw.b])
                self.mset(cc.f32(), 0.0, [], [cc.b])
                ptt, pttb = self.ps(6)
                for ri in range(2):
                    self.tr(ptt[:, ri * 128:(ri + 1) * 128], crw.f32(ri * 256, ri * 256 + 128), self.ident_f,
                            [crw.b, self.cb], [pttb])
                for ri in range(2):
                    for gs in range(2):
                        for j in range(4):
                            s_ap = ptt[64 * gs:64 * gs + 64, ri * 128 + (2 * j + gs) * 16:ri * 128 + (2 * j + gs) * 16 + 16]
                            d0 = ri * 512 + j * 128 + j * 32 + gs * 16
                            d_ap = cc.f32(d0, d0 + 16)[64 * gs:64 * gs + 64, :]
                            if ri == 0:
                                self.cp(d_ap, s_ap, [pttb], [cc.b])
                            else:
                                self.ts(d_ap, s_ap, -1.0, None, ALU.mult, None, [pttb], [cc.b])
                self.cp(ccb.bf(0, 1024), cc.f32(0, 1024), [cc.b], [ccb.b], eng="act")
            E = Et[pr % 2]
            c0b = self.raw(F32, 0, 128, cc.off + q * 128, [[0, 8], [1, 128]])
            c1b = self.raw(F32, 0, 128, cc.off + 512 + q * 128, [[0, 8], [1, 128]])
            krb = self.raw(F32, 0, 128, pk.off + 32 * 1 + pr, [[32, 8], [0, 128]])
            kib = self.raw(F32, 0, 128, pk.off + 32 * 10 + pr, [[32, 8], [0, 128]])
            t1 = Etmp[0].f32().rearrange("p (k n) -> p k n", n=128)
            t2 = Etmp[1].f32().rearrange("p (k n) -> p k n", n=128)
            er = E.bf(0, 1024).rearrange("p (k n) -> p k n", n=128)
            ei = E.bf(1024, 2048).rearrange("p (k n) -> p k n", n=128)
            rb_ = [cc.b] + pkb
            tb = [Etmp[0].b, Etmp[1].b]
            self.tt(t1, c0b, krb, ALU.mult, rb_, [Etmp[0].b])
            self.tt(t2, c1b, kib, ALU.mult, rb_, [Etmp[1].b])
            self.tt(er, t1, t2, ALU.add, tb, [E.b])
            self.tt(t1, c1b, krb, ALU.mult, rb_ + [E.b], [Etmp[0].b])
            self.tt(t2, c0b, kib, ALU.mult, rb_ + [E.b], [Etmp[1].b])
            self.tt(ei, t1, t2, ALU.subtract, tb, [E.b])

        def emit_y(pr):
            qd, q, half, sub, tix = geo(pr)
            E = Et[pr % 2]
            for st in range(8):
                ptt, pttb = self.ps(4 + evc[0] % 4)
                evc[0] += 1
                sl = (st * NCH, (st + 1) * NCH)
                self.mm(ptt[:, 0:NCH], ccb.bf(q * 128, (q + 1) * 128), loc[0].bf(*sl), True, False,
                        [ccb.b, loc[0].b], [pttb])
                self.mm(ptt[:, 0:NCH], ccb.bf(512 + q * 128, 512 + (q + 1) * 128), loc[1].bf(*sl), False, False,
                        [ccb.b, loc[1].b], [pttb])
                self.mm(ptt[:, 0:NCH], E.bf(st * 128, (st + 1) * 128), hpb[0].bf(0, NCH), False, False,
                        [E.b, hpb[0].b], [pttb])
                self.mm(ptt[:, 0:NCH], E.bf(1024 + st * 128, 1024 + (st + 1) * 128), hpb[1].bf(0, NCH), False, True,
                        [E.b, hpb[1].b], [pttb])
                if q == 0:
                    self.cp(yacc.f32(*sl), ptt[:, 0:NCH], [pttb], [yacc.b], eng="act")
                else:
                    self.tt(yacc.f32(*sl), ptt[:, 0:NCH], yacc.f32(*sl), ALU.add, [pttb, yacc.b], [yacc.b])
            if q == 3:
                for (c0, c1) in TB:
                    self.stt(yacc.f32(c0, c1), self.uD.bf(qd * T + c0, qd * T + c1), dcol.f32(qd, qd + 1),
                             yacc.f32(c0, c1), ALU.mult, ALU.add, [self.uD.bufs[qd], dcol.b, yacc.b], [yacc.b])
                for st in range(8):
                    d_ap = self.raw(BF16, 0, 128, self.ygT.t.off * 2 + qd * T + st, [[8, NCH]])
                    self.act(d_ap, yacc.f32(st * NCH, (st + 1) * NCH), AF.Gelu_apprx_tanh, [yacc.b],
                             self.ygT.bufs(qd, 0, T))

        emit_G(0)
        for pr in range(NP):
            emit_E(pr)
            emit_local(pr)
            emit_scan(pr)
            if pr + 1 < NP:
                emit_G(pr + 1)
            emit_y(pr)
        for t_ in scan + hpb + loc + Et + Etmp:
            t_.free()
        ptt, pttb = self.ps(0)
        ostt = A.alloc(128, "s5ostg")
        for ri in range(2):
            for wch in range(3):
                s_ap = self.raw(F32, 0, 128, outs.off + ri * 48 + wch, [[3, 16]])
                col = (ri * 3 + wch) * 16
                self.tr(ptt[0:16, col * 8:col * 8 + 128] if False else ptt[0:16, (ri * 3 + wch) * 128:(ri * 3 + wch) * 128 + 128]
                        if (ri * 3 + wch) < 4 else self.psum[1][0:16, (ri * 3 + wch - 4) * 128:(ri * 3 + wch - 4) * 128 + 128],
                        s_ap, self.ident_f, [outs.b, self.cb], [pttb if (ri * 3 + wch) < 4 else self.psb[1]])
        ost2 = A.alloc(6 * 128, "s5ostg2")
        self.cp(ost2.f32(0, 512)[0:16, :], ptt[0:16, :], [pttb], [ost2.b])
        self.cp(ost2.f32(512, 768)[0:16, :], self.psum[1][0:16, 0:256], [self.psb[1]], [ost2.b])
        dsts = [[orep.rearrange("(r p) -> r p", p=128), ores[0].rearrange("(r p) -> r p", p=128),
                 ores[1].rearrange("(r p) -> r p", p=128)],
                [oimp.rearrange("(r p) -> r p", p=128), oims[0].rearrange("(r p) -> r p", p=128),
                 oims[1].rearrange("(r p) -> r p", p=128)]]
        for ri in range(2):
            for wch in range(3):
                k = ri * 3 + wch
                P.dma("sp", dsts[ri][wch], ost2.f32(k * 128, (k + 1) * 128)[0:16, :], reads=[ost2.b])
        for t_ in [ccb, yacc, outs, crw, cc, ostt, ost2, dcol, pw, npw, sp_, pk, self.BBk]:
            t_.free()
        self.uD.free()
        self.yssT = FMat(A, 4, BF16, "yssT")
        wg_ = A.alloc(1024, "wglu")
        wgv = wg_.bf().rearrange("p (k n) -> p k n", n=512)
        P.dma("pool", wgv, W["w_glu"].rearrange("(k p) n -> p k n", p=128), writes=[wg_.b])
        bg = A.alloc(4, "bglu")
        P.dma("sp", bg.f32(0, 4), W["b_glu"].rearrange("(k p) -> p k", p=128), writes=[bg.b], nc_ok=True)
        gt_ = [A.alloc(256, "gate%d" % i) for i in range(2)]
        cnt = 0
        for oc in range(4):
            for (c0, c1) in TB:
                w = c1 - c0
                ptt, pttb = self.ps(cnt % 2)
                g_ = gt_[cnt % 2]
                cnt += 1
                for kc in range(4):
                    self.mm(ptt[:, 0:w], wgv[:, kc, oc * 128:(oc + 1) * 128], self.ygT.ap(kc, c0, c1),
                            kc == 0, kc == 3, [wg_.b] + self.ygT.bufs(kc, c0, c1), [pttb])
                self.act(g_.bf(0, w), ptt[:, 0:w], AF.Sigmoid, [pttb, bg.b], [g_.b], bias=bg.f32(oc, oc + 1))
                self.tt(self.yssT.ap(oc, c0, c1), self.ygT.ap(oc, c0, c1), g_.bf(0, w), ALU.mult,
                        self.ygT.bufs(oc, c0, c1) + [g_.b], self.yssT.bufs(oc, c0, c1))
        for t_ in gt_ + [wg_, bg]:
            t_.free()
        self.ygT.free()


    def proj_to(self, w_dram, nkc, srcT, dstT):
        P, A = self.P, self.A
        wv = w_dram.rearrange("(k p) n -> p k n", p=128)
        wt = [A.alloc(nkc * 256, "pw%d" % i) for i in range(2)]
        cnt = 0
        for half in range(2):
            t = wt[half]
            tv = t.bf().rearrange("p (k n) -> p k n", n=512)
            P.dma("pool", tv, wv[:, :, half * 512:(half + 1) * 512], writes=[t.b])
            for o in range(4):
                oc = half * 4 + o
                for (c0, c1) in TB:
                    w = c1 - c0
                    pt, ptb = self.ps(cnt % 2)
                    for kc in range(nkc):
                        self.mm(pt[:, 0:w], tv[:, kc, o * 128:(o + 1) * 128], srcT.ap(kc, c0, c1),
                                kc == 0, kc == nkc - 1, [t.b] + srcT.bufs(kc, c0, c1), [ptb])
                    self.cp(dstT.ap(oc, c0, c1), pt[:, 0:w], [ptb], dstT.bufs(oc, c0, c1),
                            eng=("dve" if cnt % 2 == 0 else "act"))
                    cnt += 1
        for t in wt:
            t.free()

    def mem_sets(self, memp, cmk, cmv, omkp, omvp):
        P, A, W = self.P, self.A, self.W
        self.mkT = [A.alloc(1024, "mkT%d" % i) for i in range(3)]
        self.mvt = [A.alloc(1024, "mvt%d" % i) for i in range(3)]
        mraw = A.alloc(2048, "mraw")
        P.dma("sp", mraw.f32().rearrange("p (t c) -> p t c", c=1024), memp.rearrange("(t p) c -> p t c", p=128),
              writes=[mraw.b])
        mT = A.alloc(KC * 256, "mT")
        for tt in range(2):
            for half in range(2):
                pst, psb = self.ps(half)
                for j in range(4):
                    kc = half * 4 + j
                    self.tr(pst[:, j * 128:(j + 1) * 128], mraw.f32(tt * 1024 + kc * 128, tt * 1024 + (kc + 1) * 128),
                            self.ident_f, [mraw.b, self.cb], [psb])
                dst = mT.f32().rearrange("p (k t) -> p k t", t=256)[:, half * 4:half * 4 + 4, tt * 128:(tt + 1) * 128]
                self.cp(dst, pst[:, :].rearrange("p (k t) -> p k t", t=128), [psb], [mT.b],
                        eng=("dve" if half == 0 else "act"))
        mraw.free()
        mn = A.alloc(KC * 128, "mnT")
        self.rmsnorm_gen(lambda kc: (mT.f32(kc * 256, (kc + 1) * 256), [mT.b]), self.gcol["g_mem"],
                         lambda kc: (mn.bf(kc * 256, (kc + 1) * 256), [mn.b]), 256)
        mT.free()
        wt = [A.alloc(2048, "mw%d" % i) for i in range(2)]
        stg = [A.alloc(512, "mstg%d" % i) for i in range(2)]
        cnt = 0
        for wi_, (wd_, odst) in enumerate(((W["w_ck"], omkp), (W["w_cv"], omvp))):
            wv = wd_.rearrange("(k p) n -> p k n", p=128)
            for half in range(2):
                t = wt[cnt % 2]
                tv = t.bf().rearrange("p (k n) -> p k n", n=512)
                P.dma("pool", tv, wv[:, :, half * 512:(half + 1) * 512], writes=[t.b])
                if wi_ == 0:
                    for o in range(4):
                        oc = half * 4 + o
                        pt, ptb = self.ps(2 + cnt % 2)
                        for kc in range(KC):
                            self.mm(pt[:, 0:256], tv[:, kc, o * 128:(o + 1) * 128], mn.bf(kc * 256, (kc + 1) * 256),
                                    kc == 0, kc == KC - 1, [t.b, mn.b], [ptb])
                        self.cp(self.mkT[0].bf(oc * 256, (oc + 1) * 256), pt[:, 0:256], [ptb], [self.mkT[0].b])
                for tt in range(2):
                    pt, ptb = self.ps(4 + (cnt + tt) % 2)
                    for kc in range(KC):
                        self.mm(pt[:, :], mn.bf(kc * 256 + tt * 128, kc * 256 + (tt + 1) * 128), tv[:, kc, :],
                                kc == 0, kc == KC - 1, [t.b, mn.b], [ptb])
                    st = stg[(cnt + tt) % 2]
                    self.cp(st.f32(0, 512), pt[:, :], [ptb], [st.b], eng="act")
                    P.dma("sp", odst[tt * 128:(tt + 1) * 128, half * 512:(half + 1) * 512], st.f32(0, 512), reads=[st.b])
                    if wi_ == 1:
                        self.cp(self.mvt[0].bf(tt * 1024 + half * 512, tt * 1024 + (half + 1) * 512), st.f32(0, 512),
                                [st.b], [self.mvt[0].b])
                cnt += 1
        for t_ in wt + stg + [mn]:
            t_.free()
        for s in range(2):
            kraw = A.alloc(1024, "ckraw")
            P.dma("pool", kraw.bf().rearrange("p (t c) -> p t c", c=1024), cmk[s].rearrange("(t p) c -> p t c", p=128),
                  writes=[kraw.b])
            P.dma("pool", self.mvt[1 + s].bf().rearrange("p (t c) -> p t c", c=1024),
                  cmv[s].rearrange("(t p) c -> p t c", p=128), writes=[self.mvt[1 + s].b])
            for tt in range(2):
                pt, ptb = self.ps(tt)
                ptv = pt[:, :].bitcast(BF16)
                for kc in range(KC):
                    self.tr(ptv[:, kc * 128:(kc + 1) * 128], kraw.bf(tt * 1024 + kc * 128, tt * 1024 + (kc + 1) * 128),
                            self.ident_b, [kraw.b, self.cb], [ptb])
                dst = self.mkT[1 + s].bf().rearrange("p (k t) -> p k t", t=256)[:, :, tt * 128:(tt + 1) * 128]
                self.cp(dst, ptv.rearrange("p (k t) -> p k t", t=128), [ptb], [self.mkT[1 + s].b],
                        eng=("dve" if tt == 0 else "act"))
            kraw.free()

    def cross_attn(self):
        P, A, W = self.P, self.A, self.W
        xnc = FMat(A, KC, BF16, "xnc")
        for (c0, c1) in TB:
            self.rmsnorm_to(self.hT, self.gcol["g_cross"], xnc, c0, c1)
        qc = FMat(A, KC, BF16, "qcT")
        self.proj_to(W["w_cq"], KC, xnc, qc)
        xnc.free()
        oc_ = FMat(A, KC, BF16, "ocT")
        pT = [A.alloc(256, "cpT%d" % i) for i in range(5)]
        R = [A.alloc(512, "cR%d" % i) for i in range(2)]
        qsets = [(0, c0, c1) for (c0, c1) in TB[:4]] + [(1, TP, TP + 64), (2, TP + 64, T)]
        cnt = 0
        it = 0
        pipe = Pipe(4)
        for (ms, c0, c1) in qsets:
            w = c1 - c0
            kT_ = self.mkT[ms].bf().rearrange("p (k t) -> p k t", t=256)
            vt_ = self.mvt[ms].bf().rearrange("p (t c) -> p t c", c=1024)
            for hh in range(4):
                banks = (2, 3, 4) if it % 2 == 0 else (5, 6, 7)
                O0, O1, Sm = self.ps(banks[0]), self.ps(banks[1]), self.ps(banks[2])
                r = R[it % 2]
                it += 1
                for kt in range(2):
                    pt, ptb = self.ps(cnt % 2)
                    p_ = pT[cnt % 5]
                    cnt += 1

                    def stA(pt=pt, ptb=ptb, p_=p_, kt=kt, hh=hh, ms=ms, c0=c0, c1=c1, w=w, kT_=kT_):
                        for dc in range(2):
                            self.mm(pt[:, 0:w], kT_[:, 2 * hh + dc, kt * 128:(kt + 1) * 128], qc.ap(2 * hh + dc, c0, c1),
                                    dc == 0, dc == 1, [self.mkT[ms].b] + qc.bufs(2 * hh + dc, c0, c1), [ptb])
                        self.act(p_.bf(0, w), pt[:, 0:w], AF.Exp, [ptb], [p_.b], scale=1.0 / 16.0)

                    def stB(p_=p_, kt=kt, hh=hh, ms=ms, c0=c0, c1=c1, w=w, vt_=vt_, O0=O0, O1=O1, Sm=Sm, r=r):
                        for dc, Ob in enumerate((O0, O1)):
                            self.mm(Ob[0][:, 0:w], vt_[:, kt, (2 * hh + dc) * 128:(2 * hh + dc + 1) * 128], p_.bf(0, w),
                                    kt == 0, kt == 1, [self.mvt[ms].b, p_.b], [Ob[1]])
                        self.mm(Sm[0][:, 0:w], self.ones_b, p_.bf(0, w), kt == 0, kt == 1, [p_.b, self.cb], [Sm[1]])
                        if kt == 1:
                            self.rpow(r.f32(0, w), Sm[0][:, 0:w], [Sm[1]], [r.b])
                            for dc, Ob in enumerate((O0, O1)):
                                self.tt(oc_.ap(2 * hh + dc, c0, c1), Ob[0][:, 0:w], r.f32(0, w), ALU.mult, [Ob[1], r.b],
                                        oc_.bufs(2 * hh + dc, c0, c1))
                    pipe.push(stA, stB)
        pipe.flush()
        qc.free()
        for t_ in pT + R + self.mkT + self.mvt:
            t_.free()
        self.proj_add(W["w_co"], 0, KC, oc_)
        oc_.free()


_CACHE = {}


def _get_nc(stage=99):
    if stage not in _CACHE:
        _CACHE[stage] = Builder(stage).build()
    return _CACHE[stage]


def _in_maps(inp):
    f = lambda a: np.ascontiguousarray(np.asarray(a, dtype=np.float32))
    maps = []
    shared = {}
    for nm in ["g_ffn1", "w_ffn1_gu", "w_ffn1_d", "g_mix", "w_in", "ssm_a_re", "ssm_a_im", "ssm_log_dt",
               "ssm_b_re", "ssm_b_im", "ssm_c_re", "ssm_c_im", "ssm_d", "w_glu", "b_glu", "lambda_q",
               "lambda_k", "g_subln", "w_out", "g_mem", "g_cross", "w_cq", "w_ck", "w_cv", "w_co",
               "g_ffn2", "w_ffn2_gu", "w_ffn2_d"]:
        shared[nm] = f(inp[nm])[0]
    shared["g_final"] = f(inp["g_final"])
    for nm in ["ssm_a_re", "ssm_a_im"]:
        shared[nm] = shared[nm].reshape(2048)
    for nm in ["ssm_b_re", "ssm_b_im"]:
        shared[nm] = shared[nm].reshape(2048, 16)
    for nm in ["ssm_c_re", "ssm_c_im"]:
        shared[nm] = shared[nm].reshape(512, 64)
    for nm in ["lambda_q", "lambda_k"]:
        shared[nm] = shared[nm].reshape(128)
    xp = f(inp["x_prompt"]); xs = f(inp["x_sample"])
    ck = f(inp["cache_attn_k"])[0]; cv = f(inp["cache_attn_v"])[0]
    sr = f(inp["state_s5_re"])[0]; si = f(inp["state_s5_im"])[0]
    cmk = f(inp["cache_mem_k"])[0]; cmv = f(inp["cache_mem_v"])[0]
    mp = f(inp["mem_prompt"])
    for c in range(NCORES):
        m = dict(shared)
        m["xp"] = xp[c]
        m["xs"] = xs[2 * c:2 * c + 2].reshape(TS, D)
        m["ck"] = ck[2 * c:2 * c + 2].reshape(2, PAST, 512)
        m["cv"] = cv[2 * c:2 * c + 2].reshape(2, PAST, 512)
        m["s5r"] = sr[2 * c:2 * c + 2].reshape(2, 2048)
        m["s5i"] = si[2 * c:2 * c + 2].reshape(2, 2048)
        m["cmk"] = cmk[2 * c:2 * c + 2].reshape(2, NMEM, D)
        m["cmv"] = cmv[2 * c:2 * c + 2].reshape(2, NMEM, D)
        m["memp"] = mp[c]
        maps.append(m)
    return maps


def kernel(**inp):
    nc = _get_nc()
    res = run_bass_kernel_spmd(nc, _in_maps(inp), core_ids=list(range(NCORES)))
    R = res.results
    cat = lambda k: np.stack([np.asarray(r[k], dtype=np.float32) for r in R], axis=0)
    y_prompt = cat("yp").reshape(8, TP, D)
    y_sample = cat("ys").reshape(16, 64, D)
    kp = cat("kp").reshape(1, 8, TP, 4, 128)
    vp = cat("vp").reshape(1, 8, TP, 4, 128)
    rep = cat("rep").reshape(1, 8, 32, 64)
    imp = cat("imp").reshape(1, 8, 32, 64)
    mkp = cat("mkp").reshape(1, 8, NMEM, 4, 256)
    mvp = cat("mvp").reshape(1, 8, NMEM, 4, 256)
    ks = cat("ks").reshape(1, 16, 64, 4, 128)
    vs = cat("vs").reshape(1, 16, 64, 4, 128)
    res_ = cat("res").reshape(1, 16, 32, 64)
    ims_ = cat("ims").reshape(1, 16, 32, 64)
    return (y_prompt, y_sample, kp, vp, rep, imp, mkp, mvp, ks, vs, res_, ims_)
```
